# Optimizing a Trainium2 kernel written in Bass

```python
import math
import jax, jax.numpy as jnp
from jax import lax
import numpy as np

D_MODEL = 4096
BATCH = 32
SEQ = 256
DEPTH = 2
DEC_BATCH = 4
DEC_SEQ = 1024
PAST_LEN = 512

GRID_W = 64
BRANCH_W = D_MODEL // 2
CHUNK = 128
H_M = 8
DV_M = BRANCH_W // H_M
DK_M = DV_M // 2
SGU_W = BRANCH_W
SGU_GROUPS = 8
H_A = 16
KV_A = 4
HD_A = BRANCH_W // H_A
GQA_G = H_A // KV_A
WINDOW = 128
QBLK = 128
ROPE_BASE = 10000.0
ROPE_AXIS = HD_A // 2
FFN_DIM = 11008
CONV_W = 3
EPS = 1e-6
IN_SIZES = (H_M * DK_M, H_M * DK_M, BRANCH_W, BRANCH_W, 4 * H_M,
            SGU_W, SGU_W, H_A * HD_A, KV_A * HD_A, KV_A * HD_A, 3 * D_MODEL)
SPLIT_IDX = tuple(int(s) for s in np.cumsum(IN_SIZES)[:-1])
N_IN = int(sum(IN_SIZES))

kernel_name = 'hybrid_mlstm_sgu_swa_flow_step'

f32 = jnp.float32


def rmsnorm(x, g):
    xf = x.astype(f32)
    y = xf * lax.rsqrt(jnp.mean(xf * xf, axis=-1, keepdims=True) + EPS)
    return (y * g.astype(f32)).astype(x.dtype)


def axial_rope_tables(T):
    rows = T // GRID_W
    row = jnp.repeat(jnp.arange(rows, dtype=f32), GRID_W)
    col = jnp.tile(jnp.arange(GRID_W, dtype=f32), rows)
    nf = ROPE_AXIS // 2
    inv = ROPE_BASE ** (-jnp.arange(nf, dtype=f32) / nf)
    ar = row[:, None] * inv[None, :]
    ac = col[:, None] * inv[None, :]
    return jnp.cos(ar), jnp.sin(ar), jnp.cos(ac), jnp.sin(ac)


def rotate(x, cos, sin):
    nf = x.shape[-1] // 2
    x1, x2 = x[..., :nf], x[..., nf:]
    c = cos[None, :, None, :]
    s = sin[None, :, None, :]
    return jnp.concatenate([x1 * c - x2 * s, x2 * c + x1 * s], axis=-1)


def rope2d(x, tabs):
    cr, sr, cc, sc = tabs
    xf = x.astype(f32)
    out = jnp.concatenate([rotate(xf[..., :ROPE_AXIS], cr, sr),
                           rotate(xf[..., ROPE_AXIS:], cc, sc)], axis=-1)
    return out.astype(x.dtype)


def mlstm_scan(q, k, v, i_pre, f_pre, init):
    B, T = q.shape[:2]
    nc = T // CHUNK

    def chunks(a):
        a = a.astype(f32).reshape((B, nc, CHUNK) + a.shape[2:])
        return jnp.moveaxis(a, (1, 3), (0, 2))

    xs = (chunks(q) * (DK_M ** -0.5), chunks(k), chunks(v), chunks(i_pre),
          chunks(jax.nn.log_sigmoid(f_pre.astype(f32))))
    tril = jnp.tril(jnp.ones((CHUNK, CHUNK), dtype=bool))

    def step(carry, inp):
        C, n, m = carry
        qc, kc, vc, ic, lf = inp
        b = jnp.cumsum(lf, axis=-1)
        dmat = b[..., :, None] - b[..., None, :] + ic[..., None, :]
        dmat = jnp.where(tril, dmat, -jnp.inf)
        g = b + m[..., None]
        mt = jnp.maximum(g, jnp.max(dmat, axis=-1))
        w_inter = jnp.exp(g - mt)
        a = jnp.exp(dmat - mt[..., None]) * jnp.einsum('bhld,bhsd->bhls', qc, kc)
        num = (w_inter[..., None] * jnp.einsum('bhld,bhde->bhle', qc, C)
               + jnp.einsum('bhls,bhse->bhle', a, vc))
        den = w_inter * jnp.einsum('bhld,bhd->bhl', qc, n) + jnp.sum(a, axis=-1)
        h = num / jnp.maximum(jnp.abs(den), jnp.exp(-mt))[..., None]
        m_new = mt[..., -1]
        w_s = jnp.exp(b[..., -1:] - b + ic - m_new[..., None])
        decay = jnp.exp(g[..., -1] - m_new)
        C_new = decay[..., None, None] * C + jnp.einsum('bhld,bhle->bhde', kc * w_s[..., None], vc)
        n_new = decay[..., None] * n + jnp.einsum('bhl,bhld->bhd', w_s, kc)
        return (C_new, n_new, m_new), h

    C0, n0, m0 = init
    (C, n, m), h = lax.scan(step, (C0.astype(f32), n0.astype(f32), m0.astype(f32)), xs)
    h = jnp.moveaxis(h, (0, 2), (1, 3)).reshape(B, T, H_M, DV_M)
    return h, (C, n, m)


def mlstm_bidir(q, k, v, gp, init_f, init_b):
    flip = lambda a: jnp.flip(a, axis=1)
    h_f, st_f = mlstm_scan(q, k, v, gp[:, :, 0], gp[:, :, 1], init_f)
    h_b, st_b = mlstm_scan(flip(q), flip(k), flip(v), flip(gp[:, :, 2]), flip(gp[:, :, 3]), init_b)
    return h_f + flip(h_b), st_f, st_b


def spatial_gating(u, v, norm_g, w_s, b_s):
    B, T, W = v.shape
    nc = T // CHUNK
    vn = rmsnorm(v, norm_g).reshape(B, nc, CHUNK, SGU_GROUPS, W // SGU_GROUPS)
    mixed = jnp.einsum('gts,bcsgd->bctgd', w_s, vn) + b_s.T[:, :, None]
    return u * mixed.reshape(B, T, W)


def sink_softmax(s, sink):
    sk = jnp.broadcast_to(sink.astype(f32)[:, :, None, None], s.shape[:-1] + (1,))
    return jax.nn.softmax(jnp.concatenate([s, sk], axis=-1), axis=-1)[..., :-1]


def context_attention(q, k, v, sink):
    B, S = q.shape[:2]
    nb = S // QBLK
    qb = jnp.moveaxis(q.reshape(B, nb, QBLK, KV_A, GQA_G, HD_A), 1, 0)

    def block(qi):
        s = jnp.einsum('bqkgd,bpkd->bkgqp', qi, k).astype(f32) * (HD_A ** -0.5)
        p = sink_softmax(s, sink)
        return jnp.einsum('bkgqp,bpkd->bqkgd', p.astype(v.dtype), v)

    o = lax.map(block, qb)
    return jnp.moveaxis(o, 0, 1).reshape(B, S, H_A * HD_A)


def latent_attention(q, k, v, k_ctx, v_ctx, sink):
    B, T = q.shape[:2]
    nb = T // QBLK
    qb = q.reshape(B, nb, QBLK, KV_A, GQA_G, HD_A)

    def band(a):
        ap = jnp.pad(a, ((0, 0), (QBLK, QBLK), (0, 0), (0, 0))).reshape(B, nb + 2, QBLK, KV_A, HD_A)
        return jnp.concatenate([ap[:, :-2], ap[:, 1:-1], ap[:, 2:]], axis=2)

    kb, vb = band(k), band(v)
    j = jnp.arange(nb)[:, None, None]
    a = jnp.arange(QBLK)[None, :, None]
    r = jnp.arange(3 * QBLK)[None, None, :]
    kpos = (j - 1) * QBLK + r
    mask = (jnp.abs(r - QBLK - a) <= WINDOW) & (kpos >= 0) & (kpos < T)
    scale = HD_A ** -0.5
    s_band = jnp.einsum('bnqkgd,bnrkd->bnkgqr', qb, kb).astype(f32) * scale
    s_band = jnp.where(mask[None, :, None, None], s_band, -jnp.inf)
    s_ctx = jnp.einsum('bnqkgd,bpkd->bnkgqp', qb, k_ctx).astype(f32) * scale
    p = sink_softmax(jnp.concatenate([s_band, s_ctx], axis=-1), sink).astype(v.dtype)
    p_band, p_ctx = p[..., :3 * QBLK], p[..., 3 * QBLK:]
    o = (jnp.einsum('bnkgqr,bnrkd->bnqkgd', p_band, vb)
         + jnp.einsum('bnkgqp,bpkd->bnqkgd', p_ctx, v_ctx))
    return o.reshape(B, T, H_A * HD_A)


def conv_ffn(h, w_up, conv_w, conv_b, w_down):
    a = h @ w_up
    a = lax.conv_general_dilated(a, conv_w[:, None, :], window_strides=(1,), padding='SAME',
                                 dimension_numbers=('NWC', 'WIO', 'NWC'),
                                 feature_group_count=a.shape[-1]) + conv_b
    gate, up = jnp.split(a, 2, axis=-1)
    return (jax.nn.silu(gate) * up) @ w_down


def layer(x, mod, p, ctx):
    B, T, _ = x.shape
    sh1, sc1, g1, sh2, sc2, g2 = jnp.split(mod, 6, axis=-1)
    h = rmsnorm(x, p['norm1_g']) * (1 + sc1) + sh1
    mq, mk, mv, mo, mg, su, sv, aq, ak, av, gates = jnp.split(h @ p['w_in'], SPLIT_IDX, axis=-1)
    gp = mg.reshape(B, T, 4, H_M) + p['m_gate_b']
    if ctx is None:
        zero = (jnp.zeros((B, H_M, DK_M, DV_M), f32), jnp.zeros((B, H_M, DK_M), f32),
                jnp.zeros((B, H_M), f32))
        init_f, init_b = zero, zero
    else:
        init_f, init_b = ctx[2], ctx[3]
    hm, st_f, st_b = mlstm_bidir(mq.reshape(B, T, H_M, DK_M), mk.reshape(B, T, H_M, DK_M),
                                 mv.reshape(B, T, H_M, DV_M), gp, init_f, init_b)
    hm = rmsnorm(hm.astype(x.dtype), p['m_norm_g'].reshape(H_M, DV_M)).reshape(B, T, BRANCH_W)
    hm = hm * jax.nn.sigmoid(mo)
    hs = spatial_gating(jax.nn.gelu(su), jax.nn.gelu(sv), p['sgu_norm_g'], p['sgu_w'], p['sgu_b'])
    qa = aq.reshape(B, T, H_A, HD_A)
    ka = ak.reshape(B, T, KV_A, HD_A)
    va = av.reshape(B, T, KV_A, HD_A)
    sink = p['attn_sink'].reshape(KV_A, GQA_G)
    if ctx is None:
        ha = context_attention(qa, ka, va, sink)
    else:
        tabs = axial_rope_tables(T)
        ha = latent_attention(rope2d(qa, tabs), rope2d(ka, tabs), va, ctx[0], ctx[1], sink)
    gm, gs, ga = jnp.split(jax.nn.sigmoid(gates), 3, axis=-1)
    y = gm * (hm @ p['w_br_m']) + gs * (hs @ p['w_br_s']) + ga * (ha @ p['w_br_a'])
    x = x + g1 * (y @ p['w_out'])
    h2 = rmsnorm(x, p['norm2_g']) * (1 + sc2) + sh2
    x = x + g2 * conv_ffn(h2, p['ffn_up'], p['ffn_conv_w'], p['ffn_conv_b'], p['ffn_down'])
    return x, (ka, va, st_f, st_b)


def setup_inputs(seed: int = 0) -> dict:
    key = jax.random.key(seed)
    ks = jax.random.split(key, 32)

    def nrm(i, shape, scale):
        return jax.random.normal(ks[i], shape, jnp.float32) * scale

    L = DEPTH
    F2 = 2 * FFN_DIM
    gate_base = jnp.array([0.0, 3.0, 0.0, 3.0], jnp.float32)[None, :, None]
    return dict(
        x_prompt=nrm(0, (BATCH, SEQ, D_MODEL), 1.0),
        x_sample=nrm(1, (DEC_BATCH, DEC_SEQ, D_MODEL), 1.0),
        cache_k=nrm(2, (DEC_BATCH, L, PAST_LEN, KV_A, HD_A), 1.0),
        cache_v=nrm(3, (DEC_BATCH, L, PAST_LEN, KV_A, HD_A), 1.0),
        state_C=nrm(4, (DEC_BATCH, L, 2, H_M, DK_M, DV_M), 0.3),
        state_n=nrm(5, (DEC_BATCH, L, 2, H_M, DK_M), 0.3),
        state_m=nrm(6, (DEC_BATCH, L, 2, H_M), 1.0),
        c=nrm(7, (DEC_BATCH, D_MODEL), 1.0),
        c_ctx=nrm(8, (D_MODEL,), 1.0),
        ada_w=nrm(9, (L, D_MODEL, 6 * D_MODEL), 0.5 * D_MODEL ** -0.5),
        ada_b=nrm(10, (L, 6 * D_MODEL), 0.02),
        norm1_g=1.0 + nrm(11, (L, D_MODEL), 0.1),
        w_in=nrm(12, (L, D_MODEL, N_IN), D_MODEL ** -0.5),
        m_gate_b=gate_base + nrm(13, (L, 4, H_M), 0.5),
        m_norm_g=1.0 + nrm(14, (L, BRANCH_W), 0.1),
        sgu_norm_g=1.0 + nrm(15, (L, SGU_W), 0.1),
        sgu_w=nrm(16, (L, SGU_GROUPS, CHUNK, CHUNK), CHUNK ** -0.5),
        sgu_b=1.0 + nrm(17, (L, SGU_GROUPS, CHUNK), 0.1),
        attn_sink=nrm(18, (L, H_A), 1.0),
        w_br_m=nrm(19, (L, BRANCH_W, D_MODEL), BRANCH_W ** -0.5),
        w_br_s=nrm(20, (L, SGU_W, D_MODEL), SGU_W ** -0.5),
        w_br_a=nrm(21, (L, H_A * HD_A, D_MODEL), (H_A * HD_A) ** -0.5),
        w_out=nrm(22, (L, D_MODEL, D_MODEL), D_MODEL ** -0.5),
        norm2_g=1.0 + nrm(23, (L, D_MODEL), 0.1),
        ffn_up=nrm(24, (L, D_MODEL, F2), D_MODEL ** -0.5),
        ffn_conv_w=nrm(25, (L, CONV_W, F2), CONV_W ** -0.5),
        ffn_conv_b=nrm(26, (L, F2), 0.02),
        ffn_down=nrm(27, (L, FFN_DIM, D_MODEL), FFN_DIM ** -0.5),
        final_g=1.0 + nrm(28, (D_MODEL,), 0.1),
    )


def reference(x_prompt, x_sample, cache_k, cache_v, state_C, state_n, state_m, c, c_ctx,
              ada_w, ada_b, norm1_g, w_in, m_gate_b, m_norm_g, sgu_norm_g, sgu_w, sgu_b,
              attn_sink, w_br_m, w_br_s, w_br_a, w_out, norm2_g, ffn_up, ffn_conv_w,
              ffn_conv_b, ffn_down, final_g):
    def params(l):
        return dict(norm1_g=norm1_g[l], w_in=w_in[l], m_gate_b=m_gate_b[l], m_norm_g=m_norm_g[l],
                    sgu_norm_g=sgu_norm_g[l], sgu_w=sgu_w[l], sgu_b=sgu_b[l],
                    attn_sink=attn_sink[l], w_br_m=w_br_m[l], w_br_s=w_br_s[l],
                    w_br_a=w_br_a[l], w_out=w_out[l], norm2_g=norm2_g[l], ffn_up=ffn_up[l],
                    ffn_conv_w=ffn_conv_w[l], ffn_conv_b=ffn_conv_b[l], ffn_down=ffn_down[l])

    xp = x_prompt
    ks_l, vs_l, Cs_l, ns_l, ms_l = [], [], [], [], []
    for l in range(DEPTH):
        mod = (jax.nn.silu(c_ctx) @ ada_w[l] + ada_b[l])[None, None, :]
        xp, (k_l, v_l, st_f, st_b) = layer(xp, mod, params(l), None)
        ks_l.append(k_l)
        vs_l.append(v_l)
        Cs_l.append(jnp.stack([st_f[0], st_b[0]], axis=1))
        ns_l.append(jnp.stack([st_f[1], st_b[1]], axis=1))
        ms_l.append(jnp.stack([st_f[2], st_b[2]], axis=1))
    y_prompt = rmsnorm(xp, final_g)
    dt = x_prompt.dtype
    new_cache_k = jnp.stack(ks_l, axis=1)
    new_cache_v = jnp.stack(vs_l, axis=1)
    new_state_C = jnp.stack(Cs_l, axis=1).astype(dt)
    new_state_n = jnp.stack(ns_l, axis=1).astype(dt)
    new_state_m = jnp.stack(ms_l, axis=1).astype(dt)

    xs = x_sample
    for l in range(DEPTH):
        mod = (jax.nn.silu(c) @ ada_w[l] + ada_b[l])[:, None, :]
        ctx = (cache_k[:, l], cache_v[:, l],
               (state_C[:, l, 0], state_n[:, l, 0], state_m[:, l, 0]),
               (state_C[:, l, 1], state_n[:, l, 1], state_m[:, l, 1]))
        xs, _ = layer(xs, mod, params(l), ctx)
    y_sample = rmsnorm(xs, final_g)

    return (y_prompt, y_sample, new_cache_k, new_cache_v, new_state_C, new_state_n, new_state_m)
```

```python
import numpy as np
import concourse.bass as bass
import concourse.mybir as mybir
from concourse.bass_utils import run_bass_kernel_spmd

F32 = mybir.dt.float32
BF16 = mybir.dt.bfloat16
AF = mybir.ActivationFunctionType
ALU = mybir.AluOpType
AX = mybir.AxisListType
RING = 8
NEG = -1.0e30
EPS = 1e-6

D = 4096
T = 1024
KC = 32
NL = 2
NIN = 25632
FF = 11008
FC = 86
SBUF_TOP = 206 * 1024
C_MQ, C_MK, C_MV, C_MO, C_MG, C_SU, C_SV, C_AQ, C_AK, C_AV, C_GT = (
    0, 1024, 2048, 4096, 6144, 6176, 8224, 10272, 12320, 12832, 13344)

DEBUG_OUT = []


class Dep:
    __slots__ = ("w", "r")

    def __init__(self):
        self.w = {}
        self.r = {}


class Tl(Dep):
    __slots__ = ("a",)

    def __init__(self, a):
        Dep.__init__(self)
        self.a = a


class Ring:
    def __init__(self, tiles):
        self.t = tiles
        self.i = 0

    def next(self):
        t = self.t[self.i % len(self.t)]
        self.i += 1
        return t


class KB:
    def __init__(self):
        self.nc = bass.Bass("TRN2", target_bir_lowering=False)
        nc = self.nc
        self.eng = {"pe": nc.tensor, "act": nc.scalar, "dve": nc.vector, "pool": nc.gpsimd, "sp": nc.sync}
        self.sems = {}
        self.ccnt = {}
        for e in ("pe", "act", "dve", "pool"):
            self.sems[e] = nc.alloc_semaphore("c_" + e)
            self.ccnt[e] = 0
        self.dcnt = {}
        for q in ("sp", "pool", "act"):
            self.dcnt[q] = 0
            for i in range(RING):
                self.sems[(q, i)] = nc.alloc_semaphore("d_%s%d" % (q, i))
        self.known = {e: {} for e in self.eng}
        self.psum = nc.alloc_psum_tensor("psum_all", [128, 4096], F32).ap()
        self.pb = [Dep() for _ in range(8)]
        self.excl = set(id(d) for d in self.pb)
        self.n_ins = 0
        self.poff = (nc.sbuf_base + 63) // 64 * 64
        self.top = nc.sbuf_top
        self.abase = None
        self.aoff = None
        self.tn = 0
        self.bctr = 0
        self.pctr = 0
        self.wctr = 0

    def _alloc(self, shape, dt, off):
        self.tn += 1
        h = self.nc.alloc_sbuf_tensor_at("t%d" % self.tn, list(shape), dt, offset=off)
        return Tl(h.ap())

    @staticmethod
    def _nbytes(shape, dt):
        n = 1
        for s in shape[1:]:
            n *= s
        n *= 4 if dt == F32 else 2
        return (n + 63) // 64 * 64

    def ptile(self, shape, dt=F32):
        t = self._alloc(shape, dt, self.poff)
        self.poff += self._nbytes(shape, dt)
        return t

    def start_arena(self):
        self.abase = self.poff
        self.aoff = self.abase

    def tile(self, shape, dt=F32):
        nb = self._nbytes(shape, dt)
        assert self.aoff + nb <= self.top, ("sbuf overflow", self.aoff, nb)
        t = self._alloc(shape, dt, self.aoff)
        self.aoff += nb
        return t

    def ring(self, n, shape, dt=F32):
        return Ring([self.tile(shape, dt) for _ in range(n)])

    def stage_end(self):
        self.barrier()
        self.aoff = self.abase

    def bank(self, i, n=512, dt=F32):
        a = self.psum[:, i * 512:(i + 1) * 512]
        if dt == BF16:
            a = a.bitcast(BF16)
        return a[:, :n]

    def nbank(self):
        b = self.bctr % 8
        self.bctr += 1
        return b

    def npair(self):
        p = self.pctr % 4
        self.pctr += 1
        return p

    def op(self, e, fn, reads=(), writes=(), dma=False, merge=False):
        need = {}
        if not merge:
            for b in reads:
                for k, v in b.w.items():
                    if need.get(k, 0) < v:
                        need[k] = v
                if id(b) in self.excl:
                    for k, v in b.r.items():
                        if k != e and need.get(k, 0) < v:
                            need[k] = v
            for b in writes:
                for k, v in b.w.items():
                    if need.get(k, 0) < v:
                        need[k] = v
                for k, v in b.r.items():
                    if need.get(k, 0) < v:
                        need[k] = v
        if dma:
            i = self.dcnt[e] % RING
            key = (e, i)
            v = 16 * (self.dcnt[e] // RING + 1)
            if v > 16 and need.get(key, 0) < v - 16:
                need[key] = v - 16
            self.dcnt[e] += 1
        else:
            key = e
            self.ccnt[e] += 1
            v = self.ccnt[e]
        engine = self.eng[e]
        kn = self.known[e]
        for k2, v2 in need.items():
            if k2 == "pe" and e == "pe" and not dma:
                continue
            if kn.get(k2, 0) < v2:
                engine.wait_ge(self.sems[k2], v2)
                kn[k2] = v2
        ins = fn(engine)
        ins.then_inc(self.sems[key], 16 if dma else 1)
        self.n_ins += 1
        for b in reads:
            b.r[key] = v
        for b in writes:
            if merge:
                b.w[key] = v
            else:
                b.w = {key: v}
                b.r = {}
        return (key, v)

    def barrier(self):
        evs = []
        for e, c in self.ccnt.items():
            if c:
                evs.append((e, c))
        for q, c in self.dcnt.items():
            for i in range(RING):
                n = (c - i + RING - 1) // RING if c > i else 0
                if n:
                    evs.append(((q, i), 16 * n))
        for e in self.eng:
            kn = self.known[e]
            for k, v in evs:
                if kn.get(k, 0) < v:
                    self.eng[e].wait_ge(self.sems[k], v)
                    kn[k] = v

    def dma(self, q, out, in_, reads=(), writes=(), merge=False, slow=False):
        if slow:
            return self.op(q, lambda e: e.dma_start(out=out, in_=in_, allow_slow_non_contiguous=True),
                           reads, writes, dma=True, merge=merge)
        return self.op(q, lambda e: e.dma_start(out=out, in_=in_), reads, writes, dma=True, merge=merge)

    def act(self, out, in_, func, reads, writes, bias=None, scale=None, accum=None):
        kw = {}
        if bias is not None:
            kw["bias"] = bias
        if scale is not None:
            kw["scale"] = scale
        if accum is not None:
            kw["accum_out"] = accum
        return self.op("act", lambda e: e.activation(out, in_, func, **kw), reads, writes)

    def ts(self, out, in0, s1, s2, op0, op1, reads, writes, eng="dve"):
        if s2 is None:
            return self.op(eng, lambda e: e.tensor_scalar(out, in0, s1, None, op0=op0), reads, writes)
        return self.op(eng, lambda e: e.tensor_scalar(out, in0, s1, s2, op0=op0, op1=op1), reads, writes)

    def tt(self, out, in0, in1, op, reads, writes, eng="dve"):
        return self.op(eng, lambda e: e.tensor_tensor(out, in0, in1, op=op), reads, writes)

    def stt(self, out, in0, s, in1, op0, op1, reads, writes, accum=None, eng="dve"):
        if accum is not None:
            return self.op(eng, lambda e: e.scalar_tensor_tensor(out, in0, s, in1, op0=op0, op1=op1,
                                                                 accum_out=accum), reads, writes)
        return self.op(eng, lambda e: e.scalar_tensor_tensor(out, in0, s, in1, op0=op0, op1=op1), reads, writes)

    def copy(self, out, in_, reads, writes, eng="dve"):
        if eng == "act":
            return self.act(out, in_, AF.Identity, reads, writes)
        return self.op(eng, lambda e: e.tensor_copy(out, in_), reads, writes)

    def wload(self, slot, W2d, KCn, c0, ncols, dst0, first):
        for kk in range(0, KCn, 8):
            n = min(8, KCn - kk)
            src = W2d[kk * 128:(kk + n) * 128, c0:c0 + ncols].rearrange("(k p) n -> p k n", p=128)
            self.dma("pool", slot.a[:, kk:kk + n, dst0:dst0 + ncols], src, writes=[slot],
                     merge=not (first and kk == 0))

    def gemm_fm(self, W2d, KCn, xt, Tn, chunks, epi, slots, nb=4):
        nth = max(1, Tn // 512)
        tw = min(Tn, 512)
        for b0 in range(0, len(chunks), nb):
            blk = chunks[b0:b0 + nb]
            slot = slots[self.wctr % len(slots)]
            self.wctr += 1
            runs = []
            for i, c in enumerate(blk):
                if runs and runs[-1][0] + runs[-1][1] * 128 == c:
                    runs[-1][1] += 1
                else:
                    runs.append([c, 1, i])
            first = True
            for (c0, n, di) in runs:
                self.wload(slot, W2d, KCn, c0, n * 128, di * 128, first)
                first = False
            for i, c in enumerate(blk):
                pp = self.npair()
                banks = [self.pb[2 * pp + t] for t in range(nth)]

                def mm(e, i=i, pp=pp, slot=slot):
                    ins = None
                    for kc in range(KCn):
                        for t in range(nth):
                            ins = e.matmul(self.bank(2 * pp + t, tw), slot.a[:, kc, i * 128:(i + 1) * 128],
                                           xt.a[:, kc, t * 512:t * 512 + tw],
                                           start=(kc == 0), stop=(kc == KCn - 1))
                    return ins
                self.op("pe", mm, reads=[slot, xt], writes=banks)
                epi(b0 + i, self.psum[:, pp * 1024:pp * 1024 + Tn], banks)

    def gemm_tm(self, W2d, KCn, xt, ntiles, c0, ncols, epi, slots, bw=512):
        for cb in range(0, ncols, bw):
            n = min(bw, ncols - cb)
            slot = slots[self.wctr % len(slots)]
            self.wctr += 1
            self.wload(slot, W2d, KCn, c0 + cb, n, 0, True)
            for tt in range(ntiles):
                bk = self.nbank()

                def mm(e, tt=tt, bk=bk, slot=slot, n=n):
                    ins = None
                    for kc in range(KCn):
                        ins = e.matmul(self.bank(bk, n), xt.a[:, kc, tt * 128:(tt + 1) * 128],
                                       slot.a[:, kc, 0:n], start=(kc == 0), stop=(kc == KCn - 1))
                    return ins
                self.op("pe", mm, reads=[slot, xt], writes=[self.pb[bk]])
                epi(cb, n, tt, self.bank(bk, n), self.pb[bk])


def fmv(v):
    v = np.asarray(v, np.float32)
    return np.ascontiguousarray(v.reshape(-1, 128).T)


class Prog:
    def __init__(self, layers=(0, 1), groups=(0, 1), upto=99, dummy=(), nlw=NL, only=None):
        self.dummy = set(dummy)
        self.only = only
        self.kb = KB()
        kb = self.kb
        nc = kb.nc
        self.nc = nc
        self.layers = layers
        self.groups = groups
        self.upto = upto
        self.ins = {}
        self.outs = {}
        I = self.inp
        I("xg", [2, T, D])
        I("cvecF", [128, KC, 2])
        I("ck", [NL, 512, 512]); I("cv", [NL, 512, 512])
        I("sC", [NL, 2, 8, 128, 256]); I("sn", [NL, 2, 8, 128]); I("sm", [NL, 2, 8])
        I("ada_w", [nlw, D, 6 * D]); I("adabF", [128, NL, 192])
        I("n1F", [128, NL, KC]); I("n2F", [128, NL, KC]); I("nfF", [128, KC])
        I("w_in", [nlw, D, NIN])
        I("mgb", [NL, 32]); I("mng", [NL, 2048]); I("sgn", [NL, 2048])
        I("sgwT", [NL, 8, 128, 128]); I("sgb", [NL, 8, 128]); I("sink", [NL, 16])
        I("w_br_m", [nlw, 2048, D]); I("w_br_s", [nlw, 2048, D]); I("w_br_a", [nlw, 2048, D])
        I("w_out", [nlw, D, D]); I("ffn_up", [nlw, D, 2 * FF]); I("ffn_down", [nlw, FF, D])
        I("cwF", [128, NL, 3, 2 * FC]); I("cbF", [128, NL, 2 * FC])
        I("c_ident", [128, 128]); I("c_tri", [2, 128, 128]); I("c_sel", [2, 128, 128])
        I("c_mmask", [2, 128, 128]); I("c_amask", [128, 384]); I("c_rope", [2, 128, T]); I("c_perm", [128, 128])
        O = self.outp
        O("y", [2, T, D])
        O("nk", [NL, T, 512]); O("nv", [NL, T, 512])
        O("nC", [4, NL, 2, 8, 128, 256]); O("nn", [4, NL, 2, 8, 128]); O("nm", [4, NL, 2, 8])
        S = self.scr
        self.XT = S("XT", [KC, 128, T], F32)
        self.HT = S("HT", [KC, 128, T], BF16)
        self.QT = S("QT", [8, 128, T], BF16); self.KT = S("KT", [8, 128, T], BF16)
        self.UT = S("UT", [16, 128, T], BF16)
        self.AQT = S("AQT", [16, 128, T], BF16); self.AKT = S("AKT", [4, 128, T], BF16)
        self.GTS = S("GTS", [96, 128, T], BF16)
        self.MV = S("MV", [T, 2048], BF16); self.MO = S("MO", [T, 2048], BF16)
        self.MG = S("MG", [T, 32], F32); self.SV = S("SV", [T, 2048], F32)
        self.AV = S("AV", [T, 512], BF16)
        self.HMT = S("HMT", [16, 128, T], BF16); self.HST = S("HST", [16, 128, T], BF16)
        self.HAT = S("HAT", [16, 128, T], BF16)
        self.YT = S("YT", [KC, 128, T], BF16)
        self.GU = S("GU", [FC, 128, T], BF16)
        self.consts()

    def inp(self, name, shape):
        if name in self.dummy:
            shape = [1, 128, 128]
        self.ins[name] = self.nc.dram_tensor(name, list(shape), F32, kind="ExternalInput").ap()

    def outp(self, name, shape):
        self.outs[name] = self.nc.dram_tensor(name, list(shape), F32, kind="ExternalOutput").ap()

    def scr(self, name, shape, dt):
        if name in DEBUG_OUT:
            a = self.nc.dram_tensor("dbg_" + name, list(shape), dt, kind="ExternalOutput").ap()
            self.outs["dbg_" + name] = a
            return a
        return self.nc.dram_tensor(name, list(shape), dt).ap()

    def consts(self):
        kb = self.kb
        I = self.ins

        def ld(name, shape, src, dt=F32, q=None):
            t = kb.ptile(shape, dt)
            kb.dma(q or ("sp" if dt == F32 else "pool"), t.a, src, writes=[t])
            return t
        self.identf = ld("identf", [128, 128], I["c_ident"])
        self.identb = ld("identb", [128, 128], I["c_ident"], BF16)
        self.tri = [ld("tri%d" % d, [128, 128], I["c_tri"][d]) for d in range(2)]
        self.sel = [ld("sel%d" % d, [128, 128], I["c_sel"][d]) for d in range(2)]
        self.mmask = [ld("mm%d" % d, [128, 128], I["c_mmask"][d]) for d in range(2)]
        self.amask = ld("amask", [128, 384], I["c_amask"])
        self.cosT = ld("cos", [128, T], I["c_rope"][0])
        self.sinT = ld("sin", [128, T], I["c_rope"][1])
        self.perm = ld("perm", [128, 128], I["c_perm"])
        self.onesf = kb.ptile([128, 128], F32)
        kb.op("dve", lambda e: e.memset(self.onesf.a, 1.0), writes=[self.onesf])
        self.adab = ld("adab", [128, NL, 192], I["adabF"])
        self.n1 = ld("n1", [128, NL, KC], I["n1F"])
        self.n2 = ld("n2", [128, NL, KC], I["n2F"])
        self.nf = ld("nf", [128, KC], I["nfF"])
        self.cw = ld("cw", [128, NL, 3, 2 * FC], I["cwF"])
        self.cb = ld("cb", [128, NL, 2 * FC], I["cbF"])
        self.mgb = ld("mgb", [128, NL, 32], I["mgb"].partition_broadcast(128))
        self.sinkb = ld("sinkb", [128, NL, 16], I["sink"].partition_broadcast(128))
        self.MOD = kb.ptile([128, NL, 192, 2], F32)
        self.A = kb.ptile([128, KC], F32)
        self.zero = kb.ptile([128, 1], F32)
        kb.op("dve", lambda e: e.memset(self.zero.a, 0.0), writes=[self.zero])
        kb.start_arena()

    def stage_mod(self):
        kb = self.kb
        I = self.ins
        if "ada_w" in self.dummy:
            kb.op("dve", lambda e: e.memset(self.MOD.a, 0.0), writes=[self.MOD])
            kb.stage_end()
            return
        cf = kb.tile([128, KC, 2], F32)
        kb.dma("sp", cf.a, I["cvecF"], writes=[cf])
        xc = kb.tile([128, KC, 2], BF16)
        kb.act(xc.a, cf.a, AF.Silu, [cf], [xc])
        slots = [kb.tile([128, KC, 512], BF16) for _ in range(2)]
        for l in self.layers:
            def epi(ci, P, banks, l=l):
                kb.ts(self.MOD.a[:, l, ci, :], P, self.adab.a[:, l, ci:ci + 1], None, ALU.add, None,
                      banks + [self.adab], [self.MOD])
            kb.gemm_fm(I["ada_w"][l], KC, xc, 2, list(range(0, 6 * D, 128)), epi, slots)
        kb.stage_end()

    def modv(self, l, j, g):
        return self.MOD.a[:, l, j * KC:(j + 1) * KC, g]

    def stage_loadx(self, g):
        kb = self.kb
        xin = kb.ring(2, [128, D], F32)
        stg = kb.ring(4, [128, 4, 128], F32)
        for tt in range(8):
            xi = xin.next()
            kb.dma("sp", xi.a, self.ins["xg"][g, tt * 128:(tt + 1) * 128, :], writes=[xi])
            for k4 in range(8):
                bk = kb.nbank()

                def tr(e, xi=xi, k4=k4, bk=bk):
                    ins = None
                    for i in range(4):
                        c = k4 * 4 + i
                        ins = e.transpose(kb.bank(bk)[:, i * 128:(i + 1) * 128], xi.a[:, c * 128:(c + 1) * 128],
                                          self.identf.a)
                    return ins
                kb.op("pe", tr, reads=[xi, self.identf], writes=[kb.pb[bk]])
                s = stg.next()
                kb.copy(s.a, kb.bank(bk).rearrange("p (c t) -> p c t", c=4), [kb.pb[bk]], [s],
                        eng="act" if k4 % 2 else "dve")
                kb.dma("sp", self.XT[k4 * 4:(k4 + 1) * 4, :, tt * 128:(tt + 1) * 128].rearrange("c p t -> p c t"),
                       s.a, reads=[s])
        kb.stage_end()

    def stage_norm(self, l, g, which):
        kb = self.kb
        if which == 3:
            Aap = self.nf.a
            Adep = self.nf
        else:
            gn = (self.n1 if which == 1 else self.n2)
            j = 0 if which == 1 else 3
            kb.stt(self.A.a, self.modv(l, j + 1, g), 1.0, gn.a[:, l, :], ALU.add, ALU.mult,
                   [self.MOD, gn], [self.A])
            Aap = self.A.a
            Adep = self.A
        xin = kb.ring(3, [128, T], F32)
        sq = kb.ring(2, [128, T], F32)
        for kc in range(KC):
            xi = xin.next()
            kb.dma("sp", xi.a, self.XT[kc], writes=[xi])
            s = sq.next()
            kb.act(s.a, xi.a, AF.Square, [xi], [s])

            def mm(e, s=s, kc=kc):
                e.matmul(kb.bank(0), self.onesf.a, s.a[:, 0:512], start=(kc == 0), stop=(kc == KC - 1))
                return e.matmul(kb.bank(1), self.onesf.a, s.a[:, 512:1024], start=(kc == 0), stop=(kc == KC - 1))
            kb.op("pe", mm, reads=[s, self.onesf], writes=[kb.pb[0], kb.pb[1]])
        rstd = kb.tile([128, T], F32)
        kb.act(rstd.a, kb.psum[:, 0:T], AF.Ln, [kb.pb[0], kb.pb[1]], [rstd], scale=1.0 / D, bias=EPS)
        kb.act(rstd.a, rstd.a, AF.Exp, [rstd], [rstd], scale=-0.5)
        tmp = kb.ring(2, [128, T], F32)
        if which == 3:
            yf = kb.tile([128, KC, T], F32)
        else:
            hst = kb.ring(3, [128, T], BF16)
        for kc in range(KC):
            xi = xin.next()
            kb.dma("sp", xi.a, self.XT[kc], writes=[xi])
            tm = tmp.next()
            kb.tt(tm.a, xi.a, rstd.a, ALU.mult, [xi, rstd], [tm])
            if which == 3:
                kb.act(yf.a[:, kc, :], tm.a, AF.Identity, [tm, Adep], [yf], scale=Aap[:, kc:kc + 1])
            else:
                h = hst.next()
                kb.act(h.a, tm.a, AF.Identity, [tm, Adep, self.MOD], [h], scale=Aap[:, kc:kc + 1],
                       bias=self.modv(l, j, g)[:, kc:kc + 1])
                kb.dma("sp", self.HT[kc], h.a, reads=[h])
        if which == 3:
            yo = kb.ring(4, [128, 512], F32)
            for tt in range(8):
                for k4 in range(8):
                    bk = kb.nbank()

                    def tr(e, k4=k4, bk=bk, tt=tt):
                        ins = None
                        for i in range(4):
                            ins = e.transpose(kb.bank(bk)[:, i * 128:(i + 1) * 128],
                                              yf.a[:, k4 * 4 + i, tt * 128:(tt + 1) * 128], self.identf.a)
                        return ins
                    kb.op("pe", tr, reads=[yf, self.identf], writes=[kb.pb[bk]])
                    y = yo.next()
                    kb.copy(y.a, kb.bank(bk), [kb.pb[bk]], [y], eng="act" if k4 % 2 else "dve")
                    kb.dma("sp", self.outs["y"][g, tt * 128:(tt + 1) * 128, k4 * 512:(k4 + 1) * 512], y.a, reads=[y])
        kb.stage_end()

    def load_fm(self, dst, src, nch, q="sp"):
        for c0 in range(0, nch, 8):
            n = min(8, nch - c0)
            self.kb.dma(q, dst.a[:, c0:c0 + n, :], src[c0:c0 + n].rearrange("c p t -> p c t"),
                        writes=[dst], merge=(c0 > 0))

    def stage_inproj(self, l, g):
        kb = self.kb
        W = self.ins["w_in"][l]
        ht = kb.tile([128, KC, T], BF16)
        self.load_fm(ht, self.HT, KC)
        slots = [kb.tile([128, KC, 512], BF16) for _ in range(2)]
        ob = kb.ring(4, [128, T], BF16)
        of = kb.ring(3, [128, T], F32)
        tb = kb.ring(4, [128, 512], BF16)
        tf = kb.ring(4, [128, 512], F32)
        sample = (g == 1)

        def fm_plain(dst, func, scale, eng):
            def epi(ci, P, banks):
                o = ob.next()
                if eng == "act":
                    kb.act(o.a, P, func, banks, [o], scale=scale)
                else:
                    kb.ts(o.a, P, scale, 0.0, ALU.mult, ALU.add, banks, [o])
                kb.dma("sp", dst[ci], o.a, reads=[o])
            return epi

        def fm_rope(dst, scale):
            def epi(ci, P, banks):
                xf = of.next()
                kb.act(xf.a, P, AF.Identity, banks, [xf], scale=scale)
                pp = kb.npair()
                b2 = [kb.pb[2 * pp], kb.pb[2 * pp + 1]]

                def mm(e, pp=pp, xf=xf):
                    e.matmul(kb.bank(2 * pp), self.perm.a, xf.a[:, 0:512], start=True, stop=True)
                    return e.matmul(kb.bank(2 * pp + 1), self.perm.a, xf.a[:, 512:1024], start=True, stop=True)
                kb.op("pe", mm, reads=[xf, self.perm], writes=b2)
                t2 = of.next()
                kb.tt(t2.a, kb.psum[:, pp * 1024:(pp + 1) * 1024], self.sinT.a, ALU.mult, b2 + [self.sinT], [t2])
                kb.tt(xf.a, xf.a, self.cosT.a, ALU.mult, [xf, self.cosT], [xf])
                o = ob.next()
                kb.tt(o.a, xf.a, t2.a, ALU.add, [xf, t2], [o])
                kb.dma("sp", dst[ci], o.a, reads=[o])
            return epi

        def ch(c0, n):
            return [c0 + 128 * i for i in range(n)]
        on = lambda k: self.only is None or k in self.only
        if on("mq"):
            kb.gemm_fm(W, KC, ht, T, ch(C_MQ, 8), fm_plain(self.QT, None, 128 ** -0.5, "dve"), slots)
        if on("mk"):
            kb.gemm_fm(W, KC, ht, T, ch(C_MK, 8), fm_plain(self.KT, None, 1.0, "dve"), slots)
        if on("su"):
            kb.gemm_fm(W, KC, ht, T, ch(C_SU, 16), fm_plain(self.UT, AF.Gelu_apprx_tanh, 1.0, "act"), slots)
        if not on("aq"):
            pass
        elif sample:
            kb.gemm_fm(W, KC, ht, T, ch(C_AQ, 16), fm_rope(self.AQT, 128 ** -0.5), slots)
            kb.gemm_fm(W, KC, ht, T, ch(C_AK, 4), fm_rope(self.AKT, 1.0), slots)
        else:
            kb.gemm_fm(W, KC, ht, T, ch(C_AQ, 16), fm_plain(self.AQT, None, 128 ** -0.5, "dve"), slots)
            kb.gemm_fm(W, KC, ht, T, ch(C_AK, 4), fm_plain(self.AKT, None, 1.0, "dve"), slots)
        if on("gt"):
            kb.gemm_fm(W, KC, ht, T, ch(C_GT, 96), fm_plain(self.GTS, AF.Sigmoid, 1.0, "act"), slots)

        def tm_bf16(dst, func):
            def epi(cb, n, tt, P, bd):
                o = tb.next()
                if func is None:
                    kb.copy(o.a[:, :n], P, [bd], [o])
                else:
                    kb.act(o.a[:, :n], P, func, [bd], [o])
                kb.dma("sp", dst[tt * 128:(tt + 1) * 128, cb:cb + n], o.a[:, :n], reads=[o])
            return epi

        def tm_f32(dst, func, also_bf16=None):
            def epi(cb, n, tt, P, bd):
                o = tf.next()
                if func is None:
                    kb.copy(o.a[:, :n], P, [bd], [o], eng="act")
                else:
                    kb.act(o.a[:, :n], P, func, [bd], [o])
                if dst is not None:
                    kb.dma("sp", dst[tt * 128:(tt + 1) * 128, cb:cb + n], o.a[:, :n], reads=[o])
                if also_bf16 is not None:
                    o2 = tb.next()
                    kb.copy(o2.a[:, :n], o.a[:, :n], [o], [o2])
                    kb.dma("sp", also_bf16[tt * 128:(tt + 1) * 128, cb:cb + n], o2.a[:, :n], reads=[o2])
            return epi

        def tm_mg(cb, n, tt, P, bd):
            o = tf.next()
            kb.tt(o.a[:, :32], P, self.mgb.a[:, l, :], ALU.add, [bd, self.mgb], [o])
            kb.dma("sp", self.MG[tt * 128:(tt + 1) * 128, :], o.a[:, :32], reads=[o])
        if on("mv"):
            kb.gemm_tm(W, KC, ht, 8, C_MV, 2048, tm_bf16(self.MV, None), slots)
        if on("mo"):
            kb.gemm_tm(W, KC, ht, 8, C_MO, 2048, tm_bf16(self.MO, AF.Sigmoid), slots)
        if on("mg"):
            kb.gemm_tm(W, KC, ht, 8, C_MG, 32, tm_mg, slots)
        if on("sv"):
            kb.gemm_tm(W, KC, ht, 8, C_SV, 2048, tm_f32(self.SV, AF.Gelu_apprx_tanh), slots)
        if not on("av"):
            pass
        elif sample:
            kb.gemm_tm(W, KC, ht, 8, C_AV, 512, tm_bf16(self.AV, None), slots)
        else:
            kb.gemm_tm(W, KC, ht, 8, C_AV, 512, tm_f32(self.outs["nv"][l], None, also_bf16=self.AV), slots)
            kb.gemm_tm(W, KC, ht, 8, C_AK, 512, tm_f32(self.outs["nk"][l], None), slots)
        kb.stage_end()

    def stage_mlstm(self, l, g):
        kb = self.kb
        sample = (g == 1)
        nseq = 1 if sample else 4
        ncs = 8 if sample else 2
        B3 = [128, 4, 128]
        H = range(4)
        for hh in range(2):
            hms = [kb.tile([128, 8, 4, 256], F32) for _ in range(2)]
            mark = kb.aoff
            qts = kb.tile([128, 4, T], BF16)
            kts = kb.tile([128, 4, T], BF16)
            self.load_fm(qts, self.QT[hh * 4:(hh + 1) * 4], 4)
            self.load_fm(kts, self.KT[hh * 4:(hh + 1) * 4], 4)
            km = kb.tile([128, 8, 4, 128], BF16)
            for tt in range(8):
                bk = kb.nbank()

                def tr(e, tt=tt, bk=bk):
                    ins = None
                    for j in range(4):
                        ins = e.transpose(kb.bank(bk, 512, BF16)[:, j * 128:(j + 1) * 128],
                                          kts.a[:, j, tt * 128:(tt + 1) * 128], self.identb.a)
                    return ins
                kb.op("pe", tr, reads=[kts, self.identb], writes=[kb.pb[bk]])
                kb.copy(km.a[:, tt, :, :], kb.bank(bk, 512, BF16).rearrange("p (j d) -> p j d", j=4),
                        [kb.pb[bk]], [km], eng="act" if tt % 2 else "dve")
            va = kb.tile([128, 8, 4, 260], BF16)
            kb.op("dve", lambda e: e.memset(va.a[:, :, :, 256:260], 1.0), writes=[va])
            for tt in range(8):
                kb.dma("sp", va.a[:, tt, :, 0:256],
                       self.MV[tt * 128:(tt + 1) * 128, hh * 1024:(hh + 1) * 1024].rearrange("p (j d) -> p j d", j=4),
                       writes=[va], merge=True)
            mgs = kb.tile([128, 8, 32], F32)
            kb.dma("sp", mgs.a, self.MG.rearrange("(t p) c -> p t c", p=128), writes=[mgs])
            cms = [kb.tile([128, 4, 260], F32) for _ in range(2)]
            cbs = [kb.tile([128, 4, 260], BF16) for _ in range(2)]
            mbcs = [kb.tile([128, 4], F32) for _ in range(2)]
            sm = kb.ring(96, [128, 8], F32)
            dus = kb.ring(3, B3, F32)
            dms = kb.ring(3, B3, F32)
            es = kb.ring(3, B3, F32)
            abf = kb.ring(8, [128, 128], BF16)
            ats = kb.ring(8, [128, 128], BF16)
            kws = kb.ring(8, [128, 128], BF16)
            t1s = kb.ring(8, [128, 260], F32)
            nds = kb.ring(8, [128, 260], F32)

            def step(d, tt, do_update):
                cm, cb, mbc, hm = cms[d], cbs[d], mbcs[d], hms[d]
                tok = slice(tt * 128, (tt + 1) * 128)
                gi = mgs.a[:, tt, (2 * d) * 8 + hh * 4:(2 * d) * 8 + hh * 4 + 4]
                gf = mgs.a[:, tt, (2 * d + 1) * 8 + hh * 4:(2 * d + 1) * 8 + hh * 4 + 4]
                e1 = sm.next()
                kb.act(e1.a[:, :4], gf, AF.Exp, [mgs], [e1], scale=-1.0)
                lfn = sm.next()
                kb.act(lfn.a[:, :4], e1.a[:, :4], AF.Ln, [e1], [lfn], bias=1.0)
                yield
                bk = kb.nbank()
                kb.op("pe", lambda e: e.matmul(kb.bank(bk, 4), self.tri[d].a, lfn.a[:, :4], start=True, stop=True),
                      [self.tri[d], lfn], [kb.pb[bk]])
                yield
                bm = sm.next()
                kb.copy(bm.a[:, 0:4], kb.bank(bk, 4), [kb.pb[bk]], [bm])
                u = sm.next()
                kb.tt(u.a[:, :4], gi, bm.a[:, 0:4], ALU.add, [mgs, bm], [u])
                du = dus.next()
                kb.tt(du.a, self.identf.a.unsqueeze(1).to_broadcast(B3),
                      u.a[:, :4].unsqueeze(2).to_broadcast(B3), ALU.mult, [self.identf, u], [du])
                yield
                bku = kb.nbank()
                kb.op("pe", lambda e: e.matmul(kb.bank(bku), self.onesf.a, du.a.rearrange("p j s -> p (j s)"),
                                               start=True, stop=True), [self.onesf, du], [kb.pb[bku]])
                yield
                dm = dms.next()
                kb.tt(dm.a, kb.bank(bku).rearrange("p (j s) -> p j s", j=4),
                      self.mmask[d].a.unsqueeze(1).to_broadcast(B3), ALU.add, [kb.pb[bku], self.mmask[d]], [dm])
                cmax = sm.next()
                kb.op("dve", lambda e: e.tensor_reduce(cmax.a[:, :4], dm.a, axis=AX.X, op=ALU.max), [dm], [cmax])
                mx = sm.next()
                kb.tt(mx.a[:, :4], cmax.a[:, :4], mbc.a, ALU.max, [cmax, mbc], [mx])
                nmx = sm.next()
                kb.ts(nmx.a[:, :4], mx.a[:, :4], -1.0, 0.0, ALU.mult, ALU.add, [mx], [nmx])
                wi = sm.next()
                kb.tt(wi.a[:, :4], mbc.a, mx.a[:, :4], ALU.subtract, [mbc, mx], [wi])
                kb.tt(bm.a[:, 4:8], mx.a[:, :4], bm.a[:, 0:4], ALU.subtract, [mx, bm], [bm])
                yield
                E = es.next()
                for j in H:
                    kb.act(E.a[:, j, :], dm.a[:, j, :], AF.Exp, [dm, nmx], [E], bias=nmx.a[:, j:j + 1])
                kb.act(wi.a[:, :4], wi.a[:, :4], AF.Exp, [wi], [wi])
                emt = sm.next()
                kb.act(emt.a[:, :4], bm.a[:, 4:8], AF.Exp, [bm], [emt], scale=-1.0)
                yield
                bss = []
                for j in H:
                    bs = kb.nbank()
                    bss.append(bs)
                    kb.op("pe", lambda e, bs=bs, j=j: e.matmul(kb.bank(bs, 128), qts.a[:, j, tok], kts.a[:, j, tok],
                                                              start=True, stop=True), [qts, kts], [kb.pb[bs]])
                yield
                abl = []
                for j in H:
                    ab = abf.next()
                    abl.append(ab)
                    kb.tt(ab.a, kb.bank(bss[j], 128), E.a[:, j, :], ALU.mult, [kb.pb[bss[j]], E], [ab])
                yield
                b1s = []
                for j in H:
                    b1 = kb.nbank()
                    b1s.append(b1)
                    kb.op("pe", lambda e, b1=b1, j=j: e.matmul(kb.bank(b1, 257), qts.a[:, j, tok], cb.a[:, j, 0:257],
                                                              start=True, stop=True), [qts, cb], [kb.pb[b1]])
                yield
                t1l = []
                for j in H:
                    t1 = t1s.next()
                    t1l.append(t1)
                    kb.act(t1.a[:, :257], kb.bank(b1s[j], 257), AF.Identity, [kb.pb[b1s[j]], wi], [t1],
                           scale=wi.a[:, j:j + 1])
                yield
                btl = []
                for j in H:
                    bt = kb.nbank()
                    btl.append(bt)
                    kb.op("pe", lambda e, bt=bt, ab=abl[j]: e.transpose(kb.bank(bt, 128, BF16), ab.a, self.identb.a),
                          [abl[j], self.identb], [kb.pb[bt]])
                yield
                atl = []
                for j in H:
                    at = ats.next()
                    atl.append(at)
                    kb.copy(at.a, kb.bank(btl[j], 128, BF16), [kb.pb[btl[j]]], [at], eng="act" if j % 2 else "dve")
                yield
                b2s = []
                for j in H:
                    b2 = kb.nbank()
                    b2s.append(b2)
                    kb.op("pe", lambda e, b2=b2, j=j, at=atl[j]: e.matmul(kb.bank(b2, 257), at.a, va.a[:, tt, j, 0:257],
                                                                         start=True, stop=True),
                          [atl[j], va], [kb.pb[b2]])
                yield
                ndl, denl = [], []
                for j in H:
                    nd = nds.next()
                    ndl.append(nd)
                    kb.tt(nd.a[:, :257], t1l[j].a[:, :257], kb.bank(b2s[j], 257), ALU.add, [t1l[j], kb.pb[b2s[j]]], [nd])
                yield
                for j in H:
                    den = sm.next()
                    denl.append(den)
                    kb.act(den.a[:, 0:1], ndl[j].a[:, 256:257], AF.Abs, [ndl[j]], [den])
                yield
                for j in H:
                    den = denl[j]
                    kb.tt(den.a[:, 0:1], den.a[:, 0:1], emt.a[:, j:j + 1], ALU.max, [den, emt], [den])
                    kb.op("dve", lambda e, den=den: e.reciprocal(den.a[:, 0:1], den.a[:, 0:1]), [den], [den])
                for j in H:
                    kb.ts(hm.a[:, tt, j, :], ndl[j].a[:, 0:256], denl[j].a[:, 0:1], None, ALU.mult, None,
                          [ndl[j], denl[j]], [hm])
                yield
                if not do_update:
                    return
                bsel = kb.nbank()
                kb.op("pe", lambda e: e.matmul(kb.bank(bsel, 8), self.sel[d].a, bm.a[:, 0:8], start=True, stop=True),
                      [self.sel[d], bm], [kb.pb[bsel]])
                yield
                lst = sm.next()
                kb.copy(lst.a, kb.bank(bsel, 8), [kb.pb[bsel]], [lst])
                tmp = sm.next()
                kb.tt(tmp.a[:, :4], lst.a[:, 0:4], lst.a[:, 4:8], ALU.add, [lst], [tmp])
                kb.ts(tmp.a[:, :4], tmp.a[:, :4], -1.0, 0.0, ALU.mult, ALU.add, [tmp], [tmp])
                ws = sm.next()
                kb.tt(ws.a[:, :4], u.a[:, :4], tmp.a[:, :4], ALU.add, [u, tmp], [ws])
                dc = sm.next()
                kb.tt(dc.a[:, :4], mbc.a, tmp.a[:, :4], ALU.add, [mbc, tmp], [dc])
                yield
                kb.act(ws.a[:, :4], ws.a[:, :4], AF.Exp, [ws], [ws])
                kb.act(dc.a[:, :4], dc.a[:, :4], AF.Exp, [dc], [dc])
                yield
                kwl = []
                for j in H:
                    kw = kws.next()
                    kwl.append(kw)
                    kb.ts(kw.a, km.a[:, tt, j, :], ws.a[:, j:j + 1], None, ALU.mult, None, [km, ws], [kw])
                yield
                bul = []
                for j in H:
                    bu = kb.nbank()
                    bul.append(bu)
                    kb.op("pe", lambda e, bu=bu, j=j, kw=kwl[j]: e.matmul(kb.bank(bu, 257), kw.a, va.a[:, tt, j, 0:257],
                                                                         start=True, stop=True),
                          [kwl[j], va], [kb.pb[bu]])
                yield
                for j in H:
                    kb.stt(cm.a[:, j, 0:257], cm.a[:, j, 0:257], dc.a[:, j:j + 1], kb.bank(bul[j], 257),
                           ALU.mult, ALU.add, [cm, dc, kb.pb[bul[j]]], [cm])
                yield
                kb.copy(cb.a, cm.a, [cm], [cb], eng="act")
                kb.copy(mbc.a, lst.a[:, 4:8], [lst], [mbc])
                yield

            for sq in range(nseq):
                for d in range(2):
                    cm, cb, mbc = cms[d], cbs[d], mbcs[d]
                    kb.op("dve", lambda e, cm=cm: e.memset(cm.a, 0.0), writes=[cm])
                    if sample:
                        kb.dma("sp", cm.a[:, :, 0:256],
                               self.ins["sC"][l, d, hh * 4:(hh + 1) * 4].rearrange("j k v -> k j v"), writes=[cm])
                        kb.dma("sp", cm.a[:, :, 256:257],
                               self.ins["sn"][l, d, hh * 4:(hh + 1) * 4, :].rearrange("j (k o) -> k j o", o=1),
                               writes=[cm], merge=True, slow=True)
                        kb.dma("sp", mbc.a, self.ins["sm"][l, d, hh * 4:(hh + 1) * 4].partition_broadcast(128),
                               writes=[mbc])
                    else:
                        kb.op("dve", lambda e, mbc=mbc: e.memset(mbc.a, 0.0), writes=[mbc])
                    kb.copy(cb.a, cm.a, [cm], [cb], eng="act")
                for ci in range(ncs):
                    last = (ci == ncs - 1)
                    gens = [step(0, sq * ncs + ci, not (sample and last)),
                            step(1, sq * ncs + (ncs - 1 - ci), not (sample and last))]
                    alive = [True, True]
                    while any(alive):
                        for d in range(2):
                            if alive[d]:
                                try:
                                    next(gens[d])
                                except StopIteration:
                                    alive[d] = False
                if not sample:
                    O = self.outs
                    for d in range(2):
                        cm, mbc = cms[d], mbcs[d]
                        kb.dma("sp", O["nC"][sq, l, d, hh * 4:(hh + 1) * 4].rearrange("j k v -> k j v"),
                               cm.a[:, :, 0:256], reads=[cm])
                        kb.dma("sp", O["nn"][sq, l, d, hh * 4:(hh + 1) * 4, :].rearrange("j (k o) -> k j o", o=1),
                               cm.a[:, :, 256:257], reads=[cm], slow=True)
                        kb.dma("sp", O["nm"][sq, l, d:d + 1, hh * 4:(hh + 1) * 4], mbc.a[0:1, :], reads=[mbc])
            kb.barrier()
            kb.aoff = mark
            hm = hms[0]
            kb.tt(hm.a, hm.a, hms[1].a, ALU.add, [hm, hms[1]], [hm])
            mos = kb.tile([128, 8, 1024], BF16)
            kb.dma("sp", mos.a, self.MO[:, hh * 1024:(hh + 1) * 1024].rearrange("(t p) c -> p t c", p=128),
                   writes=[mos])
            mng = kb.tile([128, 1024], F32)
            kb.dma("sp", mng.a, self.ins["mng"][l, hh * 1024:(hh + 1) * 1024].partition_broadcast(128), writes=[mng])
            sqt = kb.tile([128, 8, 4, 256], F32)
            kb.tt(sqt.a, hm.a, hm.a, ALU.mult, [hm], [sqt])
            ss = kb.tile([128, 32], F32)
            kb.op("dve", lambda e: e.tensor_reduce(ss.a, sqt.a.rearrange("p t j d -> p (t j) d"), axis=AX.X,
                                                   op=ALU.add), [sqt], [ss])
            kb.act(ss.a, ss.a, AF.Ln, [ss], [ss], scale=1.0 / 256, bias=EPS)
            kb.act(ss.a, ss.a, AF.Exp, [ss], [ss], scale=-0.5)
            hm3 = hm.a.rearrange("p t j d -> p (t j) d")
            kb.tt(hm3, hm3, ss.a.unsqueeze(2).to_broadcast([128, 32, 256]), ALU.mult, [hm, ss], [hm])
            hm2 = hm.a.rearrange("p t j d -> p t (j d)")
            kb.tt(hm2, hm2, mng.a.unsqueeze(1).to_broadcast([128, 8, 1024]), ALU.mult, [hm, mng], [hm])
            hb = kb.tile([128, 8, 1024], BF16)
            kb.tt(hb.a, hm2, mos.a, ALU.mult, [hm, mos], [hb])
            hmts = kb.tile([128, 8, T], BF16)
            for tt in range(8):
                for c4 in range(2):
                    bk = kb.nbank()

                    def tr(e, tt=tt, c4=c4, bk=bk):
                        ins = None
                        for i in range(4):
                            c = c4 * 4 + i
                            ins = e.transpose(kb.bank(bk, 512, BF16)[:, i * 128:(i + 1) * 128],
                                              hb.a[:, tt, c * 128:(c + 1) * 128], self.identb.a)
                        return ins
                    kb.op("pe", tr, reads=[hb, self.identb], writes=[kb.pb[bk]])
                    kb.copy(hmts.a[:, c4 * 4:(c4 + 1) * 4, tt * 128:(tt + 1) * 128],
                            kb.bank(bk, 512, BF16).rearrange("p (c t) -> p c t", c=4), [kb.pb[bk]], [hmts],
                            eng="act" if c4 else "dve")
            kb.dma("sp", self.HMT[hh * 8:(hh + 1) * 8].rearrange("c p t -> p c t"), hmts.a, reads=[hmts])
            kb.stage_end()

    def stage_sgu(self, l, g):
        kb = self.kb
        svs = kb.tile([128, 8, 2048], F32)
        for tt in range(8):
            kb.dma("sp", svs.a[:, tt, :], self.SV[tt * 128:(tt + 1) * 128, :], writes=[svs], merge=(tt > 0))
        sgn = kb.tile([128, 2048], F32)
        kb.dma("sp", sgn.a, self.ins["sgn"][l].partition_broadcast(128), writes=[sgn])
        wt = kb.tile([128, 8, 128], BF16)
        kb.dma("pool", wt.a, self.ins["sgwT"][l].rearrange("g s t -> s g t"), writes=[wt])
        bs = kb.tile([128, 8, 128], F32)
        kb.dma("sp", bs.a, self.ins["sgb"][l].partition_broadcast(128), writes=[bs])
        junk = kb.tile([128, 2048], F32)
        ss = kb.tile([128, 8], F32)
        kb.op("dve", lambda e: e.memset(ss.a, 0.0), writes=[ss])
        for tt in range(8):
            kb.act(junk.a, svs.a[:, tt, :], AF.Square, [svs], [junk, ss], accum=ss.a[:, tt:tt + 1])
        kb.act(ss.a, ss.a, AF.Ln, [ss], [ss], scale=1.0 / 2048, bias=EPS)
        kb.act(ss.a, ss.a, AF.Exp, [ss], [ss], scale=-0.5)
        vn = kb.tile([128, 8, 2048], BF16)
        for tt in range(8):
            kb.stt(vn.a[:, tt, :], svs.a[:, tt, :], ss.a[:, tt:tt + 1], sgn.a, ALU.mult, ALU.mult,
                   [svs, ss, sgn], [vn])
        utr = kb.ring(3, [128, T], BF16)
        tmp = kb.ring(2, [128, 4, 128], F32)
        og = kb.ring(3, [128, T], BF16)
        for fc in range(16):
            gidx = fc // 2
            ut = utr.next()
            kb.dma("sp", ut.a, self.UT[fc], writes=[ut])
            o = og.next()
            for half in range(2):
                bk = kb.nbank()

                def mm(e, half=half, bk=bk, fc=fc, gidx=gidx):
                    ins = None
                    for i in range(4):
                        tt = half * 4 + i
                        ins = e.matmul(kb.bank(bk)[:, i * 128:(i + 1) * 128], vn.a[:, tt, fc * 128:(fc + 1) * 128],
                                       wt.a[:, gidx, :], start=True, stop=True)
                    return ins
                kb.op("pe", mm, reads=[vn, wt], writes=[kb.pb[bk]])
                t = tmp.next()
                kb.tt(t.a, kb.bank(bk).rearrange("p (c t) -> p c t", c=4),
                      bs.a[:, gidx, :].unsqueeze(1).to_broadcast([128, 4, 128]), ALU.add, [kb.pb[bk], bs], [t])
                kb.tt(o.a[:, half * 512:(half + 1) * 512], t.a.rearrange("p c t -> p (c t)"),
                      ut.a[:, half * 512:(half + 1) * 512], ALU.mult, [t, ut], [o])
            kb.dma("sp", self.HST[fc], o.a, reads=[o])
        kb.stage_end()

    def stage_attn(self, l, g):
        kb = self.kb
        sample = (g == 1)
        aq = kb.tile([128, 16, T], BF16)
        self.load_fm(aq, self.AQT, 16)
        ak = kb.tile([128, 4, T], BF16)
        self.load_fm(ak, self.AKT, 4)
        av = kb.tile([128, 8, 512], BF16)
        kb.dma("sp", av.a, self.AV.rearrange("(t p) c -> p t c", p=128), writes=[av])
        if sample:
            ckt = kb.tile([128, 4, 512], BF16)
            kb.dma("pool", ckt.a, self.ins["ck"][l].rearrange("(t p) c -> p t c", p=128), writes=[ckt])
            kcT = kb.tile([128, 4, 512], BF16)
            for k in range(4):
                bk = kb.nbank()

                def tr(e, k=k, bk=bk):
                    ins = None
                    for t in range(4):
                        ins = e.transpose(kb.bank(bk, 512, BF16)[:, t * 128:(t + 1) * 128],
                                          ckt.a[:, t, k * 128:(k + 1) * 128], self.identb.a)
                    return ins
                kb.op("pe", tr, reads=[ckt, self.identb], writes=[kb.pb[bk]])
                kb.copy(kcT.a[:, k, :], kb.bank(bk, 512, BF16), [kb.pb[bk]], [kcT])
            vc = kb.tile([128, 4, 512], BF16)
            kb.dma("pool", vc.a, self.ins["cv"][l].rearrange("(t p) c -> p t c", p=128), writes=[vc])
        hat = kb.tile([128, 16, T], BF16)
        sring = kb.ring(5, [128, 896], F32)
        pring = kb.ring(4, [128, 896], F32)
        pbr = kb.ring(5, [128, 896], BF16)
        ptr = kb.ring(2, [128, 7, 4, 128], BF16)
        sm = kb.ring(32, [128, 8], F32)
        nseq, nqb = (1, 8) if sample else (4, 2)
        for sq in range(nseq):
            for k in range(4):
                for j in range(nqb):
                    qt = sq * nqb + j
                    qtok = slice(qt * 128, (qt + 1) * 128)
                    if sample:
                        lo, hi = max(0, j - 1), min(7, j + 1)
                        nb = (hi - lo + 1) * 128
                        m0 = (lo - (j - 1)) * 128
                        nkeys = 512 + nb
                        vblocks = [(vc, t) for t in range(4)] + [(av, t) for t in range(lo, hi + 1)]
                    else:
                        nkeys = 256
                        vblocks = [(av, sq * 2), (av, sq * 2 + 1)]
                    nkb = nkeys // 128
                    pT = ptr.next()
                    H = range(4)
                    pps = [kb.npair() for _ in H]
                    ss_, mxs, ps_, rss, pbs, bts = [], [], [], [], [], []
                    for g4 in H:
                        h = 4 * k + g4
                        pp = pps[g4]
                        b0, b1 = kb.pb[2 * pp], kb.pb[2 * pp + 1]
                        if sample:
                            def mm(e, pp=pp, h=h, lo=lo, nb=nb):
                                e.matmul(kb.bank(2 * pp), aq.a[:, h, qtok], kcT.a[:, k, :], start=True, stop=True)
                                return e.matmul(kb.bank(2 * pp + 1, nb), aq.a[:, h, qtok],
                                                ak.a[:, k, lo * 128:lo * 128 + nb], start=True, stop=True)
                            kb.op("pe", mm, reads=[aq, ak, kcT], writes=[b0, b1])
                        else:
                            kb.op("pe", lambda e, pp=pp, h=h: e.matmul(kb.bank(2 * pp, 256), aq.a[:, h, qtok],
                                                                      ak.a[:, k, sq * 256:(sq + 1) * 256],
                                                                      start=True, stop=True),
                                  reads=[aq, ak], writes=[b0])
                    for g4 in H:
                        pp = pps[g4]
                        b0, b1 = kb.pb[2 * pp], kb.pb[2 * pp + 1]
                        s = sring.next()
                        ss_.append(s)
                        if sample:
                            kb.copy(s.a[:, 0:512], kb.bank(2 * pp), [b0], [s], eng="act")
                            kb.tt(s.a[:, 512:512 + nb], kb.bank(2 * pp + 1, nb), self.amask.a[:, m0:m0 + nb], ALU.add,
                                  [b1, self.amask, s], [s])
                        else:
                            kb.copy(s.a[:, 0:256], kb.bank(2 * pp, 256), [b0], [s], eng="act")
                    for g4 in H:
                        s = ss_[g4]
                        mx = sm.next()
                        mxs.append(mx)
                        kb.op("dve", lambda e, mx=mx, s=s: e.tensor_reduce(mx.a[:, 0:1], s.a[:, :nkeys], axis=AX.X,
                                                                          op=ALU.max), [s], [mx])
                    for g4 in H:
                        sk = self.sinkb.a[:, l, 4 * k + g4:4 * k + g4 + 1]
                        mx = mxs[g4]
                        kb.tt(mx.a[:, 0:1], mx.a[:, 0:1], sk, ALU.max, [mx, self.sinkb], [mx])
                        kb.tt(mx.a[:, 2:3], sk, mx.a[:, 0:1], ALU.subtract, [mx, self.sinkb], [mx])
                        kb.ts(mx.a[:, 1:2], mx.a[:, 0:1], -1.0, 0.0, ALU.mult, ALU.add, [mx], [mx])
                        rs = sm.next()
                        rss.append(rs)
                        kb.op("dve", lambda e, rs=rs: e.memset(rs.a, 0.0), writes=[rs])
                    for g4 in H:
                        s, mx, rs = ss_[g4], mxs[g4], rss[g4]
                        p = pring.next()
                        ps_.append(p)
                        kb.act(p.a[:, :nkeys], s.a[:, :nkeys], AF.Exp, [s, mx, rs], [p, rs], bias=mx.a[:, 1:2],
                               accum=rs.a[:, 0:1])
                        kb.act(rs.a[:, 1:2], mx.a[:, 2:3], AF.Exp, [mx, rs], [rs])
                    for g4 in H:
                        rs = rss[g4]
                        kb.tt(rs.a[:, 2:3], rs.a[:, 0:1], rs.a[:, 1:2], ALU.add, [rs], [rs])
                        kb.op("dve", lambda e, rs=rs: e.reciprocal(rs.a[:, 3:4], rs.a[:, 2:3]), [rs], [rs])
                    for g4 in H:
                        p, rs = ps_[g4], rss[g4]
                        pb_ = pbr.next()
                        pbs.append(pb_)
                        kb.ts(pb_.a[:, :nkeys], p.a[:, :nkeys], rs.a[:, 3:4], None, ALU.mult, None, [p, rs], [pb_])
                    for g4 in H:
                        pb_ = pbs[g4]
                        bt = kb.nbank()
                        bts.append(bt)

                        def tr(e, bt=bt, pb_=pb_):
                            ins = None
                            for i in range(nkb):
                                ins = e.transpose(kb.bank(bt, 1024, BF16)[:, i * 128:(i + 1) * 128],
                                                  pb_.a[:, i * 128:(i + 1) * 128], self.identb.a)
                            return ins
                        kb.op("pe", tr, reads=[pb_, self.identb], writes=[kb.pb[bt]])
                    for g4 in H:
                        bt = bts[g4]
                        kb.copy(pT.a[:, 0:nkb, g4, :],
                                kb.bank(bt, 1024, BF16).rearrange("p (b q) -> p b q", q=128)[:, 0:nkb, :],
                                [kb.pb[bt]], [pT], eng="act" if g4 % 2 else "dve")
                    bo = kb.nbank()

                    def pv(e, bo=bo, pT=pT, vblocks=vblocks):
                        ins = None
                        for i, (vt, t) in enumerate(vblocks):
                            ins = e.matmul(kb.bank(bo), vt.a[:, t, k * 128:(k + 1) * 128],
                                           pT.a[:, i, :, :].rearrange("p g q -> p (g q)"),
                                           start=(i == 0), stop=(i == len(vblocks) - 1))
                        return ins
                    kb.op("pe", pv, reads=[pT, av] + ([vc] if sample else []), writes=[kb.pb[bo]])
                    kb.copy(hat.a[:, 4 * k:4 * k + 4, qtok], kb.bank(bo).rearrange("p (g q) -> p g q", g=4),
                            [kb.pb[bo]], [hat], eng="act")
        for c0 in (0, 8):
            kb.dma("sp", self.HAT[c0:c0 + 8].rearrange("c p t -> p c t"), hat.a[:, c0:c0 + 8, :], reads=[hat])
        kb.stage_end()

    def stage_merge(self, l, g):
        kb = self.kb
        I = self.ins
        hb = [kb.tile([128, 16, T], BF16) for _ in range(3)]
        for t, src in zip(hb, (self.HMT, self.HST, self.HAT)):
            self.load_fm(t, src, 16)
        Ws = (I["w_br_m"][l], I["w_br_s"][l], I["w_br_a"][l])
        slots = [kb.tile([128, 16, 256], BF16) for _ in range(4)]
        gtr = kb.ring(6, [128, T], BF16)
        acc = kb.ring(2, [128, T], F32)
        tmp = kb.ring(2, [128, T], F32)
        yo = kb.ring(3, [128, T], BF16)
        state = {}

        def mk_epi(b):
            def epi(ci, P, banks):
                gt = gtr.next()
                kb.dma("sp", gt.a, self.GTS[b * KC + ci], writes=[gt])
                if b == 0:
                    a = acc.next()
                    state[ci] = a
                    kb.tt(a.a, P, gt.a, ALU.mult, banks + [gt], [a])
                else:
                    a = state[ci]
                    t = tmp.next()
                    kb.tt(t.a, P, gt.a, ALU.mult, banks + [gt], [t])
                    if b == 1:
                        kb.tt(a.a, a.a, t.a, ALU.add, [a, t], [a])
                    else:
                        o = yo.next()
                        kb.tt(o.a, a.a, t.a, ALU.add, [a, t], [o])
                        kb.dma("sp", self.YT[ci], o.a, reads=[o])
            return epi
        for c0 in range(0, KC, 2):
            for b in range(3):
                def epi_b(ci, P, banks, b=b, c0=c0):
                    mk_epi(b)(c0 + ci, P, banks)
                kb.gemm_fm(Ws[b], 16, hb[b], T, [(c0 + i) * 128 for i in range(2)], epi_b, slots, nb=2)
        kb.stage_end()

    def stage_wout(self, l, g):
        kb = self.kb
        yt = kb.tile([128, KC, T], BF16)
        self.load_fm(yt, self.YT, KC)
        slots = [kb.tile([128, KC, 512], BF16) for _ in range(2)]
        xin = kb.ring(3, [128, T], F32)

        def epi(ci, P, banks):
            xi = xin.next()
            kb.dma("sp", xi.a, self.XT[ci], writes=[xi])
            kb.stt(xi.a, P, self.modv(l, 2, g)[:, ci:ci + 1], xi.a, ALU.mult, ALU.add, banks + [self.MOD, xi], [xi])
            kb.dma("sp", self.XT[ci], xi.a, reads=[xi])
        kb.gemm_fm(self.ins["w_out"][l], KC, yt, T, [i * 128 for i in range(KC)], epi, slots)
        kb.stage_end()

    def stage_ffn_up(self, l, g):
        kb = self.kb
        sample = (g == 1)
        ht = kb.tile([128, KC, T], BF16)
        self.load_fm(ht, self.HT, KC)
        slots = [kb.tile([128, KC, 256], BF16) for _ in range(3)]
        tg = kb.ring(2, [128, T], F32)
        tu = kb.ring(2, [128, T], F32)
        go = kb.ring(3, [128, T], BF16)
        cw = self.cw.a
        cbv = self.cb.a
        st = {}

        def sh(ap, a, b):
            if sample:
                return ap[:, a:T + b]
            v = ap.rearrange("p (s t) -> p s t", s=4)
            return v[:, :, a:256 + b]

        def conv(dst, P, banks, f):
            kb.act(dst.a, P, AF.Identity, banks + [self.cw, self.cb], [dst], scale=cw[:, l, 1, f:f + 1],
                   bias=cbv[:, l, f:f + 1])
            kb.stt(sh(dst.a, 1, 0), sh(P, 0, -1), cw[:, l, 0, f:f + 1], sh(dst.a, 1, 0), ALU.mult, ALU.add,
                   banks + [dst, self.cw], [dst])
            kb.stt(sh(dst.a, 0, -1), sh(P, 1, 0), cw[:, l, 2, f:f + 1], sh(dst.a, 0, -1), ALU.mult, ALU.add,
                   banks + [dst, self.cw], [dst])

        def epi(ci, P, banks):
            f = ci // 2
            if ci % 2 == 0:
                t = tg.next()
                conv(t, P, banks, f)
                kb.act(t.a, t.a, AF.Silu, [t], [t])
                st["g"] = t
            else:
                t = tu.next()
                conv(t, P, banks, FC + f)
                o = go.next()
                kb.tt(o.a, st["g"].a, t.a, ALU.mult, [st["g"], t], [o])
                kb.dma("sp", self.GU[f], o.a, reads=[o])
        chunks = []
        for f in range(FC):
            chunks += [f * 128, (FC + f) * 128]
        kb.gemm_fm(self.ins["ffn_up"][l], KC, ht, T, chunks, epi, slots, nb=2)
        kb.stage_end()

    def stage_ffn_down(self, l, g):
        kb = self.kb
        for half in range(2):
            gu = kb.tile([128, FC, 512], BF16)
            for c0 in range(0, FC, 8):
                n = min(8, FC - c0)
                kb.dma("sp", gu.a[:, c0:c0 + n, :],
                       self.GU[c0:c0 + n, :, half * 512:(half + 1) * 512].rearrange("c p t -> p c t"),
                       writes=[gu], merge=(c0 > 0))
            slots = [kb.tile([128, FC, 128], BF16) for _ in range(3)]
            xin = kb.ring(3, [128, 512], F32)

            def epi(ci, P, banks, half=half):
                xi = xin.next()
                kb.dma("sp", xi.a, self.XT[ci][:, half * 512:(half + 1) * 512], writes=[xi])
                kb.stt(xi.a, P, self.modv(l, 5, g)[:, ci:ci + 1], xi.a, ALU.mult, ALU.add,
                       banks + [self.MOD, xi], [xi])
                kb.dma("sp", self.XT[ci][:, half * 512:(half + 1) * 512], xi.a, reads=[xi])
            kb.gemm_fm(self.ins["ffn_down"][l], FC, gu, 512, [i * 128 for i in range(KC)], epi, slots, nb=1)
            kb.stage_end()

    def build(self):
        kb = self.kb
        self.stage_mod()
        for g in self.groups:
            self.stage_loadx(g)
            for l in self.layers:
                self.stage_norm(l, g, 1)
                if self.upto <= 1:
                    continue
                self.stage_inproj(l, g)
                if self.upto <= 2:
                    continue
                self.stage_mlstm(l, g)
                if self.upto <= 3:
                    continue
                self.stage_sgu(l, g)
                if self.upto <= 4:
                    continue
                self.stage_attn(l, g)
                if self.upto <= 5:
                    continue
                self.stage_merge(l, g)
                self.stage_wout(l, g)
                self.stage_norm(l, g, 2)
                self.stage_ffn_up(l, g)
                self.stage_ffn_down(l, g)
            if self.upto > 5:
                self.stage_norm(0, g, 3)
        kb.barrier()
        return self.nc


def host_consts():
    c = {}
    c["c_ident"] = np.eye(128, dtype=np.float32)
    s = np.arange(128)
    tri_f = (s[:, None] <= s[None, :]).astype(np.float32)
    tri_r = (s[:, None] >= s[None, :]).astype(np.float32)
    c["c_tri"] = np.stack([tri_f, tri_r])
    sel_f = np.zeros((128, 128), np.float32); sel_f[127, :] = 1.0
    sel_r = np.zeros((128, 128), np.float32); sel_r[0, :] = 1.0
    c["c_sel"] = np.stack([sel_f, sel_r])
    mf = np.where(s[None, :] <= s[:, None], 0.0, NEG).astype(np.float32)
    mr = np.where(s[None, :] >= s[:, None], 0.0, NEG).astype(np.float32)
    c["c_mmask"] = np.stack([mf, mr])
    a = np.arange(128)[:, None]; r = np.arange(384)[None, :]
    c["c_amask"] = np.where(np.abs(r - 128 - a) <= 128, 0.0, NEG).astype(np.float32)
    t = np.arange(T)
    row = (t // 64).astype(np.float64); col = (t % 64).astype(np.float64)
    inv = 10000.0 ** (-np.arange(32, dtype=np.float64) / 32.0)
    p = np.arange(128)
    pos = np.where((p < 64)[:, None], row[None, :], col[None, :])
    ang = (pos.astype(np.float32) * inv.astype(np.float32)[p % 32][:, None]).astype(np.float32)
    cos = np.cos(ang.astype(np.float64)); sin = np.sin(ang.astype(np.float64))
    sign = np.where((p % 64) < 32, -1.0, 1.0)[:, None]
    c["c_rope"] = np.stack([cos, sin * sign]).astype(np.float32)
    partner = np.where((p % 64) < 32, p + 32, p - 32)
    perm = np.zeros((128, 128), np.float32)
    perm[partner, p] = 1.0
    c["c_perm"] = perm
    return c


_NC_CACHE = {}


def make_in_maps(inp, ncores=8):
    f = lambda a: np.ascontiguousarray(np.asarray(a, dtype=np.float32))
    xp = f(inp["x_prompt"]); xs = f(inp["x_sample"])
    shared = {}
    for k in ("ada_w", "w_in", "w_br_m", "w_br_s", "w_br_a", "w_out", "ffn_up", "ffn_down"):
        shared[k] = f(inp[k])
    shared["adabF"] = np.ascontiguousarray(np.stack([fmv(inp["ada_b"][l]) for l in range(NL)], 1))
    shared["n1F"] = np.ascontiguousarray(np.stack([fmv(inp["norm1_g"][l]) for l in range(NL)], 1))
    shared["n2F"] = np.ascontiguousarray(np.stack([fmv(inp["norm2_g"][l]) for l in range(NL)], 1))
    shared["nfF"] = fmv(inp["final_g"])
    shared["mgb"] = f(inp["m_gate_b"]).reshape(NL, 32)
    shared["mng"] = f(inp["m_norm_g"]); shared["sgn"] = f(inp["sgu_norm_g"])
    shared["sgwT"] = np.ascontiguousarray(f(inp["sgu_w"]).transpose(0, 1, 3, 2))
    shared["sgb"] = f(inp["sgu_b"]); shared["sink"] = f(inp["attn_sink"])
    cw = f(inp["ffn_conv_w"])
    shared["cwF"] = np.ascontiguousarray(
        np.stack([np.stack([fmv(cw[l, k]) for k in range(3)], 1) for l in range(NL)], 1))
    shared["cbF"] = np.ascontiguousarray(np.stack([fmv(inp["ffn_conv_b"][l]) for l in range(NL)], 1))
    shared.update(host_consts())
    in_maps = []
    for c in range(ncores):
        j = c // 2
        m = dict(shared)
        m["xg"] = np.ascontiguousarray(np.stack([xp[4 * c:4 * c + 4].reshape(T, D), xs[j]]))
        cv = np.stack([f(inp["c_ctx"]), f(inp["c"])[j]])
        m["cvecF"] = np.ascontiguousarray(cv.reshape(2, KC, 128).transpose(2, 1, 0))
        m["ck"] = np.ascontiguousarray(f(inp["cache_k"])[j].reshape(NL, 512, 512))
        m["cv"] = np.ascontiguousarray(f(inp["cache_v"])[j].reshape(NL, 512, 512))
        m["sC"] = np.ascontiguousarray(f(inp["state_C"])[j])
        m["sn"] = np.ascontiguousarray(f(inp["state_n"])[j])
        m["sm"] = np.ascontiguousarray(f(inp["state_m"])[j])
        in_maps.append(m)
    return in_maps


def kernel(**inp):
    ncores = 8
    in_maps = make_in_maps(inp, ncores)
    if "nc" not in _NC_CACHE:
        _NC_CACHE["nc"] = Prog().build()
    nc = _NC_CACHE["nc"]
    res = run_bass_kernel_spmd(nc, in_maps, core_ids=list(range(ncores)))
    R = res.results
    y_prompt = np.concatenate([R[c]["y"][0].reshape(4, 256, D) for c in range(ncores)], 0)
    y_sample = np.stack([np.concatenate([R[2 * j]["y"][1][:512], R[2 * j + 1]["y"][1][512:]], 0) for j in range(4)], 0)
    nk = np.concatenate([R[c]["nk"].reshape(NL, 4, 256, 4, 128).transpose(1, 0, 2, 3, 4) for c in range(ncores)], 0)
    nv = np.concatenate([R[c]["nv"].reshape(NL, 4, 256, 4, 128).transpose(1, 0, 2, 3, 4) for c in range(ncores)], 0)
    nC = np.concatenate([R[c]["nC"] for c in range(ncores)], 0)
    nn = np.concatenate([R[c]["nn"] for c in range(ncores)], 0)
    nm = np.concatenate([R[c]["nm"] for c in range(ncores)], 0)
    return (y_prompt.astype(np.float32), y_sample.astype(np.float32), np.ascontiguousarray(nk),
            np.ascontiguousarray(nv), nC, nn, nm)
```

```python
import numpy as np
import concourse.bass as bass
import concourse.mybir as mybir
from concourse.bass_utils import run_bass_kernel_spmd

F32 = mybir.dt.float32
BF16 = mybir.dt.bfloat16
AF = mybir.ActivationFunctionType
ALU = mybir.AluOpType
AX = mybir.AxisListType
RING = 8
NEG = -1.0e30
EPS = 1e-6

D = 4096
T = 1024
KC = 32
NL = 2
NIN = 25632
FF = 11008
FC = 86
SBUF_TOP = 206 * 1024
C_MQ, C_MK, C_MV, C_MO, C_MG, C_SU, C_SV, C_AQ, C_AK, C_AV, C_GT = (
    0, 1024, 2048, 4096, 6144, 6176, 8224, 10272, 12320, 12832, 13344)

DEBUG_OUT = []


class Dep:
    __slots__ = ("w", "r")

    def __init__(self):
        self.w = {}
        self.r = {}


class Tl(Dep):
    __slots__ = ("a",)

    def __init__(self, a):
        Dep.__init__(self)
        self.a = a


class Ring:
    def __init__(self, tiles):
        self.t = tiles
        self.i = 0

    def next(self):
        t = self.t[self.i % len(self.t)]
        self.i += 1
        return t


class KB:
    def __init__(self):
        self.nc = bass.Bass("TRN2", target_bir_lowering=False)
        nc = self.nc
        self.eng = {"pe": nc.tensor, "act": nc.scalar, "dve": nc.vector, "pool": nc.gpsimd, "sp": nc.sync}
        self.sems = {}
        self.ccnt = {}
        for e in ("pe", "act", "dve", "pool"):
            self.sems[e] = nc.alloc_semaphore("c_" + e)
            self.ccnt[e] = 0
        self.dcnt = {}
        for q in ("sp", "pool", "act"):
            self.dcnt[q] = 0
            for i in range(RING):
                self.sems[(q, i)] = nc.alloc_semaphore("d_%s%d" % (q, i))
        self.known = {e: {} for e in self.eng}
        self.psum = nc.alloc_psum_tensor("psum_all", [128, 4096], F32).ap()
        self.pb = [Dep() for _ in range(8)]
        self.excl = set(id(d) for d in self.pb)
        self.n_ins = 0
        self.poff = (nc.sbuf_base + 63) // 64 * 64
        self.top = nc.sbuf_top
        self.abase = None
        self.aoff = None
        self.tn = 0
        self.bctr = 0
        self.pctr = 0
        self.wctr = {}
        self.bg = None
        self.bg_rate = 0
        self.in_bg = False

    def _alloc(self, shape, dt, off):
        self.tn += 1
        h = self.nc.alloc_sbuf_tensor_at("t%d" % self.tn, list(shape), dt, offset=off)
        return Tl(h.ap())

    @staticmethod
    def _nbytes(shape, dt):
        n = 1
        for s in shape[1:]:
            n *= s
        n *= 4 if dt == F32 else 2
        return (n + 63) // 64 * 64

    def ptile(self, shape, dt=F32):
        t = self._alloc(shape, dt, self.poff)
        self.poff += self._nbytes(shape, dt)
        return t

    def start_arena(self):
        self.abase = self.poff
        self.aoff = self.abase

    def tile(self, shape, dt=F32):
        nb = self._nbytes(shape, dt)
        assert self.aoff + nb <= self.top, ("sbuf overflow", self.aoff, nb)
        t = self._alloc(shape, dt, self.aoff)
        self.aoff += nb
        return t

    def ring(self, n, shape, dt=F32):
        return Ring([self.tile(shape, dt) for _ in range(n)])

    def stage_end(self):
        self.barrier()
        self.aoff = self.abase

    def bank(self, i, n=512, dt=F32):
        a = self.psum[:, i * 512:(i + 1) * 512]
        if dt == BF16:
            a = a.bitcast(BF16)
        return a[:, :n]

    def nbank(self):
        b = self.bctr % 8
        self.bctr += 1
        return b

    def npair(self):
        p = self.pctr % 4
        self.pctr += 1
        return p

    def op(self, e, fn, reads=(), writes=(), dma=False, merge=False):
        need = {}
        if not merge:
            for b in reads:
                for k, v in b.w.items():
                    if need.get(k, 0) < v:
                        need[k] = v
                if id(b) in self.excl:
                    for k, v in b.r.items():
                        if k != e and need.get(k, 0) < v:
                            need[k] = v
            for b in writes:
                for k, v in b.w.items():
                    if need.get(k, 0) < v:
                        need[k] = v
                for k, v in b.r.items():
                    if need.get(k, 0) < v:
                        need[k] = v
        if dma:
            i = self.dcnt[e] % RING
            key = (e, i)
            v = 16 * (self.dcnt[e] // RING + 1)
            if v > 16 and need.get(key, 0) < v - 16:
                need[key] = v - 16
            self.dcnt[e] += 1
        else:
            key = e
            self.ccnt[e] += 1
            v = self.ccnt[e]
        engine = self.eng[e]
        kn = self.known[e]
        for k2, v2 in need.items():
            if k2 == "pe" and e == "pe" and not dma:
                continue
            if kn.get(k2, 0) < v2:
                engine.wait_ge(self.sems[k2], v2)
                kn[k2] = v2
        ins = fn(engine)
        ins.then_inc(self.sems[key], 16 if dma else 1)
        self.n_ins += 1
        for b in reads:
            b.r[key] = v
        for b in writes:
            if merge:
                b.w[key] = v
            else:
                b.w = {key: v}
                b.r = {}
        return (key, v)

    def barrier(self):
        evs = []
        for e, c in self.ccnt.items():
            if c:
                evs.append((e, c))
        for q, c in self.dcnt.items():
            for i in range(RING):
                n = (c - i + RING - 1) // RING if c > i else 0
                if n:
                    evs.append(((q, i), 16 * n))
        for e in self.eng:
            kn = self.known[e]
            for k, v in evs:
                if kn.get(k, 0) < v:
                    self.eng[e].wait_ge(self.sems[k], v)
                    kn[k] = v

    def dma(self, q, out, in_, reads=(), writes=(), merge=False, slow=False):
        if slow:
            return self.op(q, lambda e: e.dma_start(out=out, in_=in_, allow_slow_non_contiguous=True),
                           reads, writes, dma=True, merge=merge)
        return self.op(q, lambda e: e.dma_start(out=out, in_=in_), reads, writes, dma=True, merge=merge)

    def act(self, out, in_, func, reads, writes, bias=None, scale=None, accum=None):
        kw = {}
        if bias is not None:
            kw["bias"] = bias
        if scale is not None:
            kw["scale"] = scale
        if accum is not None:
            kw["accum_out"] = accum
        return self.op("act", lambda e: e.activation(out, in_, func, **kw), reads, writes)

    def ts(self, out, in0, s1, s2, op0, op1, reads, writes, eng="dve"):
        if s2 is None:
            return self.op(eng, lambda e: e.tensor_scalar(out, in0, s1, None, op0=op0), reads, writes)
        return self.op(eng, lambda e: e.tensor_scalar(out, in0, s1, s2, op0=op0, op1=op1), reads, writes)

    def tt(self, out, in0, in1, op, reads, writes, eng="dve"):
        return self.op(eng, lambda e: e.tensor_tensor(out, in0, in1, op=op), reads, writes)

    def stt(self, out, in0, s, in1, op0, op1, reads, writes, accum=None, eng="dve"):
        if accum is not None:
            return self.op(eng, lambda e: e.scalar_tensor_tensor(out, in0, s, in1, op0=op0, op1=op1,
                                                                 accum_out=accum), reads, writes)
        return self.op(eng, lambda e: e.scalar_tensor_tensor(out, in0, s, in1, op0=op0, op1=op1), reads, writes)

    def copy(self, out, in_, reads, writes, eng="dve"):
        if eng == "act":
            return self.act(out, in_, AF.Identity, reads, writes)
        return self.op(eng, lambda e: e.tensor_copy(out, in_), reads, writes)

    def wload(self, slot, W2d, KCn, c0, ncols, dst0, first):
        for kk in range(0, KCn, 8):
            n = min(8, KCn - kk)
            src = W2d[kk * 128:(kk + n) * 128, c0:c0 + ncols].rearrange("(k p) n -> p k n", p=128)
            self.dma("pool", slot.a[:, kk:kk + n, dst0:dst0 + ncols], src, writes=[slot],
                     merge=not (first and kk == 0))

    def next_slot(self, slots):
        k = id(slots[0])
        i = self.wctr.get(k, 0)
        self.wctr[k] = i + 1
        return slots[i % len(slots)]

    def pump(self):
        if self.bg is None or self.in_bg:
            return
        self.in_bg = True
        for _ in range(self.bg_rate):
            if next(self.bg, "done") == "done":
                self.bg = None
                break
        self.in_bg = False

    def drain_bg(self):
        if self.bg is not None:
            self.in_bg = True
            for _ in self.bg:
                pass
            self.in_bg = False
            self.bg = None

    def gemm_fm(self, W2d, KCn, xt, Tn, chunks, epi, slots, nb=4):
        for _ in self.gemm_fm_gen(W2d, KCn, xt, Tn, chunks, epi, slots, nb):
            self.pump()

    def gemm_fm_gen(self, W2d, KCn, xt, Tn, chunks, epi, slots, nb=4):
        nth = max(1, Tn // 512)
        tw = min(Tn, 512)
        for b0 in range(0, len(chunks), nb):
            blk = chunks[b0:b0 + nb]
            slot = self.next_slot(slots)
            runs = []
            for i, c in enumerate(blk):
                if runs and runs[-1][0] + runs[-1][1] * 128 == c:
                    runs[-1][1] += 1
                else:
                    runs.append([c, 1, i])
            first = True
            for (c0, n, di) in runs:
                self.wload(slot, W2d, KCn, c0, n * 128, di * 128, first)
                first = False
            for i, c in enumerate(blk):
                pp = self.npair()
                banks = [self.pb[2 * pp + t] for t in range(nth)]

                def mm(e, i=i, pp=pp, slot=slot):
                    ins = None
                    for kc in range(KCn):
                        for t in range(nth):
                            ins = e.matmul(self.bank(2 * pp + t, tw), slot.a[:, kc, i * 128:(i + 1) * 128],
                                           xt.a[:, kc, t * 512:t * 512 + tw],
                                           start=(kc == 0), stop=(kc == KCn - 1))
                    return ins
                self.op("pe", mm, reads=[slot, xt], writes=banks)
                epi(b0 + i, self.psum[:, pp * 1024:pp * 1024 + Tn], banks)
            yield

    def gemm_tm(self, W2d, KCn, xt, ntiles, c0, ncols, epi, slots, bw=512):
        for cb in range(0, ncols, bw):
            n = min(bw, ncols - cb)
            slot = self.next_slot(slots)
            self.wload(slot, W2d, KCn, c0 + cb, n, 0, True)
            for tt in range(ntiles):
                bk = self.nbank()

                def mm(e, tt=tt, bk=bk, slot=slot, n=n):
                    ins = None
                    for kc in range(KCn):
                        ins = e.matmul(self.bank(bk, n), xt.a[:, kc, tt * 128:(tt + 1) * 128],
                                       slot.a[:, kc, 0:n], start=(kc == 0), stop=(kc == KCn - 1))
                    return ins
                self.op("pe", mm, reads=[slot, xt], writes=[self.pb[bk]])
                epi(cb, n, tt, self.bank(bk, n), self.pb[bk])
            self.pump()


def fmv(v):
    v = np.asarray(v, np.float32)
    return np.ascontiguousarray(v.reshape(-1, 128).T)


class Prog:
    def __init__(self, layers=(0, 1), groups=(0, 1), upto=99, dummy=(), nlw=NL, only=None):
        self.dummy = set(dummy)
        self.only = only
        self.kb = KB()
        kb = self.kb
        nc = kb.nc
        self.nc = nc
        self.layers = layers
        self.groups = groups
        self.upto = upto
        self.ins = {}
        self.outs = {}
        I = self.inp
        I("xg", [2, T, D])
        I("cvecF", [128, KC, 2])
        I("ck", [NL, 512, 512]); I("cv", [NL, 512, 512])
        I("sC", [NL, 2, 8, 128, 256]); I("sn", [NL, 2, 8, 128]); I("sm", [NL, 2, 8])
        I("ada_w", [nlw, D, 6 * D]); I("adabF", [128, NL, 192])
        I("n1F", [128, NL, KC]); I("n2F", [128, NL, KC]); I("nfF", [128, KC])
        I("w_in", [nlw, D, NIN])
        I("mgb", [NL, 32]); I("mng", [NL, 2048]); I("sgn", [NL, 2048])
        I("sgwT", [NL, 8, 128, 128]); I("sgb", [NL, 8, 128]); I("sink", [NL, 16])
        I("w_br_m", [nlw, 2048, D]); I("w_br_s", [nlw, 2048, D]); I("w_br_a", [nlw, 2048, D])
        I("w_out", [nlw, D, D]); I("ffn_up", [nlw, D, 2 * FF]); I("ffn_down", [nlw, FF, D])
        I("cwF", [128, NL, 3, 2 * FC]); I("cbF", [128, NL, 2 * FC])
        I("c_ident", [128, 128]); I("c_tri", [2, 128, 128]); I("c_sel", [2, 128, 128])
        I("c_mmask", [2, 128, 128]); I("c_amask", [128, 384]); I("c_rope", [2, 128, T]); I("c_perm", [128, 128])
        O = self.outp
        O("y", [2, T, D])
        O("nk", [NL, T, 512]); O("nv", [NL, T, 512])
        O("nC", [4, NL, 2, 8, 128, 256]); O("nn", [4, NL, 2, 8, 128]); O("nm", [4, NL, 2, 8])
        S = self.scr
        self.XT = S("XT", [KC, 128, T], F32)
        self.HT = S("HT", [KC, 128, T], BF16)
        self.QT = S("QT", [8, 128, T], BF16); self.KT = S("KT", [8, 128, T], BF16)
        self.UT = S("UT", [16, 128, T], BF16)
        self.AQT = S("AQT", [16, 128, T], BF16); self.AKT = S("AKT", [4, 128, T], BF16)
        self.GTS = S("GTS", [96, 128, T], BF16)
        self.MV = S("MV", [T, 2048], BF16); self.MO = S("MO", [T, 2048], BF16)
        self.MG = S("MG", [T, 32], F32); self.SV = S("SV", [T, 2048], F32)
        self.AV = S("AV", [T, 512], BF16)
        self.HMT = S("HMT", [16, 128, T], BF16); self.HST = S("HST", [16, 128, T], BF16)
        self.HAT = S("HAT", [16, 128, T], BF16)
        self.YT = S("YT", [KC, 128, T], BF16)
        self.GU = S("GU", [FC, 128, T], BF16)
        self.consts()

    def inp(self, name, shape):
        if name in self.dummy:
            shape = [1, 128, 128]
        self.ins[name] = self.nc.dram_tensor(name, list(shape), F32, kind="ExternalInput").ap()

    def outp(self, name, shape):
        self.outs[name] = self.nc.dram_tensor(name, list(shape), F32, kind="ExternalOutput").ap()

    def scr(self, name, shape, dt):
        if name in DEBUG_OUT:
            a = self.nc.dram_tensor("dbg_" + name, list(shape), dt, kind="ExternalOutput").ap()
            self.outs["dbg_" + name] = a
            return a
        return self.nc.dram_tensor(name, list(shape), dt).ap()

    def consts(self):
        kb = self.kb
        I = self.ins

        def ld(name, shape, src, dt=F32, q=None):
            t = kb.ptile(shape, dt)
            kb.dma(q or ("sp" if dt == F32 else "pool"), t.a, src, writes=[t])
            return t
        self.identf = ld("identf", [128, 128], I["c_ident"])
        self.identb = ld("identb", [128, 128], I["c_ident"], BF16)
        self.tri = [ld("tri%d" % d, [128, 128], I["c_tri"][d]) for d in range(2)]
        self.sel = [ld("sel%d" % d, [128, 128], I["c_sel"][d]) for d in range(2)]
        self.mmask = [ld("mm%d" % d, [128, 128], I["c_mmask"][d]) for d in range(2)]
        self.amask = ld("amask", [128, 384], I["c_amask"])
        self.cosT = ld("cos", [128, T], I["c_rope"][0])
        self.sinT = ld("sin", [128, T], I["c_rope"][1])
        self.perm = ld("perm", [128, 128], I["c_perm"])
        self.onesf = kb.ptile([128, 128], F32)
        kb.op("dve", lambda e: e.memset(self.onesf.a, 1.0), writes=[self.onesf])
        self.adab = ld("adab", [128, NL, 192], I["adabF"])
        self.n1 = ld("n1", [128, NL, KC], I["n1F"])
        self.n2 = ld("n2", [128, NL, KC], I["n2F"])
        self.nf = ld("nf", [128, KC], I["nfF"])
        self.cw = ld("cw", [128, NL, 3, 2 * FC], I["cwF"])
        self.cb = ld("cb", [128, NL, 2 * FC], I["cbF"])
        self.mgb = ld("mgb", [128, NL, 32], I["mgb"].partition_broadcast(128))
        self.sinkb = ld("sinkb", [128, NL, 16], I["sink"].partition_broadcast(128))
        self.MOD = kb.ptile([128, NL, 192, 2], F32)
        self.A = kb.ptile([128, KC], F32)
        self.zero = kb.ptile([128, 1], F32)
        self.xc = kb.ptile([128, KC, 2], BF16)
        self.mod_todo = []
        kb.op("dve", lambda e: e.memset(self.zero.a, 0.0), writes=[self.zero])
        kb.start_arena()

    def stage_mod(self):
        kb = self.kb
        I = self.ins
        if "ada_w" in self.dummy:
            kb.op("dve", lambda e: e.memset(self.MOD.a, 0.0), writes=[self.MOD])
            kb.stage_end()
            return
        cf = kb.tile([128, KC, 2], F32)
        kb.dma("sp", cf.a, I["cvecF"], writes=[cf])
        kb.act(self.xc.a, cf.a, AF.Silu, [cf], [self.xc])
        slots = [kb.tile([128, KC, 512], BF16) for _ in range(2)]
        l0 = self.layers[0]
        kb.gemm_fm(I["ada_w"][l0], KC, self.xc, 2, list(range(0, 2 * D, 128)), self.mod_epi(l0, 0), slots)
        self.mod_todo = [(l0, 64, 128)] + [(l, 0, 192) for l in self.layers[1:]]
        kb.stage_end()

    def mod_epi(self, l, c_off):
        kb = self.kb

        def epi(ci, P, banks):
            c = c_off + ci
            kb.ts(self.MOD.a[:, l, c, :], P, self.adab.a[:, l, c:c + 1], None, ALU.add, None,
                  banks + [self.adab], [self.MOD])
        return epi

    def start_bg_mod(self, rate):
        kb = self.kb
        if not self.mod_todo or "ada_w" in self.dummy:
            return
        l, c0, n = self.mod_todo.pop(0)
        slots = [kb.tile([128, KC, 128], BF16) for _ in range(2)]
        kb.bg = kb.gemm_fm_gen(self.ins["ada_w"][l], KC, self.xc, 2, [(c0 + i) * 128 for i in range(n)],
                               self.mod_epi(l, c0), slots, nb=1)
        kb.bg_rate = rate

    def modv(self, l, j, g):
        return self.MOD.a[:, l, j * KC:(j + 1) * KC, g]

    def stage_loadx(self, g):
        kb = self.kb
        xin = kb.ring(2, [128, D], F32)
        stg = kb.ring(4, [128, 4, 128], F32)
        for tt in range(8):
            xi = xin.next()
            kb.dma("sp", xi.a, self.ins["xg"][g, tt * 128:(tt + 1) * 128, :], writes=[xi])
            for k4 in range(8):
                bk = kb.nbank()

                def tr(e, xi=xi, k4=k4, bk=bk):
                    ins = None
                    for i in range(4):
                        c = k4 * 4 + i
                        ins = e.transpose(kb.bank(bk)[:, i * 128:(i + 1) * 128], xi.a[:, c * 128:(c + 1) * 128],
                                          self.identf.a)
                    return ins
                kb.op("pe", tr, reads=[xi, self.identf], writes=[kb.pb[bk]])
                s = stg.next()
                kb.copy(s.a, kb.bank(bk).rearrange("p (c t) -> p c t", c=4), [kb.pb[bk]], [s],
                        eng="act" if k4 % 2 else "dve")
                kb.dma("sp", self.XT[k4 * 4:(k4 + 1) * 4, :, tt * 128:(tt + 1) * 128].rearrange("c p t -> p c t"),
                       s.a, reads=[s])
        kb.stage_end()

    def stage_norm(self, l, g, which):
        kb = self.kb
        if which == 3:
            Aap = self.nf.a
            Adep = self.nf
        else:
            gn = (self.n1 if which == 1 else self.n2)
            j = 0 if which == 1 else 3
            kb.stt(self.A.a, self.modv(l, j + 1, g), 1.0, gn.a[:, l, :], ALU.add, ALU.mult,
                   [self.MOD, gn], [self.A])
            Aap = self.A.a
            Adep = self.A
        xin = kb.ring(3, [128, T], F32)
        sq = kb.ring(2, [128, T], F32)
        for kc in range(KC):
            xi = xin.next()
            kb.dma("sp", xi.a, self.XT[kc], writes=[xi])
            s = sq.next()
            kb.act(s.a, xi.a, AF.Square, [xi], [s])

            def mm(e, s=s, kc=kc):
                e.matmul(kb.bank(0), self.onesf.a, s.a[:, 0:512], start=(kc == 0), stop=(kc == KC - 1))
                return e.matmul(kb.bank(1), self.onesf.a, s.a[:, 512:1024], start=(kc == 0), stop=(kc == KC - 1))
            kb.op("pe", mm, reads=[s, self.onesf], writes=[kb.pb[0], kb.pb[1]])
        rstd = kb.tile([128, T], F32)
        kb.act(rstd.a, kb.psum[:, 0:T], AF.Ln, [kb.pb[0], kb.pb[1]], [rstd], scale=1.0 / D, bias=EPS)
        kb.act(rstd.a, rstd.a, AF.Exp, [rstd], [rstd], scale=-0.5)
        tmp = kb.ring(2, [128, T], F32)
        if which == 3:
            yf = kb.tile([128, KC, T], F32)
        else:
            hst = kb.ring(3, [128, T], BF16)
        for kc in range(KC):
            xi = xin.next()
            kb.dma("sp", xi.a, self.XT[kc], writes=[xi])
            tm = tmp.next()
            kb.tt(tm.a, xi.a, rstd.a, ALU.mult, [xi, rstd], [tm])
            if which == 3:
                kb.act(yf.a[:, kc, :], tm.a, AF.Identity, [tm, Adep], [yf], scale=Aap[:, kc:kc + 1])
            else:
                h = hst.next()
                kb.act(h.a, tm.a, AF.Identity, [tm, Adep, self.MOD], [h], scale=Aap[:, kc:kc + 1],
                       bias=self.modv(l, j, g)[:, kc:kc + 1])
                kb.dma("sp", self.HT[kc], h.a, reads=[h])
        if which == 3:
            yo = kb.ring(4, [128, 512], F32)
            for tt in range(8):
                for k4 in range(8):
                    bk = kb.nbank()

                    def tr(e, k4=k4, bk=bk, tt=tt):
                        ins = None
                        for i in range(4):
                            ins = e.transpose(kb.bank(bk)[:, i * 128:(i + 1) * 128],
                                              yf.a[:, k4 * 4 + i, tt * 128:(tt + 1) * 128], self.identf.a)
                        return ins
                    kb.op("pe", tr, reads=[yf, self.identf], writes=[kb.pb[bk]])
                    y = yo.next()
                    kb.copy(y.a, kb.bank(bk), [kb.pb[bk]], [y], eng="act" if k4 % 2 else "dve")
                    kb.dma("sp", self.outs["y"][g, tt * 128:(tt + 1) * 128, k4 * 512:(k4 + 1) * 512], y.a, reads=[y])
        kb.stage_end()

    def load_fm(self, dst, src, nch, q="sp"):
        for c0 in range(0, nch, 8):
            n = min(8, nch - c0)
            self.kb.dma(q, dst.a[:, c0:c0 + n, :], src[c0:c0 + n].rearrange("c p t -> p c t"),
                        writes=[dst], merge=(c0 > 0))

    def stage_inproj(self, l, g):
        kb = self.kb
        W = self.ins["w_in"][l]
        ht = kb.tile([128, KC, T], BF16)
        self.load_fm(ht, self.HT, KC)
        slots = [kb.tile([128, KC, 512], BF16) for _ in range(2)]
        self.start_bg_mod(3)
        ob = kb.ring(4, [128, T], BF16)
        of = kb.ring(3, [128, T], F32)
        tb = kb.ring(4, [128, 512], BF16)
        tf = kb.ring(4, [128, 512], F32)
        sample = (g == 1)

        def fm_plain(dst, func, scale, eng):
            def epi(ci, P, banks):
                o = ob.next()
                if eng == "act":
                    kb.act(o.a, P, func, banks, [o], scale=scale)
                else:
                    kb.ts(o.a, P, scale, 0.0, ALU.mult, ALU.add, banks, [o])
                kb.dma("sp", dst[ci], o.a, reads=[o])
            return epi

        def fm_rope(dst, scale):
            def epi(ci, P, banks):
                xf = of.next()
                kb.act(xf.a, P, AF.Identity, banks, [xf], scale=scale)
                pp = kb.npair()
                b2 = [kb.pb[2 * pp], kb.pb[2 * pp + 1]]

                def mm(e, pp=pp, xf=xf):
                    e.matmul(kb.bank(2 * pp), self.perm.a, xf.a[:, 0:512], start=True, stop=True)
                    return e.matmul(kb.bank(2 * pp + 1), self.perm.a, xf.a[:, 512:1024], start=True, stop=True)
                kb.op("pe", mm, reads=[xf, self.perm], writes=b2)
                t2 = of.next()
                kb.tt(t2.a, kb.psum[:, pp * 1024:(pp + 1) * 1024], self.sinT.a, ALU.mult, b2 + [self.sinT], [t2])
                kb.tt(xf.a, xf.a, self.cosT.a, ALU.mult, [xf, self.cosT], [xf])
                o = ob.next()
                kb.tt(o.a, xf.a, t2.a, ALU.add, [xf, t2], [o])
                kb.dma("sp", dst[ci], o.a, reads=[o])
            return epi

        def ch(c0, n):
            return [c0 + 128 * i for i in range(n)]
        on = lambda k: self.only is None or k in self.only
        if on("mq"):
            kb.gemm_fm(W, KC, ht, T, ch(C_MQ, 8), fm_plain(self.QT, None, 128 ** -0.5, "dve"), slots)
        if on("mk"):
            kb.gemm_fm(W, KC, ht, T, ch(C_MK, 8), fm_plain(self.KT, None, 1.0, "dve"), slots)
        if on("su"):
            kb.gemm_fm(W, KC, ht, T, ch(C_SU, 16), fm_plain(self.UT, AF.Gelu_apprx_tanh, 1.0, "act"), slots)
        if not on("aq"):
            pass
        elif sample:
            kb.gemm_fm(W, KC, ht, T, ch(C_AQ, 16), fm_rope(self.AQT, 128 ** -0.5), slots)
            kb.gemm_fm(W, KC, ht, T, ch(C_AK, 4), fm_rope(self.AKT, 1.0), slots)
        else:
            kb.gemm_fm(W, KC, ht, T, ch(C_AQ, 16), fm_plain(self.AQT, None, 128 ** -0.5, "dve"), slots)
            kb.gemm_fm(W, KC, ht, T, ch(C_AK, 4), fm_plain(self.AKT, None, 1.0, "dve"), slots)
        if on("gt"):
            kb.gemm_fm(W, KC, ht, T, ch(C_GT, 96), fm_plain(self.GTS, AF.Sigmoid, 1.0, "act"), slots)

        def tm_bf16(dst, func):
            def epi(cb, n, tt, P, bd):
                o = tb.next()
                if func is None:
                    kb.copy(o.a[:, :n], P, [bd], [o])
                else:
                    kb.act(o.a[:, :n], P, func, [bd], [o])
                kb.dma("sp", dst[tt * 128:(tt + 1) * 128, cb:cb + n], o.a[:, :n], reads=[o])
            return epi

        def tm_f32(dst, func, also_bf16=None):
            def epi(cb, n, tt, P, bd):
                o = tf.next()
                if func is None:
                    kb.copy(o.a[:, :n], P, [bd], [o], eng="act")
                else:
                    kb.act(o.a[:, :n], P, func, [bd], [o])
                if dst is not None:
                    kb.dma("sp", dst[tt * 128:(tt + 1) * 128, cb:cb + n], o.a[:, :n], reads=[o])
                if also_bf16 is not None:
                    o2 = tb.next()
                    kb.copy(o2.a[:, :n], o.a[:, :n], [o], [o2])
                    kb.dma("sp", also_bf16[tt * 128:(tt + 1) * 128, cb:cb + n], o2.a[:, :n], reads=[o2])
            return epi

        def tm_mg(cb, n, tt, P, bd):
            o = tf.next()
            kb.tt(o.a[:, :32], P, self.mgb.a[:, l, :], ALU.add, [bd, self.mgb], [o])
            kb.dma("sp", self.MG[tt * 128:(tt + 1) * 128, :], o.a[:, :32], reads=[o])
        if on("mv"):
            kb.gemm_tm(W, KC, ht, 8, C_MV, 2048, tm_bf16(self.MV, None), slots)
        if on("mo"):
            kb.gemm_tm(W, KC, ht, 8, C_MO, 2048, tm_bf16(self.MO, AF.Sigmoid), slots)
        if on("mg"):
            kb.gemm_tm(W, KC, ht, 8, C_MG, 32, tm_mg, slots)
        if on("sv"):
            kb.gemm_tm(W, KC, ht, 8, C_SV, 2048, tm_f32(self.SV, AF.Gelu_apprx_tanh), slots)
        if not on("av"):
            pass
        elif sample:
            kb.gemm_tm(W, KC, ht, 8, C_AV, 512, tm_bf16(self.AV, None), slots)
        else:
            kb.gemm_tm(W, KC, ht, 8, C_AV, 512, tm_f32(self.outs["nv"][l], None, also_bf16=self.AV), slots)
            kb.gemm_tm(W, KC, ht, 8, C_AK, 512, tm_f32(self.outs["nk"][l], None), slots)
        kb.drain_bg()
        kb.stage_end()

    def stage_mlstm(self, l, g):
        kb = self.kb
        sample = (g == 1)
        nseq = 1 if sample else 4
        ncs = 8 if sample else 2
        B3 = [128, 4, 128]
        H = range(4)
        for hh in range(2):
            hms = [kb.tile([128, 8, 4, 256], F32) for _ in range(2)]
            mark = kb.aoff
            qts = kb.tile([128, 4, T], BF16)
            kts = kb.tile([128, 4, T], BF16)
            self.load_fm(qts, self.QT[hh * 4:(hh + 1) * 4], 4)
            self.load_fm(kts, self.KT[hh * 4:(hh + 1) * 4], 4)
            km = kb.tile([128, 8, 4, 128], BF16)
            for tt in range(8):
                bk = kb.nbank()

                def tr(e, tt=tt, bk=bk):
                    ins = None
                    for j in range(4):
                        ins = e.transpose(kb.bank(bk, 512, BF16)[:, j * 128:(j + 1) * 128],
                                          kts.a[:, j, tt * 128:(tt + 1) * 128], self.identb.a)
                    return ins
                kb.op("pe", tr, reads=[kts, self.identb], writes=[kb.pb[bk]])
                kb.copy(km.a[:, tt, :, :], kb.bank(bk, 512, BF16).rearrange("p (j d) -> p j d", j=4),
                        [kb.pb[bk]], [km], eng="act" if tt % 2 else "dve")
            va = kb.tile([128, 8, 4, 260], BF16)
            kb.op("dve", lambda e: e.memset(va.a[:, :, :, 256:260], 1.0), writes=[va])
            for tt in range(8):
                kb.dma("sp", va.a[:, tt, :, 0:256],
                       self.MV[tt * 128:(tt + 1) * 128, hh * 1024:(hh + 1) * 1024].rearrange("p (j d) -> p j d", j=4),
                       writes=[va], merge=True)
            mgs = kb.tile([128, 8, 32], F32)
            kb.dma("sp", mgs.a, self.MG.rearrange("(t p) c -> p t c", p=128), writes=[mgs])
            cms = [kb.tile([128, 4, 260], F32) for _ in range(2)]
            cbs = [kb.tile([128, 4, 260], BF16) for _ in range(2)]
            mbcs = [kb.tile([128, 4], F32) for _ in range(2)]
            sm = kb.ring(96, [128, 8], F32)
            dus = kb.ring(3, B3, F32)
            dms = kb.ring(3, B3, F32)
            es = kb.ring(3, B3, F32)
            abf = kb.ring(8, [128, 128], BF16)
            ats = kb.ring(8, [128, 128], BF16)
            kws = kb.ring(8, [128, 128], BF16)
            t1s = kb.ring(8, [128, 260], F32)
            nds = kb.ring(8, [128, 260], F32)

            def step(d, tt, do_update):
                cm, cb, mbc, hm = cms[d], cbs[d], mbcs[d], hms[d]
                tok = slice(tt * 128, (tt + 1) * 128)
                gi = mgs.a[:, tt, (2 * d) * 8 + hh * 4:(2 * d) * 8 + hh * 4 + 4]
                gf = mgs.a[:, tt, (2 * d + 1) * 8 + hh * 4:(2 * d + 1) * 8 + hh * 4 + 4]
                e1 = sm.next()
                kb.act(e1.a[:, :4], gf, AF.Exp, [mgs], [e1], scale=-1.0)
                lfn = sm.next()
                kb.act(lfn.a[:, :4], e1.a[:, :4], AF.Ln, [e1], [lfn], bias=1.0)
                yield
                bk = kb.nbank()
                kb.op("pe", lambda e: e.matmul(kb.bank(bk, 4), self.tri[d].a, lfn.a[:, :4], start=True, stop=True),
                      [self.tri[d], lfn], [kb.pb[bk]])
                yield
                bm = sm.next()
                kb.copy(bm.a[:, 0:4], kb.bank(bk, 4), [kb.pb[bk]], [bm])
                u = sm.next()
                kb.tt(u.a[:, :4], gi, bm.a[:, 0:4], ALU.add, [mgs, bm], [u])
                du = dus.next()
                kb.tt(du.a, self.identf.a.unsqueeze(1).to_broadcast(B3),
                      u.a[:, :4].unsqueeze(2).to_broadcast(B3), ALU.mult, [self.identf, u], [du])
                yield
                bku = kb.nbank()
                kb.op("pe", lambda e: e.matmul(kb.bank(bku), self.onesf.a, du.a.rearrange("p j s -> p (j s)"),
                                               start=True, stop=True), [self.onesf, du], [kb.pb[bku]])
                yield
                dm = dms.next()
                kb.tt(dm.a, kb.bank(bku).rearrange("p (j s) -> p j s", j=4),
                      self.mmask[d].a.unsqueeze(1).to_broadcast(B3), ALU.add, [kb.pb[bku], self.mmask[d]], [dm])
                cmax = sm.next()
                kb.op("dve", lambda e: e.tensor_reduce(cmax.a[:, :4], dm.a, axis=AX.X, op=ALU.max), [dm], [cmax])
                mx = sm.next()
                kb.tt(mx.a[:, :4], cmax.a[:, :4], mbc.a, ALU.max, [cmax, mbc], [mx])
                nmx = sm.next()
                kb.ts(nmx.a[:, :4], mx.a[:, :4], -1.0, 0.0, ALU.mult, ALU.add, [mx], [nmx])
                wi = sm.next()
                kb.tt(wi.a[:, :4], mbc.a, mx.a[:, :4], ALU.subtract, [mbc, mx], [wi])
                kb.tt(bm.a[:, 4:8], mx.a[:, :4], bm.a[:, 0:4], ALU.subtract, [mx, bm], [bm])
                yield
                E = es.next()
                for j in H:
                    kb.act(E.a[:, j, :], dm.a[:, j, :], AF.Exp, [dm, nmx], [E], bias=nmx.a[:, j:j + 1])
                kb.act(wi.a[:, :4], wi.a[:, :4], AF.Exp, [wi], [wi])
                emt = sm.next()
                kb.act(emt.a[:, :4], bm.a[:, 4:8], AF.Exp, [bm], [emt], scale=-1.0)
                yield
                bss = []
                for j in H:
                    bs = kb.nbank()
                    bss.append(bs)
                    kb.op("pe", lambda e, bs=bs, j=j: e.matmul(kb.bank(bs, 128), qts.a[:, j, tok], kts.a[:, j, tok],
                                                              start=True, stop=True), [qts, kts], [kb.pb[bs]])
                yield
                abl = []
                for j in H:
                    ab = abf.next()
                    abl.append(ab)
                    kb.tt(ab.a, kb.bank(bss[j], 128), E.a[:, j, :], ALU.mult, [kb.pb[bss[j]], E], [ab])
                yield
                b1s = []
                for j in H:
                    b1 = kb.nbank()
                    b1s.append(b1)
                    kb.op("pe", lambda e, b1=b1, j=j: e.matmul(kb.bank(b1, 257), qts.a[:, j, tok], cb.a[:, j, 0:257],
                                                              start=True, stop=True), [qts, cb], [kb.pb[b1]])
                yield
                t1l = []
                for j in H:
                    t1 = t1s.next()
                    t1l.append(t1)
                    kb.act(t1.a[:, :257], kb.bank(b1s[j], 257), AF.Identity, [kb.pb[b1s[j]], wi], [t1],
                           scale=wi.a[:, j:j + 1])
                yield
                btl = []
                for j in H:
                    bt = kb.nbank()
                    btl.append(bt)
                    kb.op("pe", lambda e, bt=bt, ab=abl[j]: e.transpose(kb.bank(bt, 128, BF16), ab.a, self.identb.a),
                          [abl[j], self.identb], [kb.pb[bt]])
                yield
                atl = []
                for j in H:
                    at = ats.next()
                    atl.append(at)
                    kb.copy(at.a, kb.bank(btl[j], 128, BF16), [kb.pb[btl[j]]], [at], eng="act" if j % 2 else "dve")
                yield
                b2s = []
                for j in H:
                    b2 = kb.nbank()
                    b2s.append(b2)
                    kb.op("pe", lambda e, b2=b2, j=j, at=atl[j]: e.matmul(kb.bank(b2, 257), at.a, va.a[:, tt, j, 0:257],
                                                                         start=True, stop=True),
                          [atl[j], va], [kb.pb[b2]])
                yield
                ndl, denl = [], []
                for j in H:
                    nd = nds.next()
                    ndl.append(nd)
                    kb.tt(nd.a[:, :257], t1l[j].a[:, :257], kb.bank(b2s[j], 257), ALU.add, [t1l[j], kb.pb[b2s[j]]], [nd])
                yield
                for j in H:
                    den = sm.next()
                    denl.append(den)
                    kb.act(den.a[:, 0:1], ndl[j].a[:, 256:257], AF.Abs, [ndl[j]], [den])
                yield
                for j in H:
                    den = denl[j]
                    kb.tt(den.a[:, 0:1], den.a[:, 0:1], emt.a[:, j:j + 1], ALU.max, [den, emt], [den])
                    kb.op("dve", lambda e, den=den: e.reciprocal(den.a[:, 0:1], den.a[:, 0:1]), [den], [den])
                for j in H:
                    kb.ts(hm.a[:, tt, j, :], ndl[j].a[:, 0:256], denl[j].a[:, 0:1], None, ALU.mult, None,
                          [ndl[j], denl[j]], [hm])
                yield
                if not do_update:
                    return
                bsel = kb.nbank()
                kb.op("pe", lambda e: e.matmul(kb.bank(bsel, 8), self.sel[d].a, bm.a[:, 0:8], start=True, stop=True),
                      [self.sel[d], bm], [kb.pb[bsel]])
                yield
                lst = sm.next()
                kb.copy(lst.a, kb.bank(bsel, 8), [kb.pb[bsel]], [lst])
                tmp = sm.next()
                kb.tt(tmp.a[:, :4], lst.a[:, 0:4], lst.a[:, 4:8], ALU.add, [lst], [tmp])
                kb.ts(tmp.a[:, :4], tmp.a[:, :4], -1.0, 0.0, ALU.mult, ALU.add, [tmp], [tmp])
                ws = sm.next()
                kb.tt(ws.a[:, :4], u.a[:, :4], tmp.a[:, :4], ALU.add, [u, tmp], [ws])
                dc = sm.next()
                kb.tt(dc.a[:, :4], mbc.a, tmp.a[:, :4], ALU.add, [mbc, tmp], [dc])
                yield
                kb.act(ws.a[:, :4], ws.a[:, :4], AF.Exp, [ws], [ws])
                kb.act(dc.a[:, :4], dc.a[:, :4], AF.Exp, [dc], [dc])
                yield
                kwl = []
                for j in H:
                    kw = kws.next()
                    kwl.append(kw)
                    kb.ts(kw.a, km.a[:, tt, j, :], ws.a[:, j:j + 1], None, ALU.mult, None, [km, ws], [kw])
                yield
                bul = []
                for j in H:
                    bu = kb.nbank()
                    bul.append(bu)
                    kb.op("pe", lambda e, bu=bu, j=j, kw=kwl[j]: e.matmul(kb.bank(bu, 257), kw.a, va.a[:, tt, j, 0:257],
                                                                         start=True, stop=True),
                          [kwl[j], va], [kb.pb[bu]])
                yield
                for j in H:
                    kb.stt(cm.a[:, j, 0:257], cm.a[:, j, 0:257], dc.a[:, j:j + 1], kb.bank(bul[j], 257),
                           ALU.mult, ALU.add, [cm, dc, kb.pb[bul[j]]], [cm])
                yield
                kb.copy(cb.a, cm.a, [cm], [cb], eng="act")
                kb.copy(mbc.a, lst.a[:, 4:8], [lst], [mbc])
                yield

            for sq in range(nseq):
                for d in range(2):
                    cm, cb, mbc = cms[d], cbs[d], mbcs[d]
                    kb.op("dve", lambda e, cm=cm: e.memset(cm.a, 0.0), writes=[cm])
                    if sample:
                        kb.dma("sp", cm.a[:, :, 0:256],
                               self.ins["sC"][l, d, hh * 4:(hh + 1) * 4].rearrange("j k v -> k j v"), writes=[cm])
                        kb.dma("sp", cm.a[:, :, 256:257],
                               self.ins["sn"][l, d, hh * 4:(hh + 1) * 4, :].rearrange("j (k o) -> k j o", o=1),
                               writes=[cm], merge=True, slow=True)
                        kb.dma("sp", mbc.a, self.ins["sm"][l, d, hh * 4:(hh + 1) * 4].partition_broadcast(128),
                               writes=[mbc])
                    else:
                        kb.op("dve", lambda e, mbc=mbc: e.memset(mbc.a, 0.0), writes=[mbc])
                    kb.copy(cb.a, cm.a, [cm], [cb], eng="act")
                for ci in range(ncs):
                    last = (ci == ncs - 1)
                    gens = [step(0, sq * ncs + ci, not (sample and last)),
                            step(1, sq * ncs + (ncs - 1 - ci), not (sample and last))]
                    alive = [True, True]
                    while any(alive):
                        for d in range(2):
                            if alive[d]:
                                try:
                                    next(gens[d])
                                except StopIteration:
                                    alive[d] = False
                if not sample:
                    O = self.outs
                    for d in range(2):
                        cm, mbc = cms[d], mbcs[d]
                        kb.dma("sp", O["nC"][sq, l, d, hh * 4:(hh + 1) * 4].rearrange("j k v -> k j v"),
                               cm.a[:, :, 0:256], reads=[cm])
                        kb.dma("sp", O["nn"][sq, l, d, hh * 4:(hh + 1) * 4, :].rearrange("j (k o) -> k j o", o=1),
                               cm.a[:, :, 256:257], reads=[cm], slow=True)
                        kb.dma("sp", O["nm"][sq, l, d:d + 1, hh * 4:(hh + 1) * 4], mbc.a[0:1, :], reads=[mbc])
            kb.barrier()
            kb.aoff = mark
            hm = hms[0]
            kb.tt(hm.a, hm.a, hms[1].a, ALU.add, [hm, hms[1]], [hm])
            mos = kb.tile([128, 8, 1024], BF16)
            kb.dma("sp", mos.a, self.MO[:, hh * 1024:(hh + 1) * 1024].rearrange("(t p) c -> p t c", p=128),
                   writes=[mos])
            mng = kb.tile([128, 1024], F32)
            kb.dma("sp", mng.a, self.ins["mng"][l, hh * 1024:(hh + 1) * 1024].partition_broadcast(128), writes=[mng])
            sqt = kb.tile([128, 8, 4, 256], F32)
            kb.tt(sqt.a, hm.a, hm.a, ALU.mult, [hm], [sqt])
            ss = kb.tile([128, 32], F32)
            kb.op("dve", lambda e: e.tensor_reduce(ss.a, sqt.a.rearrange("p t j d -> p (t j) d"), axis=AX.X,
                                                   op=ALU.add), [sqt], [ss])
            kb.act(ss.a, ss.a, AF.Ln, [ss], [ss], scale=1.0 / 256, bias=EPS)
            kb.act(ss.a, ss.a, AF.Exp, [ss], [ss], scale=-0.5)
            hm3 = hm.a.rearrange("p t j d -> p (t j) d")
            kb.tt(hm3, hm3, ss.a.unsqueeze(2).to_broadcast([128, 32, 256]), ALU.mult, [hm, ss], [hm])
            hm2 = hm.a.rearrange("p t j d -> p t (j d)")
            kb.tt(hm2, hm2, mng.a.unsqueeze(1).to_broadcast([128, 8, 1024]), ALU.mult, [hm, mng], [hm])
            hb = kb.tile([128, 8, 1024], BF16)
            kb.tt(hb.a, hm2, mos.a, ALU.mult, [hm, mos], [hb])
            hmts = kb.tile([128, 8, T], BF16)
            for tt in range(8):
                for c4 in range(2):
                    bk = kb.nbank()

                    def tr(e, tt=tt, c4=c4, bk=bk):
                        ins = None
                        for i in range(4):
                            c = c4 * 4 + i
                            ins = e.transpose(kb.bank(bk, 512, BF16)[:, i * 128:(i + 1) * 128],
                                              hb.a[:, tt, c * 128:(c + 1) * 128], self.identb.a)
                        return ins
                    kb.op("pe", tr, reads=[hb, self.identb], writes=[kb.pb[bk]])
                    kb.copy(hmts.a[:, c4 * 4:(c4 + 1) * 4, tt * 128:(tt + 1) * 128],
                            kb.bank(bk, 512, BF16).rearrange("p (c t) -> p c t", c=4), [kb.pb[bk]], [hmts],
                            eng="act" if c4 else "dve")
            kb.dma("sp", self.HMT[hh * 8:(hh + 1) * 8].rearrange("c p t -> p c t"), hmts.a, reads=[hmts])
            kb.stage_end()

    def stage_sgu(self, l, g):
        kb = self.kb
        svs = kb.tile([128, 8, 2048], F32)
        for tt in range(8):
            kb.dma("sp", svs.a[:, tt, :], self.SV[tt * 128:(tt + 1) * 128, :], writes=[svs], merge=(tt > 0))
        sgn = kb.tile([128, 2048], F32)
        kb.dma("sp", sgn.a, self.ins["sgn"][l].partition_broadcast(128), writes=[sgn])
        wt = kb.tile([128, 8, 128], BF16)
        kb.dma("pool", wt.a, self.ins["sgwT"][l].rearrange("g s t -> s g t"), writes=[wt])
        bs = kb.tile([128, 8, 128], F32)
        kb.dma("sp", bs.a, self.ins["sgb"][l].partition_broadcast(128), writes=[bs])
        junk = kb.tile([128, 2048], F32)
        ss = kb.tile([128, 8], F32)
        kb.op("dve", lambda e: e.memset(ss.a, 0.0), writes=[ss])
        for tt in range(8):
            kb.act(junk.a, svs.a[:, tt, :], AF.Square, [svs], [junk, ss], accum=ss.a[:, tt:tt + 1])
        kb.act(ss.a, ss.a, AF.Ln, [ss], [ss], scale=1.0 / 2048, bias=EPS)
        kb.act(ss.a, ss.a, AF.Exp, [ss], [ss], scale=-0.5)
        vn = kb.tile([128, 8, 2048], BF16)
        for tt in range(8):
            kb.stt(vn.a[:, tt, :], svs.a[:, tt, :], ss.a[:, tt:tt + 1], sgn.a, ALU.mult, ALU.mult,
                   [svs, ss, sgn], [vn])
        utr = kb.ring(3, [128, T], BF16)
        tmp = kb.ring(2, [128, 4, 128], F32)
        og = kb.ring(3, [128, T], BF16)
        for fc in range(16):
            gidx = fc // 2
            ut = utr.next()
            kb.dma("sp", ut.a, self.UT[fc], writes=[ut])
            o = og.next()
            for half in range(2):
                bk = kb.nbank()

                def mm(e, half=half, bk=bk, fc=fc, gidx=gidx):
                    ins = None
                    for i in range(4):
                        tt = half * 4 + i
                        ins = e.matmul(kb.bank(bk)[:, i * 128:(i + 1) * 128], vn.a[:, tt, fc * 128:(fc + 1) * 128],
                                       wt.a[:, gidx, :], start=True, stop=True)
                    return ins
                kb.op("pe", mm, reads=[vn, wt], writes=[kb.pb[bk]])
                t = tmp.next()
                kb.tt(t.a, kb.bank(bk).rearrange("p (c t) -> p c t", c=4),
                      bs.a[:, gidx, :].unsqueeze(1).to_broadcast([128, 4, 128]), ALU.add, [kb.pb[bk], bs], [t])
                kb.tt(o.a[:, half * 512:(half + 1) * 512], t.a.rearrange("p c t -> p (c t)"),
                      ut.a[:, half * 512:(half + 1) * 512], ALU.mult, [t, ut], [o])
            kb.dma("sp", self.HST[fc], o.a, reads=[o])
        kb.stage_end()

    def stage_attn(self, l, g):
        kb = self.kb
        sample = (g == 1)
        aq = kb.tile([128, 16, T], BF16)
        self.load_fm(aq, self.AQT, 16)
        ak = kb.tile([128, 4, T], BF16)
        self.load_fm(ak, self.AKT, 4)
        av = kb.tile([128, 8, 512], BF16)
        kb.dma("sp", av.a, self.AV.rearrange("(t p) c -> p t c", p=128), writes=[av])
        if sample:
            ckt = kb.tile([128, 4, 512], BF16)
            kb.dma("pool", ckt.a, self.ins["ck"][l].rearrange("(t p) c -> p t c", p=128), writes=[ckt])
            kcT = kb.tile([128, 4, 512], BF16)
            for k in range(4):
                bk = kb.nbank()

                def tr(e, k=k, bk=bk):
                    ins = None
                    for t in range(4):
                        ins = e.transpose(kb.bank(bk, 512, BF16)[:, t * 128:(t + 1) * 128],
                                          ckt.a[:, t, k * 128:(k + 1) * 128], self.identb.a)
                    return ins
                kb.op("pe", tr, reads=[ckt, self.identb], writes=[kb.pb[bk]])
                kb.copy(kcT.a[:, k, :], kb.bank(bk, 512, BF16), [kb.pb[bk]], [kcT])
            vc = kb.tile([128, 4, 512], BF16)
            kb.dma("pool", vc.a, self.ins["cv"][l].rearrange("(t p) c -> p t c", p=128), writes=[vc])
        hat = kb.tile([128, 16, T], BF16)
        sring = kb.ring(5, [128, 896], F32)
        pring = kb.ring(4, [128, 896], F32)
        pbr = kb.ring(5, [128, 896], BF16)
        ptr = kb.ring(2, [128, 7, 4, 128], BF16)
        sm = kb.ring(32, [128, 8], F32)
        nseq, nqb = (1, 8) if sample else (4, 2)
        for sq in range(nseq):
            for k in range(4):
                for j in range(nqb):
                    qt = sq * nqb + j
                    qtok = slice(qt * 128, (qt + 1) * 128)
                    if sample:
                        lo, hi = max(0, j - 1), min(7, j + 1)
                        nb = (hi - lo + 1) * 128
                        m0 = (lo - (j - 1)) * 128
                        nkeys = 512 + nb
                        vblocks = [(vc, t) for t in range(4)] + [(av, t) for t in range(lo, hi + 1)]
                    else:
                        nkeys = 256
                        vblocks = [(av, sq * 2), (av, sq * 2 + 1)]
                    nkb = nkeys // 128
                    pT = ptr.next()
                    H = range(4)
                    pps = [kb.npair() for _ in H]
                    ss_, mxs, ps_, rss, pbs, bts = [], [], [], [], [], []
                    for g4 in H:
                        h = 4 * k + g4
                        pp = pps[g4]
                        b0, b1 = kb.pb[2 * pp], kb.pb[2 * pp + 1]
                        if sample:
                            def mm(e, pp=pp, h=h, lo=lo, nb=nb):
                                e.matmul(kb.bank(2 * pp), aq.a[:, h, qtok], kcT.a[:, k, :], start=True, stop=True)
                                return e.matmul(kb.bank(2 * pp + 1, nb), aq.a[:, h, qtok],
                                                ak.a[:, k, lo * 128:lo * 128 + nb], start=True, stop=True)
                            kb.op("pe", mm, reads=[aq, ak, kcT], writes=[b0, b1])
                        else:
                            kb.op("pe", lambda e, pp=pp, h=h: e.matmul(kb.bank(2 * pp, 256), aq.a[:, h, qtok],
                                                                      ak.a[:, k, sq * 256:(sq + 1) * 256],
                                                                      start=True, stop=True),
                                  reads=[aq, ak], writes=[b0])
                    for g4 in H:
                        pp = pps[g4]
                        b0, b1 = kb.pb[2 * pp], kb.pb[2 * pp + 1]
                        s = sring.next()
                        ss_.append(s)
                        if sample:
                            kb.copy(s.a[:, 0:512], kb.bank(2 * pp), [b0], [s], eng="act")
                            kb.tt(s.a[:, 512:512 + nb], kb.bank(2 * pp + 1, nb), self.amask.a[:, m0:m0 + nb], ALU.add,
                                  [b1, self.amask, s], [s])
                        else:
                            kb.copy(s.a[:, 0:256], kb.bank(2 * pp, 256), [b0], [s], eng="act")
                    for g4 in H:
                        s = ss_[g4]
                        mx = sm.next()
                        mxs.append(mx)
                        kb.op("dve", lambda e, mx=mx, s=s: e.tensor_reduce(mx.a[:, 0:1], s.a[:, :nkeys], axis=AX.X,
                                                                          op=ALU.max), [s], [mx])
                    for g4 in H:
                        sk = self.sinkb.a[:, l, 4 * k + g4:4 * k + g4 + 1]
                        mx = mxs[g4]
                        kb.tt(mx.a[:, 0:1], mx.a[:, 0:1], sk, ALU.max, [mx, self.sinkb], [mx])
                        kb.tt(mx.a[:, 2:3], sk, mx.a[:, 0:1], ALU.subtract, [mx, self.sinkb], [mx])
                        kb.ts(mx.a[:, 1:2], mx.a[:, 0:1], -1.0, 0.0, ALU.mult, ALU.add, [mx], [mx])
                        rs = sm.next()
                        rss.append(rs)
                        kb.op("dve", lambda e, rs=rs: e.memset(rs.a, 0.0), writes=[rs])
                    for g4 in H:
                        s, mx, rs = ss_[g4], mxs[g4], rss[g4]
                        p = pring.next()
                        ps_.append(p)
                        kb.act(p.a[:, :nkeys], s.a[:, :nkeys], AF.Exp, [s, mx, rs], [p, rs], bias=mx.a[:, 1:2],
                               accum=rs.a[:, 0:1])
                        kb.act(rs.a[:, 1:2], mx.a[:, 2:3], AF.Exp, [mx, rs], [rs])
                    for g4 in H:
                        rs = rss[g4]
                        kb.tt(rs.a[:, 2:3], rs.a[:, 0:1], rs.a[:, 1:2], ALU.add, [rs], [rs])
                        kb.op("dve", lambda e, rs=rs: e.reciprocal(rs.a[:, 3:4], rs.a[:, 2:3]), [rs], [rs])
                    for g4 in H:
                        p, rs = ps_[g4], rss[g4]
                        pb_ = pbr.next()
                        pbs.append(pb_)
                        kb.ts(pb_.a[:, :nkeys], p.a[:, :nkeys], rs.a[:, 3:4], None, ALU.mult, None, [p, rs], [pb_])
                    for g4 in H:
                        pb_ = pbs[g4]
                        bt = kb.nbank()
                        bts.append(bt)

                        def tr(e, bt=bt, pb_=pb_):
                            ins = None
                            for i in range(nkb):
                                ins = e.transpose(kb.bank(bt, 1024, BF16)[:, i * 128:(i + 1) * 128],
                                                  pb_.a[:, i * 128:(i + 1) * 128], self.identb.a)
                            return ins
                        kb.op("pe", tr, reads=[pb_, self.identb], writes=[kb.pb[bt]])
                    for g4 in H:
                        bt = bts[g4]
                        kb.copy(pT.a[:, 0:nkb, g4, :],
                                kb.bank(bt, 1024, BF16).rearrange("p (b q) -> p b q", q=128)[:, 0:nkb, :],
                                [kb.pb[bt]], [pT], eng="act" if g4 % 2 else "dve")
                    bo = kb.nbank()

                    def pv(e, bo=bo, pT=pT, vblocks=vblocks):
                        ins = None
                        for i, (vt, t) in enumerate(vblocks):
                            ins = e.matmul(kb.bank(bo), vt.a[:, t, k * 128:(k + 1) * 128],
                                           pT.a[:, i, :, :].rearrange("p g q -> p (g q)"),
                                           start=(i == 0), stop=(i == len(vblocks) - 1))
                        return ins
                    kb.op("pe", pv, reads=[pT, av] + ([vc] if sample else []), writes=[kb.pb[bo]])
                    kb.copy(hat.a[:, 4 * k:4 * k + 4, qtok], kb.bank(bo).rearrange("p (g q) -> p g q", g=4),
                            [kb.pb[bo]], [hat], eng="act")
        for c0 in (0, 8):
            kb.dma("sp", self.HAT[c0:c0 + 8].rearrange("c p t -> p c t"), hat.a[:, c0:c0 + 8, :], reads=[hat])
        kb.stage_end()

    def stage_merge(self, l, g):
        kb = self.kb
        I = self.ins
        hb = [kb.tile([128, 16, T], BF16) for _ in range(3)]
        for t, src in zip(hb, (self.HMT, self.HST, self.HAT)):
            self.load_fm(t, src, 16)
        Ws = (I["w_br_m"][l], I["w_br_s"][l], I["w_br_a"][l])
        slots = [kb.tile([128, 16, 256], BF16) for _ in range(4)]
        gtr = kb.ring(6, [128, T], BF16)
        acc = kb.ring(2, [128, T], F32)
        tmp = kb.ring(2, [128, T], F32)
        yo = kb.ring(3, [128, T], BF16)
        state = {}

        def mk_epi(b):
            def epi(ci, P, banks):
                gt = gtr.next()
                kb.dma("sp", gt.a, self.GTS[b * KC + ci], writes=[gt])
                if b == 0:
                    a = acc.next()
                    state[ci] = a
                    kb.tt(a.a, P, gt.a, ALU.mult, banks + [gt], [a])
                else:
                    a = state[ci]
                    t = tmp.next()
                    kb.tt(t.a, P, gt.a, ALU.mult, banks + [gt], [t])
                    if b == 1:
                        kb.tt(a.a, a.a, t.a, ALU.add, [a, t], [a])
                    else:
                        o = yo.next()
                        kb.tt(o.a, a.a, t.a, ALU.add, [a, t], [o])
                        kb.dma("sp", self.YT[ci], o.a, reads=[o])
            return epi
        for c0 in range(0, KC, 2):
            for b in range(3):
                def epi_b(ci, P, banks, b=b, c0=c0):
                    mk_epi(b)(c0 + ci, P, banks)
                kb.gemm_fm(Ws[b], 16, hb[b], T, [(c0 + i) * 128 for i in range(2)], epi_b, slots, nb=2)
        kb.stage_end()

    def stage_wout(self, l, g):
        kb = self.kb
        yt = kb.tile([128, KC, T], BF16)
        self.load_fm(yt, self.YT, KC)
        slots = [kb.tile([128, KC, 512], BF16) for _ in range(2)]
        xin = kb.ring(3, [128, T], F32)

        def epi(ci, P, banks):
            xi = xin.next()
            kb.dma("sp", xi.a, self.XT[ci], writes=[xi])
            kb.stt(xi.a, P, self.modv(l, 2, g)[:, ci:ci + 1], xi.a, ALU.mult, ALU.add, banks + [self.MOD, xi], [xi])
            kb.dma("sp", self.XT[ci], xi.a, reads=[xi])
        kb.gemm_fm(self.ins["w_out"][l], KC, yt, T, [i * 128 for i in range(KC)], epi, slots)
        kb.stage_end()

    def stage_ffn_up(self, l, g):
        kb = self.kb
        sample = (g == 1)
        ht = kb.tile([128, KC, T], BF16)
        self.load_fm(ht, self.HT, KC)
        slots = [kb.tile([128, KC, 256], BF16) for _ in range(3)]
        self.start_bg_mod(3)
        tg = kb.ring(2, [128, T], F32)
        tu = kb.ring(2, [128, T], F32)
        go = kb.ring(3, [128, T], BF16)
        cw = self.cw.a
        cbv = self.cb.a
        st = {}

        def sh(ap, a, b):
            if sample:
                return ap[:, a:T + b]
            v = ap.rearrange("p (s t) -> p s t", s=4)
            return v[:, :, a:256 + b]

        def conv(dst, P, banks, f):
            kb.act(dst.a, P, AF.Identity, banks + [self.cw, self.cb], [dst], scale=cw[:, l, 1, f:f + 1],
                   bias=cbv[:, l, f:f + 1])
            kb.stt(sh(dst.a, 1, 0), sh(P, 0, -1), cw[:, l, 0, f:f + 1], sh(dst.a, 1, 0), ALU.mult, ALU.add,
                   banks + [dst, self.cw], [dst])
            kb.stt(sh(dst.a, 0, -1), sh(P, 1, 0), cw[:, l, 2, f:f + 1], sh(dst.a, 0, -1), ALU.mult, ALU.add,
                   banks + [dst, self.cw], [dst])

        def epi(ci, P, banks):
            f = ci // 2
            if ci % 2 == 0:
                t = tg.next()
                conv(t, P, banks, f)
                kb.act(t.a, t.a, AF.Silu, [t], [t])
                st["g"] = t
            else:
                t = tu.next()
                conv(t, P, banks, FC + f)
                o = go.next()
                kb.tt(o.a, st["g"].a, t.a, ALU.mult, [st["g"], t], [o])
                kb.dma("sp", self.GU[f], o.a, reads=[o])
        chunks = []
        for f in range(FC):
            chunks += [f * 128, (FC + f) * 128]
        kb.gemm_fm(self.ins["ffn_up"][l], KC, ht, T, chunks, epi, slots, nb=2)
        kb.drain_bg()
        kb.stage_end()

    def stage_ffn_down(self, l, g):
        kb = self.kb
        for half in range(2):
            gu = kb.tile([128, FC, 512], BF16)
            for c0 in range(0, FC, 8):
                n = min(8, FC - c0)
                kb.dma("sp", gu.a[:, c0:c0 + n, :],
                       self.GU[c0:c0 + n, :, half * 512:(half + 1) * 512].rearrange("c p t -> p c t"),
                       writes=[gu], merge=(c0 > 0))
            slots = [kb.tile([128, FC, 128], BF16) for _ in range(3)]
            xin = kb.ring(3, [128, 512], F32)

            def epi(ci, P, banks, half=half):
                xi = xin.next()
                kb.dma("sp", xi.a, self.XT[ci][:, half * 512:(half + 1) * 512], writes=[xi])
                kb.stt(xi.a, P, self.modv(l, 5, g)[:, ci:ci + 1], xi.a, ALU.mult, ALU.add,
                       banks + [self.MOD, xi], [xi])
                kb.dma("sp", self.XT[ci][:, half * 512:(half + 1) * 512], xi.a, reads=[xi])
            kb.gemm_fm(self.ins["ffn_down"][l], FC, gu, 512, [i * 128 for i in range(KC)], epi, slots, nb=1)
            kb.stage_end()

    def build(self):
        kb = self.kb
        self.stage_mod()
        for g in self.groups:
            self.stage_loadx(g)
            for l in self.layers:
                self.stage_norm(l, g, 1)
                if self.upto <= 1:
                    continue
                self.stage_inproj(l, g)
                if self.upto <= 2:
                    continue
                self.stage_mlstm(l, g)
                if self.upto <= 3:
                    continue
                self.stage_sgu(l, g)
                if self.upto <= 4:
                    continue
                self.stage_attn(l, g)
                if self.upto <= 5:
                    continue
                self.stage_merge(l, g)
                self.stage_wout(l, g)
                self.stage_norm(l, g, 2)
                self.stage_ffn_up(l, g)
                self.stage_ffn_down(l, g)
            if self.upto > 5:
                self.stage_norm(0, g, 3)
        kb.barrier()
        return self.nc


def host_consts():
    c = {}
    c["c_ident"] = np.eye(128, dtype=np.float32)
    s = np.arange(128)
    tri_f = (s[:, None] <= s[None, :]).astype(np.float32)
    tri_r = (s[:, None] >= s[None, :]).astype(np.float32)
    c["c_tri"] = np.stack([tri_f, tri_r])
    sel_f = np.zeros((128, 128), np.float32); sel_f[127, :] = 1.0
    sel_r = np.zeros((128, 128), np.float32); sel_r[0, :] = 1.0
    c["c_sel"] = np.stack([sel_f, sel_r])
    mf = np.where(s[None, :] <= s[:, None], 0.0, NEG).astype(np.float32)
    mr = np.where(s[None, :] >= s[:, None], 0.0, NEG).astype(np.float32)
    c["c_mmask"] = np.stack([mf, mr])
    a = np.arange(128)[:, None]; r = np.arange(384)[None, :]
    c["c_amask"] = np.where(np.abs(r - 128 - a) <= 128, 0.0, NEG).astype(np.float32)
    t = np.arange(T)
    row = (t // 64).astype(np.float64); col = (t % 64).astype(np.float64)
    inv = 10000.0 ** (-np.arange(32, dtype=np.float64) / 32.0)
    p = np.arange(128)
    pos = np.where((p < 64)[:, None], row[None, :], col[None, :])
    ang = (pos.astype(np.float32) * inv.astype(np.float32)[p % 32][:, None]).astype(np.float32)
    cos = np.cos(ang.astype(np.float64)); sin = np.sin(ang.astype(np.float64))
    sign = np.where((p % 64) < 32, -1.0, 1.0)[:, None]
    c["c_rope"] = np.stack([cos, sin * sign]).astype(np.float32)
    partner = np.where((p % 64) < 32, p + 32, p - 32)
    perm = np.zeros((128, 128), np.float32)
    perm[partner, p] = 1.0
    c["c_perm"] = perm
    return c


_NC_CACHE = {}


def make_in_maps(inp, ncores=8):
    f = lambda a: np.ascontiguousarray(np.asarray(a, dtype=np.float32))
    xp = f(inp["x_prompt"]); xs = f(inp["x_sample"])
    shared = {}
    for k in ("ada_w", "w_in", "w_br_m", "w_br_s", "w_br_a", "w_out", "ffn_up", "ffn_down"):
        shared[k] = f(inp[k])
    shared["adabF"] = np.ascontiguousarray(np.stack([fmv(inp["ada_b"][l]) for l in range(NL)], 1))
    shared["n1F"] = np.ascontiguousarray(np.stack([fmv(inp["norm1_g"][l]) for l in range(NL)], 1))
    shared["n2F"] = np.ascontiguousarray(np.stack([fmv(inp["norm2_g"][l]) for l in range(NL)], 1))
    shared["nfF"] = fmv(inp["final_g"])
    shared["mgb"] = f(inp["m_gate_b"]).reshape(NL, 32)
    shared["mng"] = f(inp["m_norm_g"]); shared["sgn"] = f(inp["sgu_norm_g"])
    shared["sgwT"] = np.ascontiguousarray(f(inp["sgu_w"]).transpose(0, 1, 3, 2))
    shared["sgb"] = f(inp["sgu_b"]); shared["sink"] = f(inp["attn_sink"])
    cw = f(inp["ffn_conv_w"])
    shared["cwF"] = np.ascontiguousarray(
        np.stack([np.stack([fmv(cw[l, k]) for k in range(3)], 1) for l in range(NL)], 1))
    shared["cbF"] = np.ascontiguousarray(np.stack([fmv(inp["ffn_conv_b"][l]) for l in range(NL)], 1))
    shared.update(host_consts())
    in_maps = []
    for c in range(ncores):
        j = c // 2
        m = dict(shared)
        m["xg"] = np.ascontiguousarray(np.stack([xp[4 * c:4 * c + 4].reshape(T, D), xs[j]]))
        cv = np.stack([f(inp["c_ctx"]), f(inp["c"])[j]])
        m["cvecF"] = np.ascontiguousarray(cv.reshape(2, KC, 128).transpose(2, 1, 0))
        m["ck"] = np.ascontiguousarray(f(inp["cache_k"])[j].reshape(NL, 512, 512))
        m["cv"] = np.ascontiguousarray(f(inp["cache_v"])[j].reshape(NL, 512, 512))
        m["sC"] = np.ascontiguousarray(f(inp["state_C"])[j])
        m["sn"] = np.ascontiguousarray(f(inp["state_n"])[j])
        m["sm"] = np.ascontiguousarray(f(inp["state_m"])[j])
        in_maps.append(m)
    return in_maps


def kernel(**inp):
    ncores = 8
    in_maps = make_in_maps(inp, ncores)
    if "nc" not in _NC_CACHE:
        _NC_CACHE["nc"] = Prog().build()
    nc = _NC_CACHE["nc"]
    res = run_bass_kernel_spmd(nc, in_maps, core_ids=list(range(ncores)))
    R = res.results
    y_prompt = np.concatenate([R[c]["y"][0].reshape(4, 256, D) for c in range(ncores)], 0)
    y_sample = np.stack([np.concatenate([R[2 * j]["y"][1][:512], R[2 * j + 1]["y"][1][512:]], 0) for j in range(4)], 0)
    nk = np.concatenate([R[c]["nk"].reshape(NL, 4, 256, 4, 128).transpose(1, 0, 2, 3, 4) for c in range(ncores)], 0)
    nv = np.concatenate([R[c]["nv"].reshape(NL, 4, 256, 4, 128).transpose(1, 0, 2, 3, 4) for c in range(ncores)], 0)
    nC = np.concatenate([R[c]["nC"] for c in range(ncores)], 0)
    nn = np.concatenate([R[c]["nn"] for c in range(ncores)], 0)
    nm = np.concatenate([R[c]["nm"] for c in range(ncores)], 0)
    return (y_prompt.astype(np.float32), y_sample.astype(np.float32), np.ascontiguousarray(nk),
            np.ascontiguousarray(nv), nC, nn, nm)
```

```python
import numpy as np
import concourse.bass as bass
import concourse.mybir as mybir
from concourse.bass_utils import run_bass_kernel_spmd

F32 = mybir.dt.float32
BF16 = mybir.dt.bfloat16
AF = mybir.ActivationFunctionType
ALU = mybir.AluOpType
AX = mybir.AxisListType
RING = 8
NEG = -1.0e30
EPS = 1e-6

D = 4096
T = 1024
KC = 32
NL = 2
NIN = 25632
FF = 11008
FC = 86
SBUF_TOP = 206 * 1024
C_MQ, C_MK, C_MV, C_MO, C_MG, C_SU, C_SV, C_AQ, C_AK, C_AV, C_GT = (
    0, 1024, 2048, 4096, 6144, 6176, 8224, 10272, 12320, 12832, 13344)

DEBUG_OUT = []


class Dep:
    __slots__ = ("w", "r")

    def __init__(self):
        self.w = {}
        self.r = {}


class Tl(Dep):
    __slots__ = ("a",)

    def __init__(self, a):
        Dep.__init__(self)
        self.a = a


class Ring:
    def __init__(self, tiles):
        self.t = tiles
        self.i = 0

    def next(self):
        t = self.t[self.i % len(self.t)]
        self.i += 1
        return t


class KB:
    def __init__(self):
        self.nc = bass.Bass("TRN2", target_bir_lowering=False)
        nc = self.nc
        self.eng = {"pe": nc.tensor, "act": nc.scalar, "dve": nc.vector, "pool": nc.gpsimd, "sp": nc.sync}
        self.sems = {}
        self.ccnt = {}
        for e in ("pe", "act", "dve", "pool"):
            self.sems[e] = nc.alloc_semaphore("c_" + e)
            self.ccnt[e] = 0
        self.dcnt = {}
        for q in ("sp", "pool", "act"):
            self.dcnt[q] = 0
            for i in range(RING):
                self.sems[(q, i)] = nc.alloc_semaphore("d_%s%d" % (q, i))
        self.known = {e: {} for e in self.eng}
        self.psum = nc.alloc_psum_tensor("psum_all", [128, 4096], F32).ap()
        self.pb = [Dep() for _ in range(8)]
        self.excl = set(id(d) for d in self.pb)
        self.n_ins = 0
        self.poff = (nc.sbuf_base + 63) // 64 * 64
        self.top = nc.sbuf_top
        self.abase = None
        self.aoff = None
        self.tn = 0
        self.bctr = 0
        self.pctr = 0
        self.wctr = 0

    def _alloc(self, shape, dt, off):
        self.tn += 1
        h = self.nc.alloc_sbuf_tensor_at("t%d" % self.tn, list(shape), dt, offset=off)
        return Tl(h.ap())

    @staticmethod
    def _nbytes(shape, dt):
        n = 1
        for s in shape[1:]:
            n *= s
        n *= 4 if dt == F32 else 2
        return (n + 63) // 64 * 64

    def ptile(self, shape, dt=F32):
        t = self._alloc(shape, dt, self.poff)
        self.poff += self._nbytes(shape, dt)
        return t

    def start_arena(self):
        self.abase = self.poff
        self.aoff = self.abase

    def tile(self, shape, dt=F32):
        nb = self._nbytes(shape, dt)
        assert self.aoff + nb <= self.top, ("sbuf overflow", self.aoff, nb)
        t = self._alloc(shape, dt, self.aoff)
        self.aoff += nb
        return t

    def ring(self, n, shape, dt=F32):
        return Ring([self.tile(shape, dt) for _ in range(n)])

    def stage_end(self):
        self.barrier()
        self.aoff = self.abase

    def bank(self, i, n=512, dt=F32):
        a = self.psum[:, i * 512:(i + 1) * 512]
        if dt == BF16:
            a = a.bitcast(BF16)
        return a[:, :n]

    def nbank(self):
        b = self.bctr % 8
        self.bctr += 1
        return b

    def npair(self):
        p = self.pctr % 4
        self.pctr += 1
        return p

    def op(self, e, fn, reads=(), writes=(), dma=False, merge=False):
        need = {}
        if not merge:
            for b in reads:
                for k, v in b.w.items():
                    if need.get(k, 0) < v:
                        need[k] = v
                if id(b) in self.excl:
                    for k, v in b.r.items():
                        if k != e and need.get(k, 0) < v:
                            need[k] = v
            for b in writes:
                for k, v in b.w.items():
                    if need.get(k, 0) < v:
                        need[k] = v
                for k, v in b.r.items():
                    if need.get(k, 0) < v:
                        need[k] = v
        if dma:
            i = self.dcnt[e] % RING
            key = (e, i)
            v = 16 * (self.dcnt[e] // RING + 1)
            if v > 16 and need.get(key, 0) < v - 16:
                need[key] = v - 16
            self.dcnt[e] += 1
        else:
            key = e
            self.ccnt[e] += 1
            v = self.ccnt[e]
        engine = self.eng[e]
        kn = self.known[e]
        for k2, v2 in need.items():
            if k2 == "pe" and e == "pe" and not dma:
                continue
            if kn.get(k2, 0) < v2:
                engine.wait_ge(self.sems[k2], v2)
                kn[k2] = v2
        ins = fn(engine)
        ins.then_inc(self.sems[key], 16 if dma else 1)
        self.n_ins += 1
        for b in reads:
            b.r[key] = v
        for b in writes:
            if merge:
                b.w[key] = v
            else:
                b.w = {key: v}
                b.r = {}
        return (key, v)

    def barrier(self):
        evs = []
        for e, c in self.ccnt.items():
            if c:
                evs.append((e, c))
        for q, c in self.dcnt.items():
            for i in range(RING):
                n = (c - i + RING - 1) // RING if c > i else 0
                if n:
                    evs.append(((q, i), 16 * n))
        for e in self.eng:
            kn = self.known[e]
            for k, v in evs:
                if kn.get(k, 0) < v:
                    self.eng[e].wait_ge(self.sems[k], v)
                    kn[k] = v

    def dma(self, q, out, in_, reads=(), writes=(), merge=False, slow=False):
        if slow:
            return self.op(q, lambda e: e.dma_start(out=out, in_=in_, allow_slow_non_contiguous=True),
                           reads, writes, dma=True, merge=merge)
        return self.op(q, lambda e: e.dma_start(out=out, in_=in_), reads, writes, dma=True, merge=merge)

    def act(self, out, in_, func, reads, writes, bias=None, scale=None, accum=None):
        kw = {}
        if bias is not None:
            kw["bias"] = bias
        if scale is not None:
            kw["scale"] = scale
        if accum is not None:
            kw["accum_out"] = accum
        return self.op("act", lambda e: e.activation(out, in_, func, **kw), reads, writes)

    def ts(self, out, in0, s1, s2, op0, op1, reads, writes, eng="dve"):
        if s2 is None:
            return self.op(eng, lambda e: e.tensor_scalar(out, in0, s1, None, op0=op0), reads, writes)
        return self.op(eng, lambda e: e.tensor_scalar(out, in0, s1, s2, op0=op0, op1=op1), reads, writes)

    def tt(self, out, in0, in1, op, reads, writes, eng="dve"):
        return self.op(eng, lambda e: e.tensor_tensor(out, in0, in1, op=op), reads, writes)

    def stt(self, out, in0, s, in1, op0, op1, reads, writes, accum=None, eng="dve"):
        if accum is not None:
            return self.op(eng, lambda e: e.scalar_tensor_tensor(out, in0, s, in1, op0=op0, op1=op1,
                                                                 accum_out=accum), reads, writes)
        return self.op(eng, lambda e: e.scalar_tensor_tensor(out, in0, s, in1, op0=op0, op1=op1), reads, writes)

    def copy(self, out, in_, reads, writes, eng="dve"):
        if eng == "act":
            return self.act(out, in_, AF.Identity, reads, writes)
        return self.op(eng, lambda e: e.tensor_copy(out, in_), reads, writes)

    def wload(self, slot, W2d, KCn, c0, ncols, dst0, first):
        for kk in range(0, KCn, 8):
            n = min(8, KCn - kk)
            src = W2d[kk * 128:(kk + n) * 128, c0:c0 + ncols].rearrange("(k p) n -> p k n", p=128)
            self.dma("pool", slot.a[:, kk:kk + n, dst0:dst0 + ncols], src, writes=[slot],
                     merge=not (first and kk == 0))

    def gemm_fm(self, W2d, KCn, xt, Tn, chunks, epi, slots, nb=4):
        nth = max(1, Tn // 512)
        tw = min(Tn, 512)
        for b0 in range(0, len(chunks), nb):
            blk = chunks[b0:b0 + nb]
            slot = slots[self.wctr % len(slots)]
            self.wctr += 1
            runs = []
            for i, c in enumerate(blk):
                if runs and runs[-1][0] + runs[-1][1] * 128 == c:
                    runs[-1][1] += 1
                else:
                    runs.append([c, 1, i])
            first = True
            for (c0, n, di) in runs:
                self.wload(slot, W2d, KCn, c0, n * 128, di * 128, first)
                first = False
            for i, c in enumerate(blk):
                pp = self.npair()
                banks = [self.pb[2 * pp + t] for t in range(nth)]

                def mm(e, i=i, pp=pp, slot=slot):
                    ins = None
                    for kc in range(KCn):
                        for t in range(nth):
                            ins = e.matmul(self.bank(2 * pp + t, tw), slot.a[:, kc, i * 128:(i + 1) * 128],
                                           xt.a[:, kc, t * 512:t * 512 + tw],
                                           start=(kc == 0), stop=(kc == KCn - 1))
                    return ins
                self.op("pe", mm, reads=[slot, xt], writes=banks)
                epi(b0 + i, self.psum[:, pp * 1024:pp * 1024 + Tn], banks)

    def gemm_tm(self, W2d, KCn, xt, ntiles, c0, ncols, epi, slots, bw=512):
        for cb in range(0, ncols, bw):
            n = min(bw, ncols - cb)
            slot = slots[self.wctr % len(slots)]
            self.wctr += 1
            self.wload(slot, W2d, KCn, c0 + cb, n, 0, True)
            for tt in range(ntiles):
                bk = self.nbank()

                def mm(e, tt=tt, bk=bk, slot=slot, n=n):
                    ins = None
                    for kc in range(KCn):
                        ins = e.matmul(self.bank(bk, n), xt.a[:, kc, tt * 128:(tt + 1) * 128],
                                       slot.a[:, kc, 0:n], start=(kc == 0), stop=(kc == KCn - 1))
                    return ins
                self.op("pe", mm, reads=[slot, xt], writes=[self.pb[bk]])
                epi(cb, n, tt, self.bank(bk, n), self.pb[bk])


def fmv(v):
    v = np.asarray(v, np.float32)
    return np.ascontiguousarray(v.reshape(-1, 128).T)


class Prog:
    def __init__(self, layers=(0, 1), groups=(0, 1), upto=99, dummy=(), nlw=NL, only=None):
        self.dummy = set(dummy)
        self.only = only
        self.kb = KB()
        kb = self.kb
        nc = kb.nc
        self.nc = nc
        self.layers = layers
        self.groups = groups
        self.upto = upto
        self.ins = {}
        self.outs = {}
        I = self.inp
        I("xg", [2, T, D])
        I("cvecF", [128, KC, 2])
        I("ck", [NL, 512, 512]); I("cv", [NL, 512, 512])
        I("sC", [NL, 2, 8, 128, 256]); I("sn", [NL, 2, 8, 128]); I("sm", [NL, 2, 8])
        I("ada_w", [nlw, D, 6 * D]); I("adabF", [128, NL, 192])
        I("n1F", [128, NL, KC]); I("n2F", [128, NL, KC]); I("nfF", [128, KC])
        I("w_in", [nlw, D, NIN])
        I("mgb", [NL, 32]); I("mng", [NL, 2048]); I("sgn", [NL, 2048])
        I("sgwT", [NL, 8, 128, 128]); I("sgb", [NL, 8, 128]); I("sink", [NL, 16])
        I("w_br_m", [nlw, 2048, D]); I("w_br_s", [nlw, 2048, D]); I("w_br_a", [nlw, 2048, D])
        I("w_out", [nlw, D, D]); I("ffn_up", [nlw, D, 2 * FF]); I("ffn_down", [nlw, FF, D])
        I("cwF", [128, NL, 3, 2 * FC]); I("cbF", [128, NL, 2 * FC])
        I("c_ident", [128, 128]); I("c_tri", [2, 128, 128]); I("c_sel", [2, 128, 128])
        I("c_mmask", [2, 128, 128]); I("c_amask", [128, 384]); I("c_rope", [2, 128, T]); I("c_perm", [128, 128])
        O = self.outp
        O("y", [2, T, D])
        O("nk", [NL, T, 512]); O("nv", [NL, T, 512])
        O("nC", [4, NL, 2, 8, 128, 256]); O("nn", [4, NL, 2, 8, 128]); O("nm", [4, NL, 2, 8])
        S = self.scr
        self.XT = S("XT", [KC, 128, T], F32)
        self.HT = S("HT", [KC, 128, T], BF16)
        self.QT = S("QT", [8, 128, T], BF16); self.KT = S("KT", [8, 128, T], BF16)
        self.UT = S("UT", [16, 128, T], BF16)
        self.AQT = S("AQT", [16, 128, T], BF16); self.AKT = S("AKT", [4, 128, T], BF16)
        self.GTS = S("GTS", [96, 128, T], BF16)
        self.MV = S("MV", [T, 2048], BF16); self.MO = S("MO", [T, 2048], BF16)
        self.MG = S("MG", [T, 32], F32); self.SV = S("SV", [T, 2048], F32)
        self.AV = S("AV", [T, 512], BF16)
        self.HMT = S("HMT", [16, 128, T], BF16); self.HST = S("HST", [16, 128, T], BF16)
        self.HAT = S("HAT", [16, 128, T], BF16)
        self.YT = S("YT", [KC, 128, T], BF16)
        self.GU = S("GU", [FC, 128, T], BF16)
        self.consts()

    def inp(self, name, shape):
        if name in self.dummy:
            shape = [1, 128, 128]
        self.ins[name] = self.nc.dram_tensor(name, list(shape), F32, kind="ExternalInput").ap()

    def outp(self, name, shape):
        self.outs[name] = self.nc.dram_tensor(name, list(shape), F32, kind="ExternalOutput").ap()

    def scr(self, name, shape, dt):
        if name in DEBUG_OUT:
            a = self.nc.dram_tensor("dbg_" + name, list(shape), dt, kind="ExternalOutput").ap()
            self.outs["dbg_" + name] = a
            return a
        return self.nc.dram_tensor(name, list(shape), dt).ap()

    def consts(self):
        kb = self.kb
        I = self.ins

        def ld(name, shape, src, dt=F32, q=None):
            t = kb.ptile(shape, dt)
            kb.dma(q or ("sp" if dt == F32 else "pool"), t.a, src, writes=[t])
            return t
        self.identf = ld("identf", [128, 128], I["c_ident"])
        self.identb = ld("identb", [128, 128], I["c_ident"], BF16)
        self.tri = [ld("tri%d" % d, [128, 128], I["c_tri"][d]) for d in range(2)]
        self.sel = [ld("sel%d" % d, [128, 128], I["c_sel"][d]) for d in range(2)]
        self.mmask = [ld("mm%d" % d, [128, 128], I["c_mmask"][d]) for d in range(2)]
        self.amask = ld("amask", [128, 384], I["c_amask"])
        self.cosT = ld("cos", [128, T], I["c_rope"][0])
        self.sinT = ld("sin", [128, T], I["c_rope"][1])
        self.perm = ld("perm", [128, 128], I["c_perm"])
        self.onesf = kb.ptile([128, 128], F32)
        kb.op("dve", lambda e: e.memset(self.onesf.a, 1.0), writes=[self.onesf])
        self.adab = ld("adab", [128, NL, 192], I["adabF"])
        self.n1 = ld("n1", [128, NL, KC], I["n1F"])
        self.n2 = ld("n2", [128, NL, KC], I["n2F"])
        self.nf = ld("nf", [128, KC], I["nfF"])
        self.cw = ld("cw", [128, NL, 3, 2 * FC], I["cwF"])
        self.cb = ld("cb", [128, NL, 2 * FC], I["cbF"])
        self.mgb = ld("mgb", [128, NL, 32], I["mgb"].partition_broadcast(128))
        self.sinkb = ld("sinkb", [128, NL, 16], I["sink"].partition_broadcast(128))
        self.MOD = kb.ptile([128, NL, 192, 2], F32)
        self.A = kb.ptile([128, KC], F32)
        self.zero = kb.ptile([128, 1], F32)
        kb.op("dve", lambda e: e.memset(self.zero.a, 0.0), writes=[self.zero])
        kb.start_arena()

    def stage_mod(self):
        kb = self.kb
        I = self.ins
        if "ada_w" in self.dummy:
            kb.op("dve", lambda e: e.memset(self.MOD.a, 0.0), writes=[self.MOD])
            kb.stage_end()
            return
        cf = kb.tile([128, KC, 2], F32)
        kb.dma("sp", cf.a, I["cvecF"], writes=[cf])
        xc = kb.tile([128, KC, 2], BF16)
        kb.act(xc.a, cf.a, AF.Silu, [cf], [xc])
        slots = [kb.tile([128, KC, 512], BF16) for _ in range(2)]
        for l in self.layers:
            def epi(ci, P, banks, l=l):
                kb.ts(self.MOD.a[:, l, ci, :], P, self.adab.a[:, l, ci:ci + 1], None, ALU.add, None,
                      banks + [self.adab], [self.MOD])
            kb.gemm_fm(I["ada_w"][l], KC, xc, 2, list(range(0, 6 * D, 128)), epi, slots)
        kb.stage_end()

    def modv(self, l, j, g):
        return self.MOD.a[:, l, j * KC:(j + 1) * KC, g]

    def stage_loadx(self, g):
        kb = self.kb
        xin = kb.ring(2, [128, D], F32)
        stg = kb.ring(4, [128, 4, 128], F32)
        for tt in range(8):
            xi = xin.next()
            kb.dma("sp", xi.a, self.ins["xg"][g, tt * 128:(tt + 1) * 128, :], writes=[xi])
            for k4 in range(8):
                bk = kb.nbank()

                def tr(e, xi=xi, k4=k4, bk=bk):
                    ins = None
                    for i in range(4):
                        c = k4 * 4 + i
                        ins = e.transpose(kb.bank(bk)[:, i * 128:(i + 1) * 128], xi.a[:, c * 128:(c + 1) * 128],
                                          self.identf.a)
                    return ins
                kb.op("pe", tr, reads=[xi, self.identf], writes=[kb.pb[bk]])
                s = stg.next()
                kb.copy(s.a, kb.bank(bk).rearrange("p (c t) -> p c t", c=4), [kb.pb[bk]], [s],
                        eng="act" if k4 % 2 else "dve")
                kb.dma("sp", self.XT[k4 * 4:(k4 + 1) * 4, :, tt * 128:(tt + 1) * 128].rearrange("c p t -> p c t"),
                       s.a, reads=[s])
        kb.stage_end()

    def stage_norm(self, l, g, which):
        kb = self.kb
        if which == 3:
            Aap = self.nf.a
            Adep = self.nf
        else:
            gn = (self.n1 if which == 1 else self.n2)
            j = 0 if which == 1 else 3
            kb.stt(self.A.a, self.modv(l, j + 1, g), 1.0, gn.a[:, l, :], ALU.add, ALU.mult,
                   [self.MOD, gn], [self.A])
            Aap = self.A.a
            Adep = self.A
        xin = kb.ring(3, [128, T], F32)
        sq = kb.ring(2, [128, T], F32)
        for kc in range(KC):
            xi = xin.next()
            kb.dma("sp", xi.a, self.XT[kc], writes=[xi])
            s = sq.next()
            kb.act(s.a, xi.a, AF.Square, [xi], [s])

            def mm(e, s=s, kc=kc):
                e.matmul(kb.bank(0), self.onesf.a, s.a[:, 0:512], start=(kc == 0), stop=(kc == KC - 1))
                return e.matmul(kb.bank(1), self.onesf.a, s.a[:, 512:1024], start=(kc == 0), stop=(kc == KC - 1))
            kb.op("pe", mm, reads=[s, self.onesf], writes=[kb.pb[0], kb.pb[1]])
        rstd = kb.tile([128, T], F32)
        kb.act(rstd.a, kb.psum[:, 0:T], AF.Ln, [kb.pb[0], kb.pb[1]], [rstd], scale=1.0 / D, bias=EPS)
        kb.act(rstd.a, rstd.a, AF.Exp, [rstd], [rstd], scale=-0.5)
        tmp = kb.ring(2, [128, T], F32)
        if which == 3:
            yf = kb.tile([128, KC, T], F32)
        else:
            hst = kb.ring(3, [128, T], BF16)
        for kc in range(KC):
            xi = xin.next()
            kb.dma("sp", xi.a, self.XT[kc], writes=[xi])
            tm = tmp.next()
            kb.tt(tm.a, xi.a, rstd.a, ALU.mult, [xi, rstd], [tm])
            if which == 3:
                kb.act(yf.a[:, kc, :], tm.a, AF.Identity, [tm, Adep], [yf], scale=Aap[:, kc:kc + 1])
            else:
                h = hst.next()
                kb.act(h.a, tm.a, AF.Identity, [tm, Adep, self.MOD], [h], scale=Aap[:, kc:kc + 1],
                       bias=self.modv(l, j, g)[:, kc:kc + 1])
                kb.dma("sp", self.HT[kc], h.a, reads=[h])
        if which == 3:
            yo = kb.ring(4, [128, 512], F32)
            for tt in range(8):
                for k4 in range(8):
                    bk = kb.nbank()

                    def tr(e, k4=k4, bk=bk, tt=tt):
                        ins = None
                        for i in range(4):
                            ins = e.transpose(kb.bank(bk)[:, i * 128:(i + 1) * 128],
                                              yf.a[:, k4 * 4 + i, tt * 128:(tt + 1) * 128], self.identf.a)
                        return ins
                    kb.op("pe", tr, reads=[yf, self.identf], writes=[kb.pb[bk]])
                    y = yo.next()
                    kb.copy(y.a, kb.bank(bk), [kb.pb[bk]], [y], eng="act" if k4 % 2 else "dve")
                    kb.dma("sp", self.outs["y"][g, tt * 128:(tt + 1) * 128, k4 * 512:(k4 + 1) * 512], y.a, reads=[y])
        kb.stage_end()

    def load_fm(self, dst, src, nch, q="sp"):
        for c0 in range(0, nch, 8):
            n = min(8, nch - c0)
            self.kb.dma(q, dst.a[:, c0:c0 + n, :], src[c0:c0 + n].rearrange("c p t -> p c t"),
                        writes=[dst], merge=(c0 > 0))

    def stage_inproj(self, l, g):
        kb = self.kb
        W = self.ins["w_in"][l]
        ht = kb.tile([128, KC, T], BF16)
        self.load_fm(ht, self.HT, KC)
        slots = [kb.tile([128, KC, 512], BF16) for _ in range(2)]
        ob = kb.ring(4, [128, T], BF16)
        of = kb.ring(3, [128, T], F32)
        tb = kb.ring(4, [128, 512], BF16)
        tf = kb.ring(4, [128, 512], F32)
        sample = (g == 1)

        def fm_plain(dst, func, scale, eng):
            def epi(ci, P, banks):
                o = ob.next()
                if eng == "act":
                    kb.act(o.a, P, func, banks, [o], scale=scale)
                else:
                    kb.ts(o.a, P, scale, 0.0, ALU.mult, ALU.add, banks, [o])
                kb.dma("sp", dst[ci], o.a, reads=[o])
            return epi

        def fm_rope(dst, scale):
            def epi(ci, P, banks):
                xf = of.next()
                kb.act(xf.a, P, AF.Identity, banks, [xf], scale=scale)
                pp = kb.npair()
                b2 = [kb.pb[2 * pp], kb.pb[2 * pp + 1]]

                def mm(e, pp=pp, xf=xf):
                    e.matmul(kb.bank(2 * pp), self.perm.a, xf.a[:, 0:512], start=True, stop=True)
                    return e.matmul(kb.bank(2 * pp + 1), self.perm.a, xf.a[:, 512:1024], start=True, stop=True)
                kb.op("pe", mm, reads=[xf, self.perm], writes=b2)
                t2 = of.next()
                kb.tt(t2.a, kb.psum[:, pp * 1024:(pp + 1) * 1024], self.sinT.a, ALU.mult, b2 + [self.sinT], [t2])
                kb.tt(xf.a, xf.a, self.cosT.a, ALU.mult, [xf, self.cosT], [xf])
                o = ob.next()
                kb.tt(o.a, xf.a, t2.a, ALU.add, [xf, t2], [o])
                kb.dma("sp", dst[ci], o.a, reads=[o])
            return epi

        def ch(c0, n):
            return [c0 + 128 * i for i in range(n)]
        on = lambda k: self.only is None or k in self.only
        if on("mq"):
            kb.gemm_fm(W, KC, ht, T, ch(C_MQ, 8), fm_plain(self.QT, None, 128 ** -0.5, "dve"), slots)
        if on("mk"):
            kb.gemm_fm(W, KC, ht, T, ch(C_MK, 8), fm_plain(self.KT, None, 1.0, "dve"), slots)
        if on("su"):
            kb.gemm_fm(W, KC, ht, T, ch(C_SU, 16), fm_plain(self.UT, AF.Gelu_apprx_tanh, 1.0, "act"), slots)
        if not on("aq"):
            pass
        elif sample:
            kb.gemm_fm(W, KC, ht, T, ch(C_AQ, 16), fm_rope(self.AQT, 128 ** -0.5), slots)
            kb.gemm_fm(W, KC, ht, T, ch(C_AK, 4), fm_rope(self.AKT, 1.0), slots)
        else:
            kb.gemm_fm(W, KC, ht, T, ch(C_AQ, 16), fm_plain(self.AQT, None, 128 ** -0.5, "dve"), slots)
            kb.gemm_fm(W, KC, ht, T, ch(C_AK, 4), fm_plain(self.AKT, None, 1.0, "dve"), slots)
        if on("gt"):
            kb.gemm_fm(W, KC, ht, T, ch(C_GT, 96), fm_plain(self.GTS, AF.Sigmoid, 1.0, "act"), slots)

        def tm_bf16(dst, func):
            def epi(cb, n, tt, P, bd):
                o = tb.next()
                if func is None:
                    kb.copy(o.a[:, :n], P, [bd], [o])
                else:
                    kb.act(o.a[:, :n], P, func, [bd], [o])
                kb.dma("sp", dst[tt * 128:(tt + 1) * 128, cb:cb + n], o.a[:, :n], reads=[o])
            return epi

        def tm_f32(dst, func, also_bf16=None):
            def epi(cb, n, tt, P, bd):
                o = tf.next()
                if func is None:
                    kb.copy(o.a[:, :n], P, [bd], [o], eng="act")
                else:
                    kb.act(o.a[:, :n], P, func, [bd], [o])
                if dst is not None:
                    kb.dma("sp", dst[tt * 128:(tt + 1) * 128, cb:cb + n], o.a[:, :n], reads=[o])
                if also_bf16 is not None:
                    o2 = tb.next()
                    kb.copy(o2.a[:, :n], o.a[:, :n], [o], [o2])
                    kb.dma("sp", also_bf16[tt * 128:(tt + 1) * 128, cb:cb + n], o2.a[:, :n], reads=[o2])
            return epi

        def tm_mg(cb, n, tt, P, bd):
            o = tf.next()
            kb.tt(o.a[:, :32], P, self.mgb.a[:, l, :], ALU.add, [bd, self.mgb], [o])
            kb.dma("sp", self.MG[tt * 128:(tt + 1) * 128, :], o.a[:, :32], reads=[o])
        if on("mv"):
            kb.gemm_tm(W, KC, ht, 8, C_MV, 2048, tm_bf16(self.MV, None), slots)
        if on("mo"):
            kb.gemm_tm(W, KC, ht, 8, C_MO, 2048, tm_bf16(self.MO, AF.Sigmoid), slots)
        if on("mg"):
            kb.gemm_tm(W, KC, ht, 8, C_MG, 32, tm_mg, slots)
        if on("sv"):
            kb.gemm_tm(W, KC, ht, 8, C_SV, 2048, tm_f32(self.SV, AF.Gelu_apprx_tanh), slots)
        if not on("av"):
            pass
        elif sample:
            kb.gemm_tm(W, KC, ht, 8, C_AV, 512, tm_bf16(self.AV, None), slots)
        else:
            kb.gemm_tm(W, KC, ht, 8, C_AV, 512, tm_f32(self.outs["nv"][l], None, also_bf16=self.AV), slots)
            kb.gemm_tm(W, KC, ht, 8, C_AK, 512, tm_f32(self.outs["nk"][l], None), slots)
        kb.stage_end()

    def stage_mlstm(self, l, g):
        kb = self.kb
        sample = (g == 1)
        nseq = 1 if sample else 4
        ncs = 8 if sample else 2
        B3 = [128, 4, 128]
        H = range(4)
        for hh in range(2):
            hms = [kb.tile([128, 8, 4, 256], F32) for _ in range(2)]
            mark = kb.aoff
            qts = kb.tile([128, 4, T], BF16)
            kts = kb.tile([128, 4, T], BF16)
            self.load_fm(qts, self.QT[hh * 4:(hh + 1) * 4], 4)
            self.load_fm(kts, self.KT[hh * 4:(hh + 1) * 4], 4)
            km = kb.tile([128, 8, 4, 128], BF16)
            for tt in range(8):
                bk = kb.nbank()

                def tr(e, tt=tt, bk=bk):
                    ins = None
                    for j in range(4):
                        ins = e.transpose(kb.bank(bk, 512, BF16)[:, j * 128:(j + 1) * 128],
                                          kts.a[:, j, tt * 128:(tt + 1) * 128], self.identb.a)
                    return ins
                kb.op("pe", tr, reads=[kts, self.identb], writes=[kb.pb[bk]])
                kb.copy(km.a[:, tt, :, :], kb.bank(bk, 512, BF16).rearrange("p (j d) -> p j d", j=4),
                        [kb.pb[bk]], [km], eng="act" if tt % 2 else "dve")
            va = kb.tile([128, 8, 4, 260], BF16)
            kb.op("dve", lambda e: e.memset(va.a[:, :, :, 256:260], 1.0), writes=[va])
            for tt in range(8):
                kb.dma("sp", va.a[:, tt, :, 0:256],
                       self.MV[tt * 128:(tt + 1) * 128, hh * 1024:(hh + 1) * 1024].rearrange("p (j d) -> p j d", j=4),
                       writes=[va], merge=True)
            mgs = kb.tile([128, 8, 32], F32)
            kb.dma("sp", mgs.a, self.MG.rearrange("(t p) c -> p t c", p=128), writes=[mgs])
            cms = [kb.tile([128, 4, 260], F32) for _ in range(2)]
            cbs = [kb.tile([128, 4, 260], BF16) for _ in range(2)]
            mbcs = [kb.tile([128, 4], F32) for _ in range(2)]
            sm = kb.ring(96, [128, 8], F32)
            dus = kb.ring(3, B3, F32)
            dms = kb.ring(3, B3, F32)
            es = kb.ring(3, B3, F32)
            abf = kb.ring(8, [128, 128], BF16)
            ats = kb.ring(8, [128, 128], BF16)
            kws = kb.ring(8, [128, 128], BF16)
            t1s = kb.ring(8, [128, 260], F32)
            nds = kb.ring(8, [128, 260], F32)

            def step(d, tt, do_update):
                cm, cb, mbc, hm = cms[d], cbs[d], mbcs[d], hms[d]
                tok = slice(tt * 128, (tt + 1) * 128)
                gi = mgs.a[:, tt, (2 * d) * 8 + hh * 4:(2 * d) * 8 + hh * 4 + 4]
                gf = mgs.a[:, tt, (2 * d + 1) * 8 + hh * 4:(2 * d + 1) * 8 + hh * 4 + 4]
                e1 = sm.next()
                kb.act(e1.a[:, :4], gf, AF.Exp, [mgs], [e1], scale=-1.0)
                lfn = sm.next()
                kb.act(lfn.a[:, :4], e1.a[:, :4], AF.Ln, [e1], [lfn], bias=1.0)
                yield
                bk = kb.nbank()
                kb.op("pe", lambda e: e.matmul(kb.bank(bk, 4), self.tri[d].a, lfn.a[:, :4], start=True, stop=True),
                      [self.tri[d], lfn], [kb.pb[bk]])
                yield
                bm = sm.next()
                kb.copy(bm.a[:, 0:4], kb.bank(bk, 4), [kb.pb[bk]], [bm])
                u = sm.next()
                kb.tt(u.a[:, :4], gi, bm.a[:, 0:4], ALU.add, [mgs, bm], [u])
                du = dus.next()
                kb.tt(du.a, self.identf.a.unsqueeze(1).to_broadcast(B3),
                      u.a[:, :4].unsqueeze(2).to_broadcast(B3), ALU.mult, [self.identf, u], [du])
                yield
                bku = kb.nbank()
                kb.op("pe", lambda e: e.matmul(kb.bank(bku), self.onesf.a, du.a.rearrange("p j s -> p (j s)"),
                                               start=True, stop=True), [self.onesf, du], [kb.pb[bku]])
                yield
                dm = dms.next()
                kb.tt(dm.a, kb.bank(bku).rearrange("p (j s) -> p j s", j=4),
                      self.mmask[d].a.unsqueeze(1).to_broadcast(B3), ALU.add, [kb.pb[bku], self.mmask[d]], [dm])
                cmax = sm.next()
                kb.op("dve", lambda e: e.tensor_reduce(cmax.a[:, :4], dm.a, axis=AX.X, op=ALU.max), [dm], [cmax])
                mx = sm.next()
                kb.tt(mx.a[:, :4], cmax.a[:, :4], mbc.a, ALU.max, [cmax, mbc], [mx])
                nmx = sm.next()
                kb.ts(nmx.a[:, :4], mx.a[:, :4], -1.0, 0.0, ALU.mult, ALU.add, [mx], [nmx])
                wi = sm.next()
                kb.tt(wi.a[:, :4], mbc.a, mx.a[:, :4], ALU.subtract, [mbc, mx], [wi])
                kb.tt(bm.a[:, 4:8], mx.a[:, :4], bm.a[:, 0:4], ALU.subtract, [mx, bm], [bm])
                yield
                E = es.next()
                for j in H:
                    kb.act(E.a[:, j, :], dm.a[:, j, :], AF.Exp, [dm, nmx], [E], bias=nmx.a[:, j:j + 1])
                kb.act(wi.a[:, :4], wi.a[:, :4], AF.Exp, [wi], [wi])
                emt = sm.next()
                kb.act(emt.a[:, :4], bm.a[:, 4:8], AF.Exp, [bm], [emt], scale=-1.0)
                yield
                bss = []
                for j in H:
                    bs = kb.nbank()
                    bss.append(bs)
                    kb.op("pe", lambda e, bs=bs, j=j: e.matmul(kb.bank(bs, 128), qts.a[:, j, tok], kts.a[:, j, tok],
                                                              start=True, stop=True), [qts, kts], [kb.pb[bs]])
                yield
                abl = []
                for j in H:
                    ab = abf.next()
                    abl.append(ab)
                    kb.tt(ab.a, kb.bank(bss[j], 128), E.a[:, j, :], ALU.mult, [kb.pb[bss[j]], E], [ab])
                yield
                b1s = []
                for j in H:
                    b1 = kb.nbank()
                    b1s.append(b1)
                    kb.op("pe", lambda e, b1=b1, j=j: e.matmul(kb.bank(b1, 257), qts.a[:, j, tok], cb.a[:, j, 0:257],
                                                              start=True, stop=True), [qts, cb], [kb.pb[b1]])
                yield
                t1l = []
                for j in H:
                    t1 = t1s.next()
                    t1l.append(t1)
                    kb.act(t1.a[:, :257], kb.bank(b1s[j], 257), AF.Identity, [kb.pb[b1s[j]], wi], [t1],
                           scale=wi.a[:, j:j + 1])
                yield
                btl = []
                for j in H:
                    bt = kb.nbank()
                    btl.append(bt)
                    kb.op("pe", lambda e, bt=bt, ab=abl[j]: e.transpose(kb.bank(bt, 128, BF16), ab.a, self.identb.a),
                          [abl[j], self.identb], [kb.pb[bt]])
                yield
                atl = []
                for j in H:
                    at = ats.next()
                    atl.append(at)
                    kb.copy(at.a, kb.bank(btl[j], 128, BF16), [kb.pb[btl[j]]], [at], eng="act" if j % 2 else "dve")
                yield
                b2s = []
                for j in H:
                    b2 = kb.nbank()
                    b2s.append(b2)
                    kb.op("pe", lambda e, b2=b2, j=j, at=atl[j]: e.matmul(kb.bank(b2, 257), at.a, va.a[:, tt, j, 0:257],
                                                                         start=True, stop=True),
                          [atl[j], va], [kb.pb[b2]])
                yield
                ndl, denl = [], []
                for j in H:
                    nd = nds.next()
                    ndl.append(nd)
                    kb.tt(nd.a[:, :257], t1l[j].a[:, :257], kb.bank(b2s[j], 257), ALU.add, [t1l[j], kb.pb[b2s[j]]], [nd])
                yield
                for j in H:
                    den = sm.next()
                    denl.append(den)
                    kb.act(den.a[:, 0:1], ndl[j].a[:, 256:257], AF.Abs, [ndl[j]], [den])
                yield
                for j in H:
                    den = denl[j]
                    kb.tt(den.a[:, 0:1], den.a[:, 0:1], emt.a[:, j:j + 1], ALU.max, [den, emt], [den])
                    kb.op("dve", lambda e, den=den: e.reciprocal(den.a[:, 0:1], den.a[:, 0:1]), [den], [den])
                for j in H:
                    kb.ts(hm.a[:, tt, j, :], ndl[j].a[:, 0:256], denl[j].a[:, 0:1], None, ALU.mult, None,
                          [ndl[j], denl[j]], [hm])
                yield
                if not do_update:
                    return
                bsel = kb.nbank()
                kb.op("pe", lambda e: e.matmul(kb.bank(bsel, 8), self.sel[d].a, bm.a[:, 0:8], start=True, stop=True),
                      [self.sel[d], bm], [kb.pb[bsel]])
                yield
                lst = sm.next()
                kb.copy(lst.a, kb.bank(bsel, 8), [kb.pb[bsel]], [lst])
                tmp = sm.next()
                kb.tt(tmp.a[:, :4], lst.a[:, 0:4], lst.a[:, 4:8], ALU.add, [lst], [tmp])
                kb.ts(tmp.a[:, :4], tmp.a[:, :4], -1.0, 0.0, ALU.mult, ALU.add, [tmp], [tmp])
                ws = sm.next()
                kb.tt(ws.a[:, :4], u.a[:, :4], tmp.a[:, :4], ALU.add, [u, tmp], [ws])
                dc = sm.next()
                kb.tt(dc.a[:, :4], mbc.a, tmp.a[:, :4], ALU.add, [mbc, tmp], [dc])
                yield
                kb.act(ws.a[:, :4], ws.a[:, :4], AF.Exp, [ws], [ws])
                kb.act(dc.a[:, :4], dc.a[:, :4], AF.Exp, [dc], [dc])
                yield
                kwl = []
                for j in H:
                    kw = kws.next()
                    kwl.append(kw)
                    kb.ts(kw.a, km.a[:, tt, j, :], ws.a[:, j:j + 1], None, ALU.mult, None, [km, ws], [kw])
                yield
                bul = []
                for j in H:
                    bu = kb.nbank()
                    bul.append(bu)
                    kb.op("pe", lambda e, bu=bu, j=j, kw=kwl[j]: e.matmul(kb.bank(bu, 257), kw.a, va.a[:, tt, j, 0:257],
                                                                         start=True, stop=True),
                          [kwl[j], va], [kb.pb[bu]])
                yield
                for j in H:
                    kb.stt(cm.a[:, j, 0:257], cm.a[:, j, 0:257], dc.a[:, j:j + 1], kb.bank(bul[j], 257),
                           ALU.mult, ALU.add, [cm, dc, kb.pb[bul[j]]], [cm])
                yield
                kb.copy(cb.a, cm.a, [cm], [cb], eng="act")
                kb.copy(mbc.a, lst.a[:, 4:8], [lst], [mbc])
                yield

            for sq in range(nseq):
                for d in range(2):
                    cm, cb, mbc = cms[d], cbs[d], mbcs[d]
                    kb.op("dve", lambda e, cm=cm: e.memset(cm.a, 0.0), writes=[cm])
                    if sample:
                        kb.dma("sp", cm.a[:, :, 0:256],
                               self.ins["sC"][l, d, hh * 4:(hh + 1) * 4].rearrange("j k v -> k j v"), writes=[cm])
                        kb.dma("sp", cm.a[:, :, 256:257],
                               self.ins["sn"][l, d, hh * 4:(hh + 1) * 4, :].rearrange("j (k o) -> k j o", o=1),
                               writes=[cm], merge=True, slow=True)
                        kb.dma("sp", mbc.a, self.ins["sm"][l, d, hh * 4:(hh + 1) * 4].partition_broadcast(128),
                               writes=[mbc])
                    else:
                        kb.op("dve", lambda e, mbc=mbc: e.memset(mbc.a, 0.0), writes=[mbc])
                    kb.copy(cb.a, cm.a, [cm], [cb], eng="act")
                for ci in range(ncs):
                    last = (ci == ncs - 1)
                    gens = [step(0, sq * ncs + ci, not (sample and last)),
                            step(1, sq * ncs + (ncs - 1 - ci), not (sample and last))]
                    alive = [True, True]
                    while any(alive):
                        for d in range(2):
                            if alive[d]:
                                try:
                                    next(gens[d])
                                except StopIteration:
                                    alive[d] = False
                if not sample:
                    O = self.outs
                    for d in range(2):
                        cm, mbc = cms[d], mbcs[d]
                        kb.dma("sp", O["nC"][sq, l, d, hh * 4:(hh + 1) * 4].rearrange("j k v -> k j v"),
                               cm.a[:, :, 0:256], reads=[cm])
                        kb.dma("sp", O["nn"][sq, l, d, hh * 4:(hh + 1) * 4, :].rearrange("j (k o) -> k j o", o=1),
                               cm.a[:, :, 256:257], reads=[cm], slow=True)
                        kb.dma("sp", O["nm"][sq, l, d:d + 1, hh * 4:(hh + 1) * 4], mbc.a[0:1, :], reads=[mbc])
            kb.barrier()
            kb.aoff = mark
            hm = hms[0]
            kb.tt(hm.a, hm.a, hms[1].a, ALU.add, [hm, hms[1]], [hm])
            mos = kb.tile([128, 8, 1024], BF16)
            kb.dma("sp", mos.a, self.MO[:, hh * 1024:(hh + 1) * 1024].rearrange("(t p) c -> p t c", p=128),
                   writes=[mos])
            mng = kb.tile([128, 1024], F32)
            kb.dma("sp", mng.a, self.ins["mng"][l, hh * 1024:(hh + 1) * 1024].partition_broadcast(128), writes=[mng])
            sqt = kb.tile([128, 8, 4, 256], F32)
            kb.tt(sqt.a, hm.a, hm.a, ALU.mult, [hm], [sqt])
            ss = kb.tile([128, 32], F32)
            kb.op("dve", lambda e: e.tensor_reduce(ss.a, sqt.a.rearrange("p t j d -> p (t j) d"), axis=AX.X,
                                                   op=ALU.add), [sqt], [ss])
            kb.act(ss.a, ss.a, AF.Ln, [ss], [ss], scale=1.0 / 256, bias=EPS)
            kb.act(ss.a, ss.a, AF.Exp, [ss], [ss], scale=-0.5)
            hm3 = hm.a.rearrange("p t j d -> p (t j) d")
            kb.tt(hm3, hm3, ss.a.unsqueeze(2).to_broadcast([128, 32, 256]), ALU.mult, [hm, ss], [hm])
            hm2 = hm.a.rearrange("p t j d -> p t (j d)")
            kb.tt(hm2, hm2, mng.a.unsqueeze(1).to_broadcast([128, 8, 1024]), ALU.mult, [hm, mng], [hm])
            hb = kb.tile([128, 8, 1024], BF16)
            kb.tt(hb.a, hm2, mos.a, ALU.mult, [hm, mos], [hb])
            hmts = kb.tile([128, 8, T], BF16)
            for tt in range(8):
                for c4 in range(2):
                    bk = kb.nbank()

                    def tr(e, tt=tt, c4=c4, bk=bk):
                        ins = None
                        for i in range(4):
                            c = c4 * 4 + i
                            ins = e.transpose(kb.bank(bk, 512, BF16)[:, i * 128:(i + 1) * 128],
                                              hb.a[:, tt, c * 128:(c + 1) * 128], self.identb.a)
                        return ins
                    kb.op("pe", tr, reads=[hb, self.identb], writes=[kb.pb[bk]])
                    kb.copy(hmts.a[:, c4 * 4:(c4 + 1) * 4, tt * 128:(tt + 1) * 128],
                            kb.bank(bk, 512, BF16).rearrange("p (c t) -> p c t", c=4), [kb.pb[bk]], [hmts],
                            eng="act" if c4 else "dve")
            kb.dma("sp", self.HMT[hh * 8:(hh + 1) * 8].rearrange("c p t -> p c t"), hmts.a, reads=[hmts])
            kb.stage_end()

    def stage_sgu(self, l, g):
        kb = self.kb
        svs = kb.tile([128, 8, 2048], F32)
        for tt in range(8):
            kb.dma("sp", svs.a[:, tt, :], self.SV[tt * 128:(tt + 1) * 128, :], writes=[svs], merge=(tt > 0))
        sgn = kb.tile([128, 2048], F32)
        kb.dma("sp", sgn.a, self.ins["sgn"][l].partition_broadcast(128), writes=[sgn])
        wt = kb.tile([128, 8, 128], BF16)
        kb.dma("pool", wt.a, self.ins["sgwT"][l].rearrange("g s t -> s g t"), writes=[wt])
        bs = kb.tile([128, 8, 128], F32)
        kb.dma("sp", bs.a, self.ins["sgb"][l].partition_broadcast(128), writes=[bs])
        junk = kb.tile([128, 2048], F32)
        ss = kb.tile([128, 8], F32)
        kb.op("dve", lambda e: e.memset(ss.a, 0.0), writes=[ss])
        for tt in range(8):
            kb.act(junk.a, svs.a[:, tt, :], AF.Square, [svs], [junk, ss], accum=ss.a[:, tt:tt + 1])
        kb.act(ss.a, ss.a, AF.Ln, [ss], [ss], scale=1.0 / 2048, bias=EPS)
        kb.act(ss.a, ss.a, AF.Exp, [ss], [ss], scale=-0.5)
        vn = kb.tile([128, 8, 2048], BF16)
        for tt in range(8):
            kb.stt(vn.a[:, tt, :], svs.a[:, tt, :], ss.a[:, tt:tt + 1], sgn.a, ALU.mult, ALU.mult,
                   [svs, ss, sgn], [vn])
        utr = kb.ring(3, [128, T], BF16)
        tmp = kb.ring(2, [128, 4, 128], F32)
        og = kb.ring(3, [128, T], BF16)
        for fc in range(16):
            gidx = fc // 2
            ut = utr.next()
            kb.dma("sp", ut.a, self.UT[fc], writes=[ut])
            o = og.next()
            for half in range(2):
                bk = kb.nbank()

                def mm(e, half=half, bk=bk, fc=fc, gidx=gidx):
                    ins = None
                    for i in range(4):
                        tt = half * 4 + i
                        ins = e.matmul(kb.bank(bk)[:, i * 128:(i + 1) * 128], vn.a[:, tt, fc * 128:(fc + 1) * 128],
                                       wt.a[:, gidx, :], start=True, stop=True)
                    return ins
                kb.op("pe", mm, reads=[vn, wt], writes=[kb.pb[bk]])
                t = tmp.next()
                kb.tt(t.a, kb.bank(bk).rearrange("p (c t) -> p c t", c=4),
                      bs.a[:, gidx, :].unsqueeze(1).to_broadcast([128, 4, 128]), ALU.add, [kb.pb[bk], bs], [t])
                kb.tt(o.a[:, half * 512:(half + 1) * 512], t.a.rearrange("p c t -> p (c t)"),
                      ut.a[:, half * 512:(half + 1) * 512], ALU.mult, [t, ut], [o])
            kb.dma("sp", self.HST[fc], o.a, reads=[o])
        kb.stage_end()

    def stage_attn(self, l, g):
        kb = self.kb
        sample = (g == 1)
        aq = kb.tile([128, 16, T], BF16)
        self.load_fm(aq, self.AQT, 16)
        ak = kb.tile([128, 4, T], BF16)
        self.load_fm(ak, self.AKT, 4)
        av = kb.tile([128, 8, 512], BF16)
        kb.dma("sp", av.a, self.AV.rearrange("(t p) c -> p t c", p=128), writes=[av])
        if sample:
            ckt = kb.tile([128, 4, 512], BF16)
            kb.dma("pool", ckt.a, self.ins["ck"][l].rearrange("(t p) c -> p t c", p=128), writes=[ckt])
            kcT = kb.tile([128, 4, 512], BF16)
            for k in range(4):
                bk = kb.nbank()

                def tr(e, k=k, bk=bk):
                    ins = None
                    for t in range(4):
                        ins = e.transpose(kb.bank(bk, 512, BF16)[:, t * 128:(t + 1) * 128],
                                          ckt.a[:, t, k * 128:(k + 1) * 128], self.identb.a)
                    return ins
                kb.op("pe", tr, reads=[ckt, self.identb], writes=[kb.pb[bk]])
                kb.copy(kcT.a[:, k, :], kb.bank(bk, 512, BF16), [kb.pb[bk]], [kcT])
            vc = kb.tile([128, 4, 512], BF16)
            kb.dma("pool", vc.a, self.ins["cv"][l].rearrange("(t p) c -> p t c", p=128), writes=[vc])
        hat = kb.tile([128, 16, T], BF16)
        sring = kb.ring(5, [128, 896], F32)
        pring = kb.ring(4, [128, 896], F32)
        pbr = kb.ring(5, [128, 896], BF16)
        ptr = kb.ring(2, [128, 7, 4, 128], BF16)
        sm4 = kb.ring(8, [128, 16], F32)
        nseq, nqb = (1, 8) if sample else (4, 2)
        for sq in range(nseq):
            for k in range(4):
                for j in range(nqb):
                    qt = sq * nqb + j
                    qtok = slice(qt * 128, (qt + 1) * 128)
                    if sample:
                        lo, hi = max(0, j - 1), min(7, j + 1)
                        nb = (hi - lo + 1) * 128
                        m0 = (lo - (j - 1)) * 128
                        nkeys = 512 + nb
                        vblocks = [(vc, t) for t in range(4)] + [(av, t) for t in range(lo, hi + 1)]
                    else:
                        nkeys = 256
                        vblocks = [(av, sq * 2), (av, sq * 2 + 1)]
                    nkb = nkeys // 128
                    pT = ptr.next()
                    H = range(4)
                    pps = [kb.npair() for _ in H]
                    ss_, mxs, ps_, rss, pbs, bts = [], [], [], [], [], []
                    for g4 in H:
                        h = 4 * k + g4
                        pp = pps[g4]
                        b0, b1 = kb.pb[2 * pp], kb.pb[2 * pp + 1]
                        if sample:
                            def mm(e, pp=pp, h=h, lo=lo, nb=nb):
                                e.matmul(kb.bank(2 * pp), aq.a[:, h, qtok], kcT.a[:, k, :], start=True, stop=True)
                                return e.matmul(kb.bank(2 * pp + 1, nb), aq.a[:, h, qtok],
                                                ak.a[:, k, lo * 128:lo * 128 + nb], start=True, stop=True)
                            kb.op("pe", mm, reads=[aq, ak, kcT], writes=[b0, b1])
                        else:
                            kb.op("pe", lambda e, pp=pp, h=h: e.matmul(kb.bank(2 * pp, 256), aq.a[:, h, qtok],
                                                                      ak.a[:, k, sq * 256:(sq + 1) * 256],
                                                                      start=True, stop=True),
                                  reads=[aq, ak], writes=[b0])
                    for g4 in H:
                        pp = pps[g4]
                        b0, b1 = kb.pb[2 * pp], kb.pb[2 * pp + 1]
                        s = sring.next()
                        ss_.append(s)
                        if sample:
                            kb.copy(s.a[:, 0:512], kb.bank(2 * pp), [b0], [s], eng="act")
                            kb.tt(s.a[:, 512:512 + nb], kb.bank(2 * pp + 1, nb), self.amask.a[:, m0:m0 + nb], ALU.add,
                                  [b1, self.amask, s], [s])
                        else:
                            kb.copy(s.a[:, 0:256], kb.bank(2 * pp, 256), [b0], [s], eng="act")
                    st = sm4.next()
                    rs = sm4.next()
                    for g4 in H:
                        s = ss_[g4]
                        kb.op("dve", lambda e, st=st, s=s, g4=g4: e.tensor_reduce(st.a[:, g4:g4 + 1], s.a[:, :nkeys],
                                                                                 axis=AX.X, op=ALU.max), [s], [st])
                    sk4 = self.sinkb.a[:, l, 4 * k:4 * k + 4]
                    kb.tt(st.a[:, 0:4], st.a[:, 0:4], sk4, ALU.max, [st, self.sinkb], [st])
                    kb.tt(st.a[:, 8:12], sk4, st.a[:, 0:4], ALU.subtract, [st, self.sinkb], [st])
                    kb.ts(st.a[:, 4:8], st.a[:, 0:4], -1.0, 0.0, ALU.mult, ALU.add, [st], [st])
                    kb.op("dve", lambda e, rs=rs: e.memset(rs.a, 0.0), writes=[rs])
                    for g4 in H:
                        s = ss_[g4]
                        p = pring.next()
                        ps_.append(p)
                        kb.act(p.a[:, :nkeys], s.a[:, :nkeys], AF.Exp, [s, st, rs], [p, rs], bias=st.a[:, 4 + g4:5 + g4],
                               accum=rs.a[:, g4:g4 + 1])
                    kb.act(rs.a[:, 4:8], st.a[:, 8:12], AF.Exp, [st, rs], [rs])
                    kb.tt(rs.a[:, 8:12], rs.a[:, 0:4], rs.a[:, 4:8], ALU.add, [rs], [rs])
                    kb.op("dve", lambda e, rs=rs: e.reciprocal(rs.a[:, 12:16], rs.a[:, 8:12]), [rs], [rs])
                    for g4 in H:
                        p = ps_[g4]
                        pb_ = pbr.next()
                        pbs.append(pb_)
                        kb.ts(pb_.a[:, :nkeys], p.a[:, :nkeys], rs.a[:, 12 + g4:13 + g4], None, ALU.mult, None,
                              [p, rs], [pb_])
                    for g4 in H:
                        pb_ = pbs[g4]
                        bt = kb.nbank()
                        bts.append(bt)

                        def tr(e, bt=bt, pb_=pb_):
                            ins = None
                            for i in range(nkb):
                                ins = e.transpose(kb.bank(bt, 1024, BF16)[:, i * 128:(i + 1) * 128],
                                                  pb_.a[:, i * 128:(i + 1) * 128], self.identb.a)
                            return ins
                        kb.op("pe", tr, reads=[pb_, self.identb], writes=[kb.pb[bt]])
                    for g4 in H:
                        bt = bts[g4]
                        kb.copy(pT.a[:, 0:nkb, g4, :],
                                kb.bank(bt, 1024, BF16).rearrange("p (b q) -> p b q", q=128)[:, 0:nkb, :],
                                [kb.pb[bt]], [pT], eng="act" if g4 % 2 else "dve")
                    bo = kb.nbank()

                    def pv(e, bo=bo, pT=pT, vblocks=vblocks):
                        ins = None
                        for i, (vt, t) in enumerate(vblocks):
                            ins = e.matmul(kb.bank(bo), vt.a[:, t, k * 128:(k + 1) * 128],
                                           pT.a[:, i, :, :].rearrange("p g q -> p (g q)"),
                                           start=(i == 0), stop=(i == len(vblocks) - 1))
                        return ins
                    kb.op("pe", pv, reads=[pT, av] + ([vc] if sample else []), writes=[kb.pb[bo]])
                    kb.copy(hat.a[:, 4 * k:4 * k + 4, qtok], kb.bank(bo).rearrange("p (g q) -> p g q", g=4),
                            [kb.pb[bo]], [hat], eng="act")
        for c0 in (0, 8):
            kb.dma("sp", self.HAT[c0:c0 + 8].rearrange("c p t -> p c t"), hat.a[:, c0:c0 + 8, :], reads=[hat])
        kb.stage_end()

    def stage_merge(self, l, g):
        kb = self.kb
        I = self.ins
        hb = [kb.tile([128, 16, T], BF16) for _ in range(3)]
        for t, src in zip(hb, (self.HMT, self.HST, self.HAT)):
            self.load_fm(t, src, 16)
        Ws = (I["w_br_m"][l], I["w_br_s"][l], I["w_br_a"][l])
        slots = [kb.tile([128, 16, 256], BF16) for _ in range(4)]
        gtr = kb.ring(6, [128, T], BF16)
        acc = kb.ring(2, [128, T], F32)
        tmp = kb.ring(2, [128, T], F32)
        yo = kb.ring(3, [128, T], BF16)
        state = {}

        def mk_epi(b):
            def epi(ci, P, banks):
                gt = gtr.next()
                kb.dma("sp", gt.a, self.GTS[b * KC + ci], writes=[gt])
                if b == 0:
                    a = acc.next()
                    state[ci] = a
                    kb.tt(a.a, P, gt.a, ALU.mult, banks + [gt], [a])
                else:
                    a = state[ci]
                    t = tmp.next()
                    kb.tt(t.a, P, gt.a, ALU.mult, banks + [gt], [t])
                    if b == 1:
                        kb.tt(a.a, a.a, t.a, ALU.add, [a, t], [a])
                    else:
                        o = yo.next()
                        kb.tt(o.a, a.a, t.a, ALU.add, [a, t], [o])
                        kb.dma("sp", self.YT[ci], o.a, reads=[o])
            return epi
        for c0 in range(0, KC, 2):
            for b in range(3):
                def epi_b(ci, P, banks, b=b, c0=c0):
                    mk_epi(b)(c0 + ci, P, banks)
                kb.gemm_fm(Ws[b], 16, hb[b], T, [(c0 + i) * 128 for i in range(2)], epi_b, slots, nb=2)
        kb.stage_end()

    def stage_wout(self, l, g):
        kb = self.kb
        yt = kb.tile([128, KC, T], BF16)
        self.load_fm(yt, self.YT, KC)
        slots = [kb.tile([128, KC, 512], BF16) for _ in range(2)]
        xin = kb.ring(3, [128, T], F32)

        def epi(ci, P, banks):
            xi = xin.next()
            kb.dma("sp", xi.a, self.XT[ci], writes=[xi])
            kb.stt(xi.a, P, self.modv(l, 2, g)[:, ci:ci + 1], xi.a, ALU.mult, ALU.add, banks + [self.MOD, xi], [xi])
            kb.dma("sp", self.XT[ci], xi.a, reads=[xi])
        kb.gemm_fm(self.ins["w_out"][l], KC, yt, T, [i * 128 for i in range(KC)], epi, slots)
        kb.stage_end()

    def stage_ffn_up(self, l, g):
        kb = self.kb
        sample = (g == 1)
        ht = kb.tile([128, KC, T], BF16)
        self.load_fm(ht, self.HT, KC)
        slots = [kb.tile([128, KC, 256], BF16) for _ in range(3)]
        tg = kb.ring(2, [128, T], F32)
        tu = kb.ring(2, [128, T], F32)
        go = kb.ring(3, [128, T], BF16)
        cw = self.cw.a
        cbv = self.cb.a
        st = {}

        def sh(ap, a, b):
            if sample:
                return ap[:, a:T + b]
            v = ap.rearrange("p (s t) -> p s t", s=4)
            return v[:, :, a:256 + b]

        def conv(dst, P, banks, f):
            kb.act(dst.a, P, AF.Identity, banks + [self.cw, self.cb], [dst], scale=cw[:, l, 1, f:f + 1],
                   bias=cbv[:, l, f:f + 1])
            kb.stt(sh(dst.a, 1, 0), sh(P, 0, -1), cw[:, l, 0, f:f + 1], sh(dst.a, 1, 0), ALU.mult, ALU.add,
                   banks + [dst, self.cw], [dst])
            kb.stt(sh(dst.a, 0, -1), sh(P, 1, 0), cw[:, l, 2, f:f + 1], sh(dst.a, 0, -1), ALU.mult, ALU.add,
                   banks + [dst, self.cw], [dst])

        def epi(ci, P, banks):
            f = ci // 2
            if ci % 2 == 0:
                t = tg.next()
                conv(t, P, banks, f)
                kb.act(t.a, t.a, AF.Silu, [t], [t])
                st["g"] = t
            else:
                t = tu.next()
                conv(t, P, banks, FC + f)
                o = go.next()
                kb.tt(o.a, st["g"].a, t.a, ALU.mult, [st["g"], t], [o])
                kb.dma("sp", self.GU[f], o.a, reads=[o])
        chunks = []
        for f in range(FC):
            chunks += [f * 128, (FC + f) * 128]
        kb.gemm_fm(self.ins["ffn_up"][l], KC, ht, T, chunks, epi, slots, nb=2)
        kb.stage_end()

    def stage_ffn_down(self, l, g):
        kb = self.kb
        for half in range(2):
            gu = kb.tile([128, FC, 512], BF16)
            for c0 in range(0, FC, 8):
                n = min(8, FC - c0)
                kb.dma("sp", gu.a[:, c0:c0 + n, :],
                       self.GU[c0:c0 + n, :, half * 512:(half + 1) * 512].rearrange("c p t -> p c t"),
                       writes=[gu], merge=(c0 > 0))
            slots = [kb.tile([128, FC, 128], BF16) for _ in range(3)]
            xin = kb.ring(3, [128, 512], F32)

            def epi(ci, P, banks, half=half):
                xi = xin.next()
                kb.dma("sp", xi.a, self.XT[ci][:, half * 512:(half + 1) * 512], writes=[xi])
                kb.stt(xi.a, P, self.modv(l, 5, g)[:, ci:ci + 1], xi.a, ALU.mult, ALU.add,
                       banks + [self.MOD, xi], [xi])
                kb.dma("sp", self.XT[ci][:, half * 512:(half + 1) * 512], xi.a, reads=[xi])
            kb.gemm_fm(self.ins["ffn_down"][l], FC, gu, 512, [i * 128 for i in range(KC)], epi, slots, nb=1)
            kb.stage_end()

    def build(self):
        kb = self.kb
        self.stage_mod()
        for g in self.groups:
            self.stage_loadx(g)
            for l in self.layers:
                self.stage_norm(l, g, 1)
                if self.upto <= 1:
                    continue
                self.stage_inproj(l, g)
                if self.upto <= 2:
                    continue
                self.stage_mlstm(l, g)
                if self.upto <= 3:
                    continue
                self.stage_sgu(l, g)
                if self.upto <= 4:
                    continue
                self.stage_attn(l, g)
                if self.upto <= 5:
                    continue
                self.stage_merge(l, g)
                self.stage_wout(l, g)
                self.stage_norm(l, g, 2)
                self.stage_ffn_up(l, g)
                self.stage_ffn_down(l, g)
            if self.upto > 5:
                self.stage_norm(0, g, 3)
        kb.barrier()
        return self.nc


def host_consts():
    c = {}
    c["c_ident"] = np.eye(128, dtype=np.float32)
    s = np.arange(128)
    tri_f = (s[:, None] <= s[None, :]).astype(np.float32)
    tri_r = (s[:, None] >= s[None, :]).astype(np.float32)
    c["c_tri"] = np.stack([tri_f, tri_r])
    sel_f = np.zeros((128, 128), np.float32); sel_f[127, :] = 1.0
    sel_r = np.zeros((128, 128), np.float32); sel_r[0, :] = 1.0
    c["c_sel"] = np.stack([sel_f, sel_r])
    mf = np.where(s[None, :] <= s[:, None], 0.0, NEG).astype(np.float32)
    mr = np.where(s[None, :] >= s[:, None], 0.0, NEG).astype(np.float32)
    c["c_mmask"] = np.stack([mf, mr])
    a = np.arange(128)[:, None]; r = np.arange(384)[None, :]
    c["c_amask"] = np.where(np.abs(r - 128 - a) <= 128, 0.0, NEG).astype(np.float32)
    t = np.arange(T)
    row = (t // 64).astype(np.float64); col = (t % 64).astype(np.float64)
    inv = 10000.0 ** (-np.arange(32, dtype=np.float64) / 32.0)
    p = np.arange(128)
    pos = np.where((p < 64)[:, None], row[None, :], col[None, :])
    ang = (pos.astype(np.float32) * inv.astype(np.float32)[p % 32][:, None]).astype(np.float32)
    cos = np.cos(ang.astype(np.float64)); sin = np.sin(ang.astype(np.float64))
    sign = np.where((p % 64) < 32, -1.0, 1.0)[:, None]
    c["c_rope"] = np.stack([cos, sin * sign]).astype(np.float32)
    partner = np.where((p % 64) < 32, p + 32, p - 32)
    perm = np.zeros((128, 128), np.float32)
    perm[partner, p] = 1.0
    c["c_perm"] = perm
    return c


_NC_CACHE = {}


def make_in_maps(inp, ncores=8):
    f = lambda a: np.ascontiguousarray(np.asarray(a, dtype=np.float32))
    xp = f(inp["x_prompt"]); xs = f(inp["x_sample"])
    shared = {}
    for k in ("ada_w", "w_in", "w_br_m", "w_br_s", "w_br_a", "w_out", "ffn_up", "ffn_down"):
        shared[k] = f(inp[k])
    shared["adabF"] = np.ascontiguousarray(np.stack([fmv(inp["ada_b"][l]) for l in range(NL)], 1))
    shared["n1F"] = np.ascontiguousarray(np.stack([fmv(inp["norm1_g"][l]) for l in range(NL)], 1))
    shared["n2F"] = np.ascontiguousarray(np.stack([fmv(inp["norm2_g"][l]) for l in range(NL)], 1))
    shared["nfF"] = fmv(inp["final_g"])
    shared["mgb"] = f(inp["m_gate_b"]).reshape(NL, 32)
    shared["mng"] = f(inp["m_norm_g"]); shared["sgn"] = f(inp["sgu_norm_g"])
    shared["sgwT"] = np.ascontiguousarray(f(inp["sgu_w"]).transpose(0, 1, 3, 2))
    shared["sgb"] = f(inp["sgu_b"]); shared["sink"] = f(inp["attn_sink"])
    cw = f(inp["ffn_conv_w"])
    shared["cwF"] = np.ascontiguousarray(
        np.stack([np.stack([fmv(cw[l, k]) for k in range(3)], 1) for l in range(NL)], 1))
    shared["cbF"] = np.ascontiguousarray(np.stack([fmv(inp["ffn_conv_b"][l]) for l in range(NL)], 1))
    shared.update(host_consts())
    in_maps = []
    for c in range(ncores):
        j = c // 2
        m = dict(shared)
        m["xg"] = np.ascontiguousarray(np.stack([xp[4 * c:4 * c + 4].reshape(T, D), xs[j]]))
        cv = np.stack([f(inp["c_ctx"]), f(inp["c"])[j]])
        m["cvecF"] = np.ascontiguousarray(cv.reshape(2, KC, 128).transpose(2, 1, 0))
        m["ck"] = np.ascontiguousarray(f(inp["cache_k"])[j].reshape(NL, 512, 512))
        m["cv"] = np.ascontiguousarray(f(inp["cache_v"])[j].reshape(NL, 512, 512))
        m["sC"] = np.ascontiguousarray(f(inp["state_C"])[j])
        m["sn"] = np.ascontiguousarray(f(inp["state_n"])[j])
        m["sm"] = np.ascontiguousarray(f(inp["state_m"])[j])
        in_maps.append(m)
    return in_maps


def kernel(**inp):
    ncores = 8
    in_maps = make_in_maps(inp, ncores)
    if "nc" not in _NC_CACHE:
        _NC_CACHE["nc"] = Prog().build()
    nc = _NC_CACHE["nc"]
    res = run_bass_kernel_spmd(nc, in_maps, core_ids=list(range(ncores)))
    R = res.results
    y_prompt = np.concatenate([R[c]["y"][0].reshape(4, 256, D) for c in range(ncores)], 0)
    y_sample = np.stack([np.concatenate([R[2 * j]["y"][1][:512], R[2 * j + 1]["y"][1][512:]], 0) for j in range(4)], 0)
    nk = np.concatenate([R[c]["nk"].reshape(NL, 4, 256, 4, 128).transpose(1, 0, 2, 3, 4) for c in range(ncores)], 0)
    nv = np.concatenate([R[c]["nv"].reshape(NL, 4, 256, 4, 128).transpose(1, 0, 2, 3, 4) for c in range(ncores)], 0)
    nC = np.concatenate([R[c]["nC"] for c in range(ncores)], 0)
    nn = np.concatenate([R[c]["nn"] for c in range(ncores)], 0)
    nm = np.concatenate([R[c]["nm"] for c in range(ncores)], 0)
    return (y_prompt.astype(np.float32), y_sample.astype(np.float32), np.ascontiguousarray(nk),
            np.ascontiguousarray(nv), nC, nn, nm)
```
